# Optimizing a Trainium2 kernel written in Bass

```python
import jax, jax.numpy as jnp
from jax import lax
import numpy as np

D_MODEL = 1024
BATCH = 16
SEQ = 4096
DEPTH = 1

CHUNK = 64
N_MEM = 256
GDN_HEAD_DIM = 128
GDN_WIDTH = D_MODEL
GDN_HEADS = GDN_WIDTH // GDN_HEAD_DIM
CONV_WIDTH = 4
GLA_HEADS = 4
GLA_KEY_WIDTH = D_MODEL // 2
GLA_VAL_WIDTH = D_MODEL
GLA_KEY_DIM = GLA_KEY_WIDTH // GLA_HEADS
GLA_VAL_DIM = GLA_VAL_WIDTH // GLA_HEADS
GLA_GATE_RANK = 16
GLA_GATE_TAU = 16.0
XATTN_HEADS = 4
XATTN_HEAD_DIM = D_MODEL // XATTN_HEADS
D_FF = 4 * D_MODEL
NORM_EPS = 1e-6
IN_SIZES = (GDN_WIDTH, GDN_WIDTH, GDN_WIDTH, GDN_WIDTH, GDN_HEADS, GDN_HEADS,
            GLA_KEY_WIDTH, GLA_KEY_WIDTH, GLA_VAL_WIDTH, GLA_VAL_WIDTH, GLA_GATE_RANK,
            D_MODEL, D_MODEL)
IN_WIDTH = 4 * GDN_WIDTH + 2 * GDN_HEADS + 2 * GLA_KEY_WIDTH + 2 * GLA_VAL_WIDTH + GLA_GATE_RANK + 2 * D_MODEL

kernel_name = "hybrid_gdn_gla_gated_merge_xattn_sqrelu"


def rmsnorm(x, g):
    xf = x.astype(jnp.float32)
    y = xf * lax.rsqrt(jnp.mean(xf * xf, axis=-1, keepdims=True) + NORM_EPS)
    return (y * g.astype(jnp.float32)).astype(x.dtype)


def l2norm(x):
    return x * lax.rsqrt(jnp.sum(x * x, axis=-1, keepdims=True) + NORM_EPS)


def causal_depthwise_conv(x, w):
    c = x.shape[-1]
    return lax.conv_general_dilated(
        x, w.astype(x.dtype)[:, None, :], window_strides=(1,),
        padding=[(w.shape[0] - 1, 0)], dimension_numbers=("NWC", "WIO", "NWC"),
        feature_group_count=c)


def to_chunks(t, n_heads):
    b, s, _ = t.shape
    return t.reshape(b, s // CHUNK, CHUNK, n_heads, -1).transpose(0, 3, 1, 2, 4)


def scalar_chunks(t):
    b, s, h = t.shape
    return t.reshape(b, s // CHUNK, CHUNK, h).transpose(0, 3, 1, 2)


def from_chunks(t):
    b, h, n, c, d = t.shape
    return t.transpose(0, 2, 3, 1, 4).reshape(b, n * c, h, d)


def gated_delta_rule_chunked(q, k, v, g, beta):
    dk, dv = k.shape[-1], v.shape[-1]
    idx = jnp.arange(CHUNK)
    strict = idx[:, None] > idx[None, :]
    incl = idx[:, None] >= idx[None, :]
    gc = jnp.cumsum(g, axis=-1)
    decay = jnp.exp(jnp.where(incl, gc[..., :, None] - gc[..., None, :], -jnp.inf))
    kk = jnp.einsum('bhnid,bhnjd->bhnij', k, k)
    lower = jnp.where(strict, beta[..., :, None] * kk * decay, 0.0)
    rhs = jnp.concatenate([beta[..., None] * v, (beta * jnp.exp(gc))[..., None] * k], axis=-1)
    sol = lax.linalg.triangular_solve(lower, rhs, left_side=True, lower=True, unit_diagonal=True)
    u, w = sol[..., :dv], sol[..., dv:]
    a_qk = jnp.einsum('bhnid,bhnjd->bhnij', q, k) * decay
    q_dec = q * jnp.exp(gc)[..., None]
    k_dec = k * jnp.exp(gc[..., -1:] - gc)[..., None]
    g_end = jnp.exp(gc[..., -1])
    xs = tuple(jnp.moveaxis(t, 2, 0) for t in (u, w, q_dec, a_qk, k_dec, g_end))

    def step(state, inp):
        u_c, w_c, q_c, a_c, k_c, ge = inp
        delta = u_c - jnp.einsum('bhcd,bhde->bhce', w_c, state)
        o = jnp.einsum('bhcd,bhde->bhce', q_c, state) + jnp.einsum('bhij,bhje->bhie', a_c, delta)
        state = ge[..., None, None] * state + jnp.einsum('bhcd,bhce->bhde', k_c, delta)
        return state, o

    s0 = jnp.zeros(q.shape[:2] + (dk, dv), q.dtype)
    _, o = lax.scan(step, s0, xs)
    return jnp.moveaxis(o, 0, 2)


def gla_chunked(q, k, v, gk):
    dk, dv = k.shape[-1], v.shape[-1]
    idx = jnp.arange(CHUNK)
    incl = idx[:, None] >= idx[None, :]
    bc = jnp.cumsum(gk, axis=-2)
    b_ref = bc[..., CHUNK // 2:CHUNK // 2 + 1, :]
    a_qk = jnp.einsum('bhnid,bhnjd->bhnij', q * jnp.exp(bc - b_ref), k * jnp.exp(b_ref - bc))
    o_intra = jnp.einsum('bhnij,bhnje->bhnie', jnp.where(incl, a_qk, 0.0), v)
    q_dec = q * jnp.exp(bc)
    k_dec = k * jnp.exp(bc[..., -1:, :] - bc)
    g_end = jnp.exp(bc[..., -1, :])
    xs = tuple(jnp.moveaxis(t, 2, 0) for t in (q_dec, o_intra, k_dec, v, g_end))

    def step(state, inp):
        q_c, oi_c, k_c, v_c, ge = inp
        o = jnp.einsum('bhcd,bhde->bhce', q_c, state) + oi_c
        state = ge[..., :, None] * state + jnp.einsum('bhcd,bhce->bhde', k_c, v_c)
        return state, o

    s0 = jnp.zeros(q.shape[:2] + (dk, dv), q.dtype)
    _, o = lax.scan(step, s0, xs)
    return jnp.moveaxis(o, 0, 2)


def setup_inputs(seed: int = 0) -> dict:
    key = jax.random.key(seed)
    ks = iter(jax.random.split(key, 32))
    nrm = lambda shape, scale: jax.random.normal(next(ks), shape, jnp.float32) * scale
    gain = lambda n: 1.0 + nrm((DEPTH, n), 0.02)
    a_log = jnp.log(jax.random.uniform(next(ks), (DEPTH, GDN_HEADS), jnp.float32, 1.0, 16.0))
    dt = jnp.exp(jax.random.uniform(next(ks), (DEPTH, GDN_HEADS), jnp.float32, np.log(1e-3), np.log(1e-1)))
    dt_bias = dt + jnp.log(-jnp.expm1(-dt))
    return {
        "x": nrm((BATCH, SEQ, D_MODEL), 1.0),
        "mem": nrm((BATCH, N_MEM, D_MODEL), 1.0),
        "norm_mix_g": gain(D_MODEL),
        "w_in": nrm((DEPTH, D_MODEL, IN_WIDTH), D_MODEL ** -0.5),
        "gdn_conv_w": nrm((DEPTH, CONV_WIDTH, 3 * GDN_WIDTH), CONV_WIDTH ** -0.5),
        "gdn_a_log": a_log,
        "gdn_dt_bias": dt_bias,
        "gdn_norm_g": gain(GDN_HEAD_DIM),
        "gla_w_gate2": nrm((DEPTH, GLA_GATE_RANK, GLA_KEY_WIDTH), GLA_GATE_RANK ** -0.5),
        "gla_b_gate": nrm((DEPTH, GLA_KEY_WIDTH), 0.1),
        "gla_norm_g": gain(GLA_VAL_DIM),
        "w_branch_gdn": nrm((DEPTH, GDN_WIDTH, D_MODEL), GDN_WIDTH ** -0.5),
        "w_branch_gla": nrm((DEPTH, GLA_VAL_WIDTH, D_MODEL), GLA_VAL_WIDTH ** -0.5),
        "w_out": nrm((DEPTH, D_MODEL, D_MODEL), D_MODEL ** -0.5),
        "norm_xattn_g": gain(D_MODEL),
        "norm_mem_g": gain(D_MODEL),
        "xattn_wq": nrm((DEPTH, D_MODEL, D_MODEL), D_MODEL ** -0.5),
        "xattn_wk": nrm((DEPTH, D_MODEL, D_MODEL), D_MODEL ** -0.5),
        "xattn_wv": nrm((DEPTH, D_MODEL, D_MODEL), D_MODEL ** -0.5),
        "xattn_wo": nrm((DEPTH, D_MODEL, D_MODEL), D_MODEL ** -0.5),
        "norm_mlp_g": gain(D_MODEL),
        "mlp_w1": nrm((DEPTH, D_MODEL, D_FF), D_MODEL ** -0.5),
        "mlp_w2": nrm((DEPTH, D_FF, D_MODEL), D_FF ** -0.5),
        "norm_final_g": 1.0 + nrm((D_MODEL,), 0.02),
    }


def reference(x, mem, norm_mix_g, w_in, gdn_conv_w, gdn_a_log, gdn_dt_bias, gdn_norm_g,
              gla_w_gate2, gla_b_gate, gla_norm_g, w_branch_gdn, w_branch_gla, w_out,
              norm_xattn_g, norm_mem_g, xattn_wq, xattn_wk, xattn_wv, xattn_wo,
              norm_mlp_g, mlp_w1, mlp_w2, norm_final_g):
    dt = x.dtype
    f32 = jnp.float32
    bsz, seq, _ = x.shape
    split_at = [int(v) for v in np.cumsum(IN_SIZES)[:-1]]
    for i in range(DEPTH):
        h = rmsnorm(x, norm_mix_g[i])
        proj = h @ w_in[i].astype(dt)
        (gq, gk_, gv, gz, ga, gb, lq, lk, lv, lr, lgate, gate_a, gate_b) = jnp.split(proj, split_at, axis=-1)

        qkv = jax.nn.silu(causal_depthwise_conv(jnp.concatenate([gq, gk_, gv], -1), gdn_conv_w[i]))
        cq, ck, cv = jnp.split(qkv.astype(f32), [GDN_WIDTH, 2 * GDN_WIDTH], axis=-1)
        q_a = l2norm(to_chunks(cq, GDN_HEADS)) * (GDN_HEAD_DIM ** -0.5)
        k_a = l2norm(to_chunks(ck, GDN_HEADS))
        v_a = to_chunks(cv, GDN_HEADS)
        g_a = -jnp.exp(gdn_a_log[i].astype(f32)) * jax.nn.softplus(ga.astype(f32) + gdn_dt_bias[i].astype(f32))
        beta_a = jax.nn.sigmoid(gb.astype(f32))
        o_a = gated_delta_rule_chunked(q_a, k_a, v_a, scalar_chunks(g_a), scalar_chunks(beta_a))
        o_a = rmsnorm(from_chunks(o_a).astype(dt), gdn_norm_g[i]) * jax.nn.silu(gz.reshape(bsz, seq, GDN_HEADS, GDN_HEAD_DIM))
        y_a = o_a.reshape(bsz, seq, GDN_WIDTH) @ w_branch_gdn[i].astype(dt)

        log_fg = jax.nn.log_sigmoid((lgate @ gla_w_gate2[i].astype(dt)).astype(f32) + gla_b_gate[i].astype(f32)) / GLA_GATE_TAU
        q_b = to_chunks(lq.astype(f32), GLA_HEADS) * (GLA_KEY_DIM ** -0.5)
        k_b = to_chunks(lk.astype(f32), GLA_HEADS)
        v_b = to_chunks(lv.astype(f32), GLA_HEADS)
        o_b = gla_chunked(q_b, k_b, v_b, to_chunks(log_fg, GLA_HEADS))
        o_b = rmsnorm(from_chunks(o_b).astype(dt), gla_norm_g[i]) * jax.nn.silu(lr.reshape(bsz, seq, GLA_HEADS, GLA_VAL_DIM))
        y_b = o_b.reshape(bsz, seq, GLA_VAL_WIDTH) @ w_branch_gla[i].astype(dt)

        merged = jax.nn.sigmoid(gate_a) * y_a + jax.nn.sigmoid(gate_b) * y_b
        x = x + merged @ w_out[i].astype(dt)

        h = rmsnorm(x, norm_xattn_g[i])
        m = rmsnorm(mem.astype(dt), norm_mem_g[i])
        q = (h @ xattn_wq[i].astype(dt)).reshape(bsz, seq, XATTN_HEADS, XATTN_HEAD_DIM)
        k = (m @ xattn_wk[i].astype(dt)).reshape(bsz, -1, XATTN_HEADS, XATTN_HEAD_DIM)
        v = (m @ xattn_wv[i].astype(dt)).reshape(bsz, -1, XATTN_HEADS, XATTN_HEAD_DIM)
        s = jnp.einsum('bshd,bmhd->bhsm', q, k).astype(f32) * (XATTN_HEAD_DIM ** -0.5)
        p = jax.nn.softmax(s, axis=-1).astype(dt)
        o = jnp.einsum('bhsm,bmhd->bshd', p, v).reshape(bsz, seq, D_MODEL)
        x = x + o @ xattn_wo[i].astype(dt)

        h = rmsnorm(x, norm_mlp_g[i])
        x = x + jnp.square(jax.nn.relu(h @ mlp_w1[i].astype(dt))) @ mlp_w2[i].astype(dt)
    return rmsnorm(x, norm_final_g)
```

```python
from contextlib import ExitStack
from concourse.bass_utils import run_bass_kernel_spmd
import numpy as np
import concourse.bass as bass
import concourse.mybir as mybir

F32 = mybir.dt.float32
BF16 = mybir.dt.bfloat16
AF = mybir.ActivationFunctionType
ALU = mybir.AluOpType
AX = mybir.AxisListType

ENGS = ("sync", "scalar", "vector", "gpsimd", "tensor")


class Buf:
    __slots__ = ("name", "last_w", "readers", "excl")

    def __init__(self, name):
        self.name = name
        self.last_w = None
        self.readers = []
        self.excl = False


class V:
    __slots__ = ("ap", "buf", "dram")

    def __init__(self, ap, buf, dram=False):
        self.ap = ap
        self.buf = buf
        self.dram = dram

    def __getitem__(self, idx):
        return V(self.ap[idx], self.buf, self.dram)

    def re(self, pat, **kw):
        return V(self.ap.rearrange(pat, **kw), self.buf, self.dram)

    def bc(self, dt):
        return V(self.ap.bitcast(dt), self.buf, self.dram)


class Tile(V):
    def __init__(self, t, name):
        V.__init__(self, t[:], Buf(name))
        self.t = t


class Op:
    __slots__ = ("eng", "fn", "deps", "is_dma", "sem", "sigval", "signal", "idx", "waits")

    def __init__(self, eng, fn, deps, is_dma):
        self.eng = eng
        self.fn = fn
        self.deps = deps
        self.is_dma = is_dma
        self.sem = None
        self.sigval = 0
        self.signal = False
        self.waits = []


class Prog:
    def __init__(self, nc, es, same_engine_sync=False):
        self.nc = nc
        self.es = es
        self.ops = {e: [] for e in ENGS}
        self.same_engine_sync = same_engine_sync
        self.dma_sems = {}
        self.final_deps = []
        self.nsb = 0
        self.phase = ""
        self.op_phase = {e: [] for e in ENGS}

    def sb(self, name, shape, dt):
        t = self.es.enter_context(self.nc.sbuf_tensor("sb_" + name, list(shape), dt))
        return Tile(t, name)

    def dram(self, ap, name):
        return V(ap, Buf(name), True)

    def ps(self, name, shape, dt):
        t = self.es.enter_context(self.nc.psum_tensor("ps_" + name, list(shape), dt))
        tl = Tile(t, name)
        tl.buf.excl = True
        return tl

    def rec(self, eng, fn, reads=(), writes=(), is_dma=False, dma_key=None):
        deps = set()
        rb = [v.buf for v in reads if isinstance(v, V)]
        wb = [v.buf for v in writes if isinstance(v, V)]
        for b in rb:
            if b.last_w is not None:
                deps.add(b.last_w)
            if b.excl:
                for r in b.readers:
                    if r.eng != eng:
                        deps.add(r)
        for b in wb:
            lw = b.last_w
            if lw is not None and (lw.eng != eng or lw.is_dma or is_dma):
                deps.add(lw)
            for r in b.readers:
                if r.eng != eng or r.is_dma or is_dma:
                    deps.add(r)
        op = Op(eng, fn, deps, is_dma)
        if is_dma:
            op.sem = dma_key
        for b in wb:
            b.last_w = op
            b.readers = []
        for b in rb:
            if b not in wb:
                b.readers.append(op)
        op.idx = len(self.ops[eng])
        self.ops[eng].append(op)
        self.op_phase[eng].append(self.phase)
        return op

    def finalize(self, block_es):
        nc = self.nc
        for e in ENGS:
            for op in self.ops[e]:
                if op.is_dma:
                    op.signal = True
                for d in op.deps:
                    if d.is_dma:
                        continue
                    d.signal = True
        last = {}
        for e in ENGS:
            for op in self.ops[e]:
                if op.is_dma:
                    last[op.sem] = op
        for op in last.values():
            if op not in self.final_deps:
                self.final_deps.append(op)
        for d in self.final_deps:
            d.signal = True
        eng_sem = {e: block_es.enter_context(nc.semaphore("sem_" + e)) for e in ENGS}
        dma_sem = {}
        for e in ENGS:
            cnt = 0
            for op in self.ops[e]:
                if op.is_dma:
                    key = op.sem
                    if key not in dma_sem:
                        dma_sem[key] = [block_es.enter_context(nc.semaphore("dsem_%d" % len(dma_sem))), 0]
                    ent = dma_sem[key]
                    ent[1] += 16
                    op.sem = ent[0]
                    op.sigval = ent[1]
                elif op.signal:
                    cnt += 1
                    op.sem = eng_sem[e]
                    op.sigval = cnt
        nwaits = 0
        for e in ENGS:
            seen = {}
            for op in self.ops[e]:
                need = {}
                for d in op.deps:
                    if d is op:
                        continue
                    k = id(d.sem)
                    if seen.get(k, 0) >= d.sigval:
                        continue
                    if k not in need or need[k][1] < d.sigval:
                        need[k] = (d.sem, d.sigval)
                for k, (s, v) in need.items():
                    seen[k] = v
                    op.waits.append((s, v))
                    nwaits += 1
        self.nwaits = nwaits
        self.nsems = len(dma_sem) + len(ENGS)
        finals = [(d.sem, d.sigval) for d in self.final_deps]
        return finals

    def emit(self, block, finals, final_eng="gpsimd"):
        P = self

        def run(ename, e):
            for op in P.ops[ename]:
                for (s, v) in op.waits:
                    e.wait_ge(s, v)
                ins = op.fn(e)
                if op.signal:
                    ins.then_inc(op.sem, 16 if op.is_dma else 1)
            if ename == final_eng:
                for (s, v) in finals:
                    e.wait_ge(s, v)

        @block.sync
        def _(e):
            run("sync", e)

        @block.scalar
        def _(e):
            run("scalar", e)

        @block.vector
        def _(e):
            run("vector", e)

        @block.gpsimd
        def _(e):
            run("gpsimd", e)

        @block.tensor
        def _(e):
            run("tensor", e)

    def dma(self, eng, out, in_, key=None):
        k = key
        if k is None:
            if isinstance(out, V) and not out.dram:
                k = out.buf
            elif isinstance(in_, V) and not in_.dram:
                k = in_.buf
            else:
                k = out.buf if isinstance(out, V) else in_.buf
        o = out.ap if isinstance(out, V) else out
        i = in_.ap if isinstance(in_, V) else in_
        return self.rec(eng, lambda e: e.dma_start(out=o, in_=i),
                        reads=[in_], writes=[out], is_dma=True, dma_key=k)

    def mm(self, out, lhsT, rhs, start=True, stop=True):
        return self.rec("tensor", lambda e: e.matmul(out.ap, lhsT.ap, rhs.ap, start=start, stop=stop),
                        reads=[lhsT, rhs], writes=[out])

    def tr(self, out, in_, ident):
        return self.rec("tensor", lambda e: e.transpose(out.ap, in_.ap, ident.ap),
                        reads=[in_, ident], writes=[out])

    def act(self, out, in_, func, bias=None, scale=None, accum_out=None, eng="scalar"):
        kw = {}
        reads = [in_]
        if bias is not None:
            kw["bias"] = bias.ap if isinstance(bias, V) else bias
            reads.append(bias)
        if scale is not None:
            kw["scale"] = scale.ap if isinstance(scale, V) else scale
            reads.append(scale)
        writes = [out]
        if accum_out is not None:
            kw["accum_out"] = accum_out.ap
            writes.append(accum_out)
        return self.rec(eng, lambda e: e.activation(out.ap, in_.ap, func, **kw), reads=reads, writes=writes)

    def ts(self, eng, out, in0, s1, s2, op0, op1=None, accum_out=None):
        reads = [in0, s1, s2]
        a1 = s1.ap if isinstance(s1, V) else s1
        a2 = s2.ap if isinstance(s2, V) else s2
        kw = {}
        writes = [out]
        if op1 is not None:
            kw["op1"] = op1
        if accum_out is not None:
            kw["accum_out"] = accum_out.ap
            writes.append(accum_out)
        return self.rec(eng, lambda e: e.tensor_scalar(out=out.ap, in0=in0.ap, scalar1=a1, scalar2=a2, op0=op0, **kw),
                        reads=reads, writes=writes)

    def stt(self, eng, out, in0, scalar, in1, op0, op1):
        sc = scalar.ap if isinstance(scalar, V) else scalar
        eng = "vector"
        return self.rec(eng, lambda e: e.scalar_tensor_tensor(out=out.ap, in0=in0.ap, scalar=sc, in1=in1.ap, op0=op0, op1=op1),
                        reads=[in0, scalar, in1], writes=[out])

    def tt(self, eng, out, in0, in1, op):
        return self.rec(eng, lambda e: e.tensor_tensor(out=out.ap, in0=in0.ap, in1=in1.ap, op=op),
                        reads=[in0, in1], writes=[out])

    def copy(self, eng, out, in_):
        if eng == "scalar":
            return self.rec(eng, lambda e: e.copy(out=out.ap, in_=in_.ap), reads=[in_], writes=[out])
        return self.rec(eng, lambda e: e.tensor_copy(out=out.ap, in_=in_.ap), reads=[in_], writes=[out])

    def reduce(self, eng, out, in_, op, axis=AX.X):
        return self.rec(eng, lambda e: e.tensor_reduce(out=out.ap, in_=in_.ap, axis=axis, op=op),
                        reads=[in_], writes=[out])

    def recip(self, eng, out, in_):
        return self.rec(eng, lambda e: e.reciprocal(out=out.ap, in_=in_.ap), reads=[in_], writes=[out])

    def scan(self, out, d0, d1, initial, op0, op1):
        return self.rec("vector", lambda e: e.tensor_tensor_scan(out=out.ap, data0=d0.ap, data1=d1.ap, initial=initial, op0=op0, op1=op1),
                        reads=[d0, d1], writes=[out])

    def memset(self, eng, out, val):
        return self.rec(eng, lambda e: e.memset(out.ap, val), reads=[], writes=[out])


D = 1024
NH = 8
HD = 128
LH = 4
LKD = 128
LVD = 256
NMEM = 256
DFF = 4096
NS = 2
T = NS * 128
EPS = 1e-6
NBLK = 48
BIG = 1.0e30
DGE_SCRATCH = 1024

B_GDN = 0
B_GLA = 8
B_R = 12
B_GATE = 14
B_BG = 18
B_BL = 20
B_OUT = 22
B_WQ = 24
B_WO = 26
B_W1 = 28
B_W2 = 36
B_WK = 44
B_WV = 46

IN_SIZES = (1024, 1024, 1024, 1024, 8, 8, 512, 512, 1024, 1024, 16, 1024, 1024)


def host_layout(inp):
    f = lambda a: np.ascontiguousarray(np.asarray(a, dtype=np.float32))
    w_in = f(inp["w_in"][0])
    offs = np.cumsum((0,) + IN_SIZES)
    gq, gk, gv, gz, ga, gb, lq, lk, lv, lr, lgate, gate_a, gate_b = [w_in[:, offs[i]:offs[i + 1]] for i in range(13)]
    blocks = []

    def blk(mat):
        assert mat.shape == (1024, 512), mat.shape
        return mat.reshape(8, 128, 512).transpose(1, 0, 2)

    for h in range(8):
        s = slice(h * 128, (h + 1) * 128)
        blocks.append(blk(np.concatenate([gq[:, s], gk[:, s], gv[:, s], gz[:, s]], axis=1)))
    for h in range(4):
        blocks.append(blk(np.concatenate([lq[:, h * 128:(h + 1) * 128], lk[:, h * 128:(h + 1) * 128],
                                          lv[:, h * 256:(h + 1) * 256]], axis=1)))
    for c in range(2):
        blocks.append(blk(lr[:, c * 512:(c + 1) * 512]))
    for g in (gate_a, gate_b):
        for c in range(2):
            blocks.append(blk(g[:, c * 512:(c + 1) * 512]))
    for name in ("w_branch_gdn", "w_branch_gla", "w_out", "xattn_wq", "xattn_wo"):
        w = f(inp[name][0])
        for c in range(2):
            blocks.append(blk(w[:, c * 512:(c + 1) * 512]))
    w1 = f(inp["mlp_w1"][0])
    for c in range(8):
        blocks.append(blk(w1[:, c * 512:(c + 1) * 512]))
    w2 = f(inp["mlp_w2"][0])
    for fg in range(4):
        for c in range(2):
            blocks.append(blk(w2[fg * 1024:(fg + 1) * 1024, c * 512:(c + 1) * 512]))
    for name in ("xattn_wk", "xattn_wv"):
        w = f(inp[name][0])
        for c in range(2):
            blocks.append(blk(w[:, c * 512:(c + 1) * 512]))
    wblk = np.ascontiguousarray(np.stack(blocks, 0)).reshape(NBLK, 128, 4096)
    wsm = np.ascontiguousarray(np.concatenate([ga, gb, lgate], axis=1).reshape(8, 128, 32).transpose(1, 0, 2)).reshape(128, 256)

    def gcol(g):
        return np.ascontiguousarray(f(g).reshape(8, 128).T)

    rep = lambda v: np.ascontiguousarray(np.broadcast_to(f(v).reshape(1, -1), (128, f(v).size)))
    gcols = np.concatenate([gcol(inp["norm_mix_g"][0]), gcol(inp["norm_xattn_g"][0]),
                            gcol(inp["norm_mlp_g"][0]), gcol(inp["norm_mem_g"][0])], axis=1)
    cwt = f(inp["gdn_conv_w"][0])
    cw = np.ascontiguousarray(cwt.reshape(4, 24, 128).transpose(2, 1, 0)).reshape(128, 96)
    small = np.concatenate([
        gcols,
        cw,
        rep(inp["gdn_a_log"][0]),
        rep(inp["gdn_dt_bias"][0]),
        np.ascontiguousarray(f(inp["gla_b_gate"][0]).reshape(4, 128).T),
    ], axis=1)
    small = np.ascontiguousarray(small)
    wide = np.concatenate([
        rep(inp["norm_final_g"]),
        rep(np.tile(f(inp["gdn_norm_g"][0]), 8)),
        rep(np.tile(f(inp["gla_norm_g"][0]), 4)),
    ], axis=1)
    wide = np.ascontiguousarray(wide)
    wg2 = f(inp["gla_w_gate2"][0])
    p = np.arange(128)[:, None]
    q = np.arange(128)[None, :]
    ident = (p == q).astype(np.float32)
    tri = (p <= q).astype(np.float32)
    same32 = (p // 32) == (q // 32)
    pm_d = np.where((p > q) & same32, 0.0, BIG).astype(np.float32)
    pm_r = np.where((p > q) & (~same32), 0.0, BIG).astype(np.float32)
    nm_t = np.where(q >= p, 0.0, -BIG).astype(np.float32)
    m01t = (q >= p).astype(np.float32)
    ones = np.ones((128, 128), np.float32)
    consts = np.ascontiguousarray(np.concatenate([ident, tri, pm_d, pm_r, nm_t, m01t, ones], axis=1))
    return dict(wblk=wblk, wsm=wsm, small=small, wide=wide, wg2=wg2, consts=consts)


class RR:
    def __init__(self, items):
        self.items = items
        self.i = 0
        self.held = set()

    def next(self):
        for _ in range(len(self.items) + 1):
            k = self.i % len(self.items)
            self.i += 1
            if k not in self.held:
                return self.items[k]
        raise RuntimeError("all held")

    def hold(self, it):
        self.held.add(self.items.index(it))

    def release(self, it):
        self.held.discard(self.items.index(it))


class _Stop(Exception):
    pass


def build(nseq, S, taps=None, stop=None):
    assert S % T == 0
    nst = S // T
    nc = bass.Bass("TRN2", target_bir_lowering=False, dynamic_dma_scratch_size=DGE_SCRATCH)
    dr = lambda name, shape, dt=F32, kind="ExternalInput": nc.dram_tensor(name, list(shape), dt, kind=kind).ap()
    x_d = dr("x", [nseq, S, D])
    mem_d = dr("mem", [nseq, NMEM, D])
    wblk_d = dr("wblk", [NBLK, 128, 4096])
    wsm_d = dr("wsm", [128, 256])
    small_d = dr("small", [128, 148])
    wide_d = dr("wide", [128, 3072])
    wg2_d = dr("wg2", [16, 512])
    consts_d = dr("consts", [128, 7 * 128])
    y_d = dr("y", [nseq, S, D], F32, "ExternalOutput")
    wbf_ap = dr("wbf", [NBLK, 128, 4096], BF16, "Internal")
    tap_outs = {}

    es = ExitStack()
    P = Prog(nc, es)
    wbf = [P.dram(wbf_ap[b], "wbf%d" % b) for b in range(NBLK)]

    def chk(name):
        P.phase = "after_" + name
        if stop == name:
            raise _Stop()

    def tap(name, v, shape, dt=F32):
        if taps is None or name not in taps or name in tap_outs:
            return
        o = dr("tap_" + name, shape, dt, "ExternalOutput")
        tap_outs[name] = o
        stg = P.sb("tapstg_" + name, shape, dt)
        P.copy("vector", stg, v)
        d = P.dma("sync", o, stg)
        P.final_deps.append(d)

    cst = P.sb("cst", [128, 7 * 128], F32)
    ident_f = cst[:, 0:128]
    tri_f = cst[:, 128:256]
    pm_d = cst[:, 256:384]
    pm_r = cst[:, 384:512]
    nm_t = cst[:, 512:640]
    m01_f = cst[:, 640:768]
    ones_f = cst[:, 768:896]
    ident_b = P.sb("ident_b", [128, 128], BF16)
    ident4_b = P.sb("ident4_b", [128, 4, 128], BF16)
    ones_b = P.sb("ones_b", [128, 128], BF16)
    m01t4 = P.sb("m01t4", [128, 4, 128], BF16)
    small = P.sb("small", [128, 148], F32)
    gcols = small[:, 0:32]
    cw = small[:, 32:128]
    alog = small[:, 128:136]
    dtb = small[:, 136:144]
    bgate = small[:, 144:148]
    negA = P.sb("negA", [128, 8], F32)
    negb = P.sb("negb", [128, 4], F32)
    gfin = P.sb("gfin", [128, 1024], F32)
    wide_b = P.sb("wide_b", [128, 2048], BF16)
    gng = wide_b[:, 0:1024]
    lng = wide_b[:, 1024:2048]
    wsm = P.sb("wsm", [128, 8, 32], BF16)
    wg2 = P.sb("wg2", [16, 512], BF16)

    Sg = P.sb("Sg", [128, 8, 128], F32)
    Sgb = [P.sb("Sgb%d" % g, [128, 4, 128], BF16) for g in range(2)]
    Sl = P.sb("Sl", [128, 4, 256], F32)
    Slb = P.sb("Slb", [128, 4, 256], BF16)
    halo = P.sb("halo", [128, 24, 3], F32)
    KT = P.sb("KT", [128, 8, 256], BF16)
    Vt = P.sb("Vt", [128, 2, 1024], BF16)

    NSLOT = 5
    slots = [P.sb("wslot%d" % i, [128, 8, 512], BF16) for i in range(NSLOT)]
    xt = P.sb("xt", [128, NS, 1024], F32)
    ttiles = RR([P.sb("tt%d" % i, [128, 8, T], BF16) for i in range(4)])
    banks = [P.ps("bank%d" % i, [128, 512], F32) for i in range(8)]
    psA = RR(banks)

    hb = P.sb("hb", [128, 1024], BF16)
    smalls = RR([P.sb("sm%d" % i, [128, 8], F32) for i in range(24)])

    raws = RR([P.sb("raw%d" % i, [128, 3 + T], F32) for i in range(6)])
    convy = RR([P.sb("convy%d" % i, [128, T], F32) for i in range(6)])
    sc_q = RR([P.sb("scq%d" % i, [128, T], BF16) for i in range(2)])
    qn = [P.sb("qn%d" % g, [128, 4, T], BF16) for g in range(2)]
    kn = [P.sb("kn%d" % g, [128, 4, T], BF16) for g in range(2)]
    vs = [P.sb("vs%d" % g, [128, 4, T], BF16) for g in range(2)]
    zg = P.sb("zg", [128, NS, 1024], BF16)
    gab = P.sb("gab", [128, NS, 32], F32)
    lgT = P.sb("lgT", [16, T], BF16)
    vl = P.sb("vl", [128, NS, 1024], BF16)
    rg = P.sb("rg", [128, NS, 1024], BF16)
    sa = P.sb("sa", [128, NS, 1024], BF16)
    sb_ = P.sb("sb_", [128, NS, 1024], BF16)
    wide_f = RR([P.sb("widef%d" % i, [128, 512], F32) for i in range(2)])

    GT = P.sb("GT", [128, 4, 128], F32)
    Gs = P.sb("Gs", [128, 4, 128], F32)
    EG = P.sb("EG", [128, 4, 128], BF16)
    args = RR([P.sb("arg%d" % i, [128, 4, 128], F32) for i in range(2)])
    mk = lambda n: P.sb(n, [128, 4, 128], BF16)
    Fd, Fr, Dt = mk("Fd"), mk("Fr"), mk("Dt")
    Ld, Rr, AqkT, LdT, qdT, Rw, Ru, kdec = [mk(n) for n in ("Ld", "Rr", "AqkT", "LdT", "qdT", "Rw", "Ru", "kdec")]
    Mp = [mk("Mp0"), mk("Mp1")]
    MTp = [mk("MTp0"), mk("MTp1")]
    Yp = [mk("Yp0"), mk("Yp1")]
    Dp = [mk("Dp0"), mk("Dp1")]
    Zt, wTn, dlt = [mk(n) for n in ("Zt", "wTn", "dlt")]
    Wn, Nb, T1 = Dt, Fr, Fd
    tokb = [P.sb("tokb%d" % i, [128, 1024], BF16) for i in range(2)]
    oa = tokb[0]

    cs = P.sb("cs", [128, 4, T], F32)
    glp = [P.sb("glp%d" % i, [128, 4, T], BF16) for i in range(4)]
    smallsB = RR([P.sb("smB%d" % i, [128, 8], F32) for i in range(12)])
    lsc = RR([P.sb("lsc%d" % i, [128, T], F32) for i in range(2)])
    efac = RR([P.sb("efac%d" % i, [128, 128], F32) for i in range(4)])
    ATm = mk("ATm")
    kdl = mk("kdl")
    ob = tokb[1]

    pt = vl[:, 0, :].re("p (a b) -> p a b", a=4)
    pT = vl[:, 1, :].re("p (a b) -> p a b", a=8)
    ox = tokb[1]
    hidq = [P.sb("hidq%d" % i, [128, 8, T], BF16) for i in range(2)]
    yam = zg


    if taps is not None and "probe" in taps:
        for kb in range(64, 0, -1):
            try:
                es2 = ExitStack()
                es2.enter_context(nc.sbuf_tensor("probe%d" % kb, [128, kb * 256], F32))
                print("SBUF slack >= %d KB" % kb)
                es2.close()
                break
            except Exception as ex:
                pass
    order = []
    for q_ in range(nseq):
        order += [B_WK, B_WK + 1, B_WV, B_WV + 1]
        for st in range(nst):
            order += list(range(0, 28)) + [28, 29, 36, 37, 30, 31, 38, 39, 32, 33, 40, 41, 34, 35, 42, 43]
    wstate = {"issued": 0, "cur": 0}

    def wissue():
        i = wstate["issued"]
        if i < len(order):
            prep_block(order[i])
            P.dma("sync", slots[i % NSLOT].re("p a b -> p (a b)"), wbf[order[i]])
            wstate["issued"] += 1

    def wget(expect):
        c = wstate["cur"]
        assert order[c] == expect, (c, order[c], expect)
        while wstate["issued"] <= min(c + NSLOT - 1, len(order) - 1):
            wissue()
        wstate["cur"] += 1
        return slots[c % NSLOT]

    P.dma("sync", cst, consts_d)
    P.dma("sync", small, small_d)
    P.dma("sync", gfin, wide_d[:, 0:1024])
    P.copy("vector", ident_b, ident_f)
    P.copy("vector", ones_b, ones_f)
    for h in range(4):
        P.copy("vector", ident4_b[:, h, :], ident_f)
        P.copy("vector", m01t4[:, h, :], m01_f)
    P.dma("sync", wide_f.items[1][0:16, :], wg2_d)
    P.copy("vector", wg2, wide_f.items[1][0:16, :])
    P.act(negA, alog, AF.Exp)
    P.ts("vector", negA, negA, -1.0, None, ALU.mult)
    P.ts("vector", negb, bgate, -1.0, None, ALU.mult)
    for c in range(4):
        sf_ = wide_f.items[c % 2]
        P.dma("sync", sf_, wide_d[:, 1024 + c * 512:1024 + (c + 1) * 512])
        P.copy("vector", wide_b[:, c * 512:(c + 1) * 512], sf_)
    gain_of = {}
    for b in range(0, 18):
        gain_of[b] = 0
    gain_of[B_WQ] = gain_of[B_WQ + 1] = 1
    for b in range(B_W1, B_W1 + 8):
        gain_of[b] = 2
    for b in range(B_WK, B_WK + 4):
        gain_of[b] = 3
    pstf = [P.sb("pstf%d" % i, [128, 512], F32) for i in range(2)]
    pstb = [P.sb("pstb%d" % i, [128, 512], BF16) for i in range(2)]
    pstate = {"qi": 0, "done": set()}

    def prep_block(b):
        if b in pstate["done"]:
            return
        pstate["done"].add(b)
        gi = gain_of.get(b, None)
        for kc in range(8):
            qi = pstate["qi"]
            sf = pstf[qi % 2]
            sbf = pstb[qi % 2]
            eng = "vector" if qi % 2 == 0 else "scalar"
            P.dma("sync", sf, wblk_d[b][:, kc * 512:(kc + 1) * 512])
            if gi is None:
                P.copy(eng, sbf, sf)
            elif eng == "scalar":
                P.act(sbf, sf, AF.Identity, scale=gcols[:, gi * 8 + kc:gi * 8 + kc + 1])
            else:
                P.ts(eng, sbf, sf, gcols[:, gi * 8 + kc:gi * 8 + kc + 1], None, ALU.mult)
            P.dma("sync", wbf[b][:, kc * 512:(kc + 1) * 512], sbf)
            pstate["qi"] += 1

    stf = [wide_f.items[0]]
    sf = stf[0]
    P.dma("sync", sf[:, 0:256], wsm_d)
    for kc in range(8):
        P.ts("vector", wsm[:, kc, :], sf[:, kc * 32:(kc + 1) * 32], gcols[:, kc:kc + 1], None, ALU.mult)

    def rstd_of(xrow, ncols):
        ss = smalls.next()
        P.act(hb[:, 0:ncols], xrow, AF.Square, accum_out=ss[:, 0:1])
        rs = smalls.next()
        P.ts("vector", rs[:, 0:1], ss[:, 0:1], 1.0 / ncols, EPS, ALU.mult, ALU.add)
        P.act(rs[:, 1:2], rs[:, 0:1], AF.Ln)
        P.act(rs[:, 2:3], rs[:, 1:2], AF.Exp, scale=-0.5)
        return rs[:, 2:3]

    def to_T(src_b, dstT, col0, ncol=128, evac="scalar"):
        pb = psA.next().bc(BF16)
        for c in range(8):
            P.tr(pb[:, c * 128:(c + 1) * 128], src_b[:, c * 128:(c + 1) * 128], ident_b)
        P.copy(evac, dstT[:, :, col0:col0 + 128], pb.re("p (a b) -> p a b", a=8))

    def norm_T(xrow, dstT, col0):
        rs = rstd_of(xrow, 1024)
        P.ts("vector", hb, xrow, rs, None, ALU.mult)
        to_T(hb, dstT, col0)

    def kv_prep(q_):
        mT = ttiles.next()
        for mc in range(2):
            P.dma("sync", xt[:, 0, :], mem_d[q_, mc * 128:(mc + 1) * 128, :])
            norm_T(xt[:, 0, :], mT, mc * 128)
        for c2 in range(2):
            wt = wget(B_WK + c2)
            for cc in range(4):
                pb = psA.next()
                for kc in range(8):
                    P.mm(pb[:, 0:256], wt[:, kc, cc * 128:(cc + 1) * 128], mT[:, kc, 0:256], start=(kc == 0), stop=(kc == 7))
                P.copy("scalar", KT[:, c2 * 4 + cc, :], pb[:, 0:256])
        for c2 in range(2):
            wt = wget(B_WV + c2)
            for mc in range(2):
                pb = psA.next()
                for kc in range(8):
                    P.mm(pb, mT[:, kc, mc * 128:(mc + 1) * 128], wt[:, kc, :], start=(kc == 0), stop=(kc == 7))
                P.copy("scalar", Vt[:, mc, c2 * 512:(c2 + 1) * 512], pb)

    def gdn_front(h, hT, wt):
        pbs, rws, ys = [], [], []
        for j in range(3):
            pb = psA.next()
            for kc in range(8):
                P.mm(pb[:, 0:T], wt[:, kc, j * 128:(j + 1) * 128], hT[:, kc, :], start=(kc == 0), stop=(kc == 7))
            pbs.append(pb)
        for j in range(3):
            g = j * 8 + h
            raw = raws.next()
            P.copy("gpsimd", raw[:, 0:3], halo[:, g, :])
            P.copy("scalar", raw[:, 3:3 + T], pbs[j][:, 0:T])
            P.copy("gpsimd", halo[:, g, :], raw[:, T:T + 3])
            rws.append(raw)
        for j in range(3):
            g = j * 8 + h
            y = convy.next()
            P.ts("vector", y, rws[j][:, 3:3 + T], cw[:, g * 4 + 3:g * 4 + 4], None, ALU.mult)
            ys.append(y)
        for jj in (2, 1, 0):
            for j in range(3):
                g = j * 8 + h
                P.stt("vector", ys[j], rws[j][:, jj:jj + T], cw[:, g * 4 + jj:g * 4 + jj + 1], ys[j], ALU.mult, ALU.add)
        return ys, rws

    def gdn_back(h, ys, rws):
        g4, hh = divmod(h, 4)
        es = [rws[j][:, 3:3 + T] for j in range(3)]
        for j in range(3):
            P.act(es[j], ys[j], AF.Exp, scale=-1.0)
        for j in range(3):
            P.act(es[j], es[j], AF.Ln, bias=1.0)
        for j in range(3):
            P.act(es[j], es[j], AF.Exp, scale=-1.0)
        for j in range(2):
            P.tt("vector" if j == 0 else "gpsimd", ys[j], ys[j], es[j], ALU.mult)
        P.tt("gpsimd", vs[g4][:, hh, :], ys[2], es[2], ALU.mult)
        pns = []
        for j in range(2):
            sq = sc_q.next()
            P.tt("gpsimd", sq, ys[j], ys[j], ALU.mult)
            pn = psA.next()
            P.mm(pn[:, 0:T], ones_b, sq)
            pns.append(pn)
        for j in range(2):
            P.act(es[j], pns[j][:, 0:T], AF.Ln, bias=EPS)
        for j in range(2):
            if j == 0:
                P.act(es[j], es[j], AF.Exp, scale=-0.5, bias=float(np.log(HD ** -0.5)))
            else:
                P.act(es[j], es[j], AF.Exp, scale=-0.5)
        for j in range(2):
            dst = (qn if j == 0 else kn)[g4][:, hh, :]
            P.tt("vector", dst, ys[j], es[j], ALU.mult)

    zbank = {}

    def gdn_z(h, hT, wt):
        g4, hh = divmod(h, 4)
        for s in range(NS):
            if hh == 0:
                zbank[s] = psA.next()
                psA.hold(zbank[s])
            pb = zbank[s]
            for kc in range(8):
                P.mm(pb[:, hh * 128:(hh + 1) * 128], hT[:, kc, s * 128:(s + 1) * 128], wt[:, kc, 384:512], start=(kc == 0), stop=(kc == 7))
            if hh == 3:
                e = wide_f.next()
                P.act(e, pb, AF.Exp, scale=-1.0)
                P.act(e, e, AF.Ln, bias=1.0)
                P.act(e, e, AF.Exp, scale=-1.0)
                P.tt("vector", e, pb, e, ALU.mult)
                P.tt("gpsimd", zg[:, s, g4 * 512:(g4 + 1) * 512], e, gng[:, g4 * 512:(g4 + 1) * 512], ALU.mult)
                psA.release(pb)

    dsc = {}

    def gdn_scalars(s):
        t8 = smalls.next()
        P.tt("vector", t8, gab[:, s, 0:8], dtb, ALU.add)
        P.act(t8, t8, AF.Exp)
        sp8 = smalls.next()
        P.act(sp8, t8, AF.Ln, bias=1.0)
        g8 = smalls.next()
        P.tt("vector", g8, sp8, negA, ALU.mult)
        eb8 = smalls.next()
        P.act(eb8, gab[:, s, 8:16], AF.Exp, scale=-1.0)
        lb8 = smalls.next()
        P.act(lb8, eb8, AF.Ln, bias=1.0)
        pb = psA.next()
        P.mm(pb[:, 0:8], tri_f, g8)
        gc8 = smalls.next()
        P.copy("vector", gc8, pb[:, 0:8])
        gcb8 = smalls.next()
        P.tt("vector", gcb8, gc8, lb8, ALU.subtract)
        beta8 = smalls.next()
        P.act(beta8, lb8, AF.Exp, scale=-1.0)
        bg8 = smalls.next()
        P.act(bg8, gcb8, AF.Exp)
        tap("g8", g8, [128, 8]); tap("gc8", gc8, [128, 8]); tap("beta8", beta8, [128, 8])
        chk("G1")
        dsc[s] = (g8, gc8, gcb8, beta8, bg8)

    def gdn_unit(s, g4, oaT):
        c0 = s * 128
        cols = slice(c0, c0 + 128)
        if g4 == 0:
            gdn_scalars(s)
        g8, gc8, gcb8, beta8, bg8 = dsc[s]
        hs = [g4 * 4 + i for i in range(4)]
        for hh, h in enumerate(hs):
            P.act(GT[:, hh, :], tri_f, AF.Identity, scale=g8[:, h:h + 1])
        pb = psA.next()
        P.mm(pb, ones_f, GT.re("p a b -> p (a b)"))
        yield
        P.copy("scalar", Gs.re("p a b -> p (a b)"), pb)
        P.act(EG.re("p a b -> p (a b)"), Gs.re("p a b -> p (a b)"), AF.Exp)
        gl4 = smalls.next()
        P.copy("vector", gl4[:, 0:4], Gs[:, :, 127])
        gend4 = smalls.next()
        P.act(gend4[:, 0:4], gl4[:, 0:4], AF.Exp)
        ekd4 = smalls.next()
        P.tt("vector", ekd4[:, 0:4], gl4[:, 0:4], gc8[:, g4 * 4:g4 * 4 + 4], ALU.subtract)
        P.act(ekd4[:, 0:4], ekd4[:, 0:4], AF.Exp)
        chk("G2")
        pbk = psA.next().bc(BF16)
        pbv = psA.next().bc(BF16)
        for hh, h in enumerate(hs):
            P.tr(pbk[:, hh * 128:(hh + 1) * 128], kn[g4][:, hh, cols], ident_b)
        for hh, h in enumerate(hs):
            P.tr(pbv[:, hh * 128:(hh + 1) * 128], vs[g4][:, hh, cols], ident_b)
        yield
        for hh, h in enumerate(hs):
            P.ts("vector", Rw[:, hh, :], pbk[:, hh * 128:(hh + 1) * 128], bg8[:, h:h + 1], None, ALU.mult)
            P.act(kdec[:, hh, :], pbk[:, hh * 128:(hh + 1) * 128], AF.Identity, scale=ekd4[:, hh:hh + 1])
            P.act(Ru[:, hh, :], pbv[:, hh * 128:(hh + 1) * 128], AF.Identity, scale=beta8[:, h:h + 1])
        chk("G3")
        pkk = psA.next()
        pqk = psA.next()
        for hh, h in enumerate(hs):
            P.mm(pkk[:, hh * 128:(hh + 1) * 128], kn[g4][:, hh, cols], kn[g4][:, hh, cols])
        for hh, h in enumerate(hs):
            P.mm(pqk[:, hh * 128:(hh + 1) * 128], kn[g4][:, hh, cols], qn[g4][:, hh, cols])
        yield
        a_d = args.next()
        for hh, h in enumerate(hs):
            P.stt("vector" if hh % 2 == 0 else "gpsimd", a_d[:, hh, :], Gs[:, hh, :], gcb8[:, h:h + 1], pm_d, ALU.subtract, ALU.max)
        P.act(Fd.re("p a b -> p (a b)"), a_d.re("p a b -> p (a b)"), AF.Exp, scale=-1.0)
        a_r = args.next()
        for hh, h in enumerate(hs):
            P.stt("vector" if hh % 2 == 0 else "gpsimd", a_r[:, hh, :], Gs[:, hh, :], gcb8[:, h:h + 1], pm_r, ALU.subtract, ALU.max)
        P.act(Fr.re("p a b -> p (a b)"), a_r.re("p a b -> p (a b)"), AF.Exp, scale=-1.0)
        a_t = args.next()
        for hh, h in enumerate(hs):
            P.stt("vector" if hh % 2 == 0 else "gpsimd", a_t[:, hh, :], Gs[:, hh, :], gc8[:, h:h + 1], nm_t, ALU.subtract, ALU.min)
        P.act(Dt.re("p a b -> p (a b)"), a_t.re("p a b -> p (a b)"), AF.Exp)
        fl = lambda t_: t_.re("p a b -> p (a b)")
        P.tt("vector", fl(Ld), pkk, fl(Fd), ALU.mult)
        P.tt("vector", fl(Rr), pkk, fl(Fr), ALU.mult)
        P.tt("vector", fl(AqkT), pqk, fl(Dt), ALU.mult)
        for hh in range(4):
            P.tt("gpsimd", qdT[:, hh, :], qn[g4][:, hh, cols], EG[:, hh, :], ALU.mult)
        chk("G4")
        pbt = psA.next().bc(BF16)
        for hh in range(4):
            P.tr(pbt[:, hh * 128:(hh + 1) * 128], Ld[:, hh, :], ident_b)
        yield
        P.copy("scalar", fl(LdT), pbt[:, 0:512])
        chk("G5")
        P.tt("vector", fl(Yp[0]), fl(ident4_b), fl(LdT), ALU.subtract)
        P.tt("gpsimd", fl(Dp[0]), fl(ident4_b), fl(Ld), ALU.subtract)

        def squares(Mc, MTc, Mn, MTn):
            p1 = psA.next()
            p2 = psA.next()
            for hh in range(4):
                P.mm(p1[:, hh * 128:(hh + 1) * 128], MTc[:, hh, :], Mc[:, hh, :])
            for hh in range(4):
                P.mm(p2[:, hh * 128:(hh + 1) * 128], Mc[:, hh, :], MTc[:, hh, :])
            return p1, p2

        Mc, MTc = Ld, LdT
        Mn, MTn = Mp[1], MTp[1]
        p1, p2 = squares(Mc, MTc, Mn, MTn)
        yield
        P.copy("scalar", fl(Mn), p1)
        P.copy("vector", fl(MTn), p2)
        yi = 0
        for m in range(1, 5):
            Mc, MTc = Mn, MTn
            p3 = psA.next()
            p4 = psA.next()
            for hh in range(4):
                P.mm(p3[:, hh * 128:(hh + 1) * 128], Mc[:, hh, :], Yp[yi][:, hh, :], start=True, stop=False)
                P.mm(p3[:, hh * 128:(hh + 1) * 128], ident_b, Yp[yi][:, hh, :], start=False, stop=True)
            for hh in range(4):
                P.mm(p4[:, hh * 128:(hh + 1) * 128], MTc[:, hh, :], Dp[yi][:, hh, :], start=True, stop=False)
                P.mm(p4[:, hh * 128:(hh + 1) * 128], ident_b, Dp[yi][:, hh, :], start=False, stop=True)
            if m < 4:
                Mn, MTn = Mp[(m + 1) % 2], MTp[(m + 1) % 2]
                p1, p2 = squares(Mc, MTc, Mn, MTn)
            yield
            P.copy("scalar", fl(Yp[1 - yi]), p3)
            P.copy("vector", fl(Dp[1 - yi]), p4)
            if m < 4:
                P.copy("scalar", fl(Mn), p1)
                P.copy("vector", fl(MTn), p2)
            yi = 1 - yi
        Yd, Dd = Yp[yi], Dp[yi]
        chk("G6")
        DRu, DRw = Mp[0], MTp[0]
        p1 = psA.next()
        p2 = psA.next()
        p3 = psA.next()
        p4 = psA.next()
        for hh in range(4):
            P.mm(p1[:, hh * 128:(hh + 1) * 128], Rr[:, hh, :], Yd[:, hh, :])
        for hh in range(4):
            P.mm(p2[:, hh * 128:(hh + 1) * 128], Yd[:, hh, :], Rr[:, hh, :])
        for hh in range(4):
            P.mm(p3[:, hh * 128:(hh + 1) * 128], Yd[:, hh, :], Ru[:, hh, :])
        for hh in range(4):
            P.mm(p4[:, hh * 128:(hh + 1) * 128], Yd[:, hh, :], Rw[:, hh, :])
        yield
        P.tt("vector", fl(Wn), fl(ident4_b), p1, ALU.subtract)
        P.copy("scalar", fl(Nb), p2)
        P.copy("scalar", fl(DRu), p3)
        P.copy("vector", fl(DRw), p4)
        p3 = psA.next()
        for hh in range(4):
            P.mm(p3[:, hh * 128:(hh + 1) * 128], Nb[:, hh, :], Wn[:, hh, :])
        yield
        P.copy("scalar", fl(T1), p3)
        p4 = psA.next()
        for hh in range(4):
            P.mm(p4[:, hh * 128:(hh + 1) * 128], Nb[:, hh, :], T1[:, hh, :], start=True, stop=False)
            P.mm(p4[:, hh * 128:(hh + 1) * 128], ident_b, Wn[:, hh, :], start=False, stop=True)
        yield
        P.copy("scalar", fl(Zt), p4)
        chk("G7")
        p6 = psA.next()
        for hh in range(4):
            P.mm(p6[:, hh * 128:(hh + 1) * 128], DRw[:, hh, :], Zt[:, hh, :])
        yield
        P.ts("vector", fl(wTn), p6, -1.0, None, ALU.mult)
        p7 = psA.next()
        for hh in range(4):
            P.mm(p7[:, hh * 128:(hh + 1) * 128], Zt[:, hh, :], DRu[:, hh, :], start=True, stop=False)
            P.mm(p7[:, hh * 128:(hh + 1) * 128], wTn[:, hh, :], Sgb[g4][:, hh, :], start=False, stop=True)
        yield
        P.copy("scalar", fl(dlt), p7)
        p8 = psA.next()
        for hh in range(4):
            P.mm(p8[:, hh * 128:(hh + 1) * 128], qdT[:, hh, :], Sgb[g4][:, hh, :], start=True, stop=False)
            P.mm(p8[:, hh * 128:(hh + 1) * 128], AqkT[:, hh, :], dlt[:, hh, :], start=False, stop=True)
        p9 = psA.next()
        for hh in range(4):
            P.mm(p9[:, hh * 128:(hh + 1) * 128], kdec[:, hh, :], dlt[:, hh, :])
        yield
        for hh, h in enumerate(hs):
            P.stt("vector", Sg[:, h, :], Sg[:, h, :], gend4[:, hh:hh + 1], p9[:, hh * 128:(hh + 1) * 128], ALU.mult, ALU.add)
        P.copy("scalar", fl(Sgb[g4]), Sg[:, g4 * 4:(g4 + 1) * 4, :].re("p a b -> p (a b)"))
        sq = wide_f.next()
        P.act(sq, p8, AF.Square)
        ss4 = smalls.next()
        P.reduce("vector", ss4[:, 0:4], sq.re("p (a b) -> p a b", a=4), ALU.add)
        P.ts("vector", ss4[:, 0:4], ss4[:, 0:4], 1.0 / HD, EPS, ALU.mult, ALU.add)
        P.act(ss4[:, 0:4], ss4[:, 0:4], AF.Ln)
        rs4 = smalls.next()
        P.act(rs4[:, 0:4], ss4[:, 0:4], AF.Exp, scale=-0.5)
        for hh, h in enumerate(hs):
            P.stt("vector", oa[:, h * 128:(h + 1) * 128], p8[:, hh * 128:(hh + 1) * 128], rs4[:, hh:hh + 1],
                  zg[:, s, h * 128:(h + 1) * 128], ALU.mult, ALU.mult)
        if g4 == 1:
            tap("oa", oa, [128, 1024], BF16)
            to_T(oa, oaT, c0)
        yield

    def gla_proj(h, hT, wt):
        pz = psA.next()
        P.mm(pz[:, 0:T], wg2[:, h * 128:(h + 1) * 128], lgT)
        l_ = lsc.next()
        P.act(l_, pz[:, 0:T], AF.Exp, scale=-1.0, bias=negb[:, h:h + 1])
        P.act(l_, l_, AF.Ln, bias=1.0)
        for s in range(NS):
            P.scan(cs[:, h, s * 128:(s + 1) * 128], ones_f, l_[:, s * 128:(s + 1) * 128], 0.0, ALU.mult, ALU.add)
        yield
        pq = psA.next()
        psA.hold(pq)
        for kc in range(8):
            P.mm(pq[:, 0:T], wt[:, kc, 0:128], hT[:, kc, :], start=(kc == 0), stop=(kc == 7))
        yield
        pk = psA.next()
        for kc in range(8):
            P.mm(pk[:, 0:T], wt[:, kc, 128:256], hT[:, kc, :], start=(kc == 0), stop=(kc == 7))
        for s in range(NS):
            cols = slice(s * 128, (s + 1) * 128)
            cv = cs[:, h, cols]
            c2 = smallsB.next()
            P.ts("vector", c2[:, 0:1], cv[:, 64:65], 1.0 / 16, None, ALU.mult)
            P.ts("vector", c2[:, 1:2], cv[:, 64:65], -1.0 / 16, None, ALU.mult)
            P.ts("vector", c2[:, 2:3], cv[:, 127:128], -1.0 / 16, None, ALU.mult)
            eq = efac.next()
            P.act(eq, cv, AF.Exp, scale=-1.0 / 16, bias=c2[:, 0:1])
            ek = efac.next()
            P.act(ek, cv, AF.Exp, scale=1.0 / 16, bias=c2[:, 1:2])
            ed = efac.next()
            P.act(ed, cv, AF.Exp, scale=-1.0 / 16)
            ekd = efac.next()
            P.act(ekd, cv, AF.Exp, scale=1.0 / 16, bias=c2[:, 2:3])
            sc = float(LKD ** -0.5)
            P.stt("vector", glp[0][:, h, cols], pq[:, cols], sc, eq, ALU.mult, ALU.mult)
            P.tt("vector", glp[1][:, h, cols], pk[:, cols], ek, ALU.mult)
            P.stt("vector", glp[2][:, h, cols], pq[:, cols], sc, ed, ALU.mult, ALU.mult)
            P.tt("vector", glp[3][:, h, cols], pk[:, cols], ekd, ALU.mult)
            P.copy("gpsimd", gendl[:, s, h:h + 1], ed[:, 127:128])
        psA.release(pq)
        yield
        for s in range(NS):
            pv = psA.next()
            for kc in range(8):
                P.mm(pv[:, 0:256], hT[:, kc, s * 128:(s + 1) * 128], wt[:, kc, 256:512], start=(kc == 0), stop=(kc == 7))
            P.copy("scalar", vl[:, s, h * 256:(h + 1) * 256], pv[:, 0:256])
            yield

    gendl = P.sb("gendl", [128, NS, 4], F32)

    def gla_r(c, hT, wt):
        for s in range(NS):
            pb = psA.next()
            for kc in range(8):
                P.mm(pb, hT[:, kc, s * 128:(s + 1) * 128], wt[:, kc, :], start=(kc == 0), stop=(kc == 7))
            e = wide_f.next()
            P.act(e, pb, AF.Exp, scale=-1.0)
            P.act(e, e, AF.Ln, bias=1.0)
            P.act(e, e, AF.Exp, scale=-1.0)
            P.tt("vector", e, pb, e, ALU.mult)
            P.tt("gpsimd", rg[:, s, c * 512:(c + 1) * 512], e, lng[:, c * 512:(c + 1) * 512], ALU.mult)
            yield

    def gla_core(s, obT):
        c0 = s * 128
        cols = slice(c0, c0 + 128)
        fl = lambda t_: t_.re("p a b -> p (a b)")
        pa = psA.next()
        for h in range(4):
            P.mm(pa[:, h * 128:(h + 1) * 128], glp[1][:, h, cols], glp[0][:, h, cols])
        P.tt("vector", fl(ATm), pa, fl(m01t4), ALU.mult)
        yield
        pbt = psA.next().bc(BF16)
        for h in range(4):
            P.tr(pbt[:, h * 128:(h + 1) * 128], glp[3][:, h, cols], ident_b)
        P.copy("scalar", fl(kdl), pbt[:, 0:512])
        yield
        po = [psA.next(), psA.next()]
        for h in range(4):
            o_ = po[h // 2][:, (h % 2) * 256:(h % 2 + 1) * 256]
            P.mm(o_, glp[2][:, h, cols], Slb[:, h, :], start=True, stop=False)
            P.mm(o_, ATm[:, h, :], vl[:, s, h * 256:(h + 1) * 256], start=False, stop=True)
        ss4 = smallsB.next()
        for half in range(2):
            sq = wide_f.next()
            P.act(sq, po[half], AF.Square)
            P.reduce("vector", ss4[:, half * 2:half * 2 + 2], sq.re("p (a b) -> p a b", a=2), ALU.add)
        P.ts("vector", ss4[:, 0:4], ss4[:, 0:4], 1.0 / LVD, EPS, ALU.mult, ALU.add)
        P.act(ss4[:, 0:4], ss4[:, 0:4], AF.Ln)
        rs4 = smallsB.next()
        P.act(rs4[:, 0:4], ss4[:, 0:4], AF.Exp, scale=-0.5)
        for h in range(4):
            P.stt("vector", ob[:, h * 256:(h + 1) * 256], po[h // 2][:, (h % 2) * 256:(h % 2 + 1) * 256], rs4[:, h:h + 1],
                  rg[:, s, h * 256:(h + 1) * 256], ALU.mult, ALU.mult)
        yield
        pu = [psA.next(), psA.next()]
        for h in range(4):
            P.mm(pu[h // 2][:, (h % 2) * 256:(h % 2 + 1) * 256], kdl[:, h, :], vl[:, s, h * 256:(h + 1) * 256])
        for h in range(4):
            P.stt("vector", Sl[:, h, :], Sl[:, h, :], gendl[:, s, h:h + 1], pu[h // 2][:, (h % 2) * 256:(h % 2 + 1) * 256], ALU.mult, ALU.add)
        P.copy("scalar", fl(Slb), fl(Sl))
        yield
        tap("ob", ob, [128, 1024], BF16)
        to_T(ob, obT, c0)
        yield

    def gates_proj(i, hT, wt):
        dst = sa if i < 2 else sb_
        c = i % 2
        for s in range(NS):
            pb = psA.next()
            for kc in range(8):
                P.mm(pb, hT[:, kc, s * 128:(s + 1) * 128], wt[:, kc, :], start=(kc == 0), stop=(kc == 7))
            e = wide_f.next()
            P.act(e, pb, AF.Exp, scale=-1.0)
            P.act(e, e, AF.Ln, bias=1.0)
            P.act(dst[:, s, c * 512:(c + 1) * 512], e, AF.Exp, scale=-1.0)
            yield

    def stage_A(q_, st):
        t0 = st * T
        P.phase = "A_start"
        P.dma("sync", xt, x_d[q_, t0:t0 + T, :].rearrange("(s p) d -> p s d", p=128))
        hT = ttiles.next()
        for s in range(NS):
            norm_T(xt[:, s, :], hT, s * 128)
        tap("hT", hT, [128, 8, T], BF16)
        chk("A_norm")
        for s in range(NS):
            pb = psA.next()
            for kc in range(8):
                P.mm(pb[:, 0:32], hT[:, kc, s * 128:(s + 1) * 128], wsm[:, kc, :], start=(kc == 0), stop=(kc == 7))
            P.copy("scalar", gab[:, s, :], pb[:, 0:32])
        pb = psA.next()
        for kc in range(8):
            P.mm(pb[0:16, 0:T], wsm[:, kc, 16:32], hT[:, kc, :], start=(kc == 0), stop=(kc == 7))
        P.copy("scalar", lgT, pb[0:16, 0:T])
        chk("A_small")
        prev = None
        for h in range(8):
            wt = wget(B_GDN + h)
            cur = gdn_front(h, hT, wt)
            gdn_z(h, hT, wt)
            if prev is not None:
                gdn_back(*prev)
            prev = (h,) + cur
        gdn_back(*prev)
        tap("qn0", qn[0], [128, 4, T], BF16); tap("kn0", kn[0], [128, 4, T], BF16); tap("vs0", vs[0], [128, 4, T], BF16)
        tap("zg", zg, [128, NS, 1024], BF16)
        chk("A_gdnproj")
        oaT = ttiles.next()
        obT = ttiles.next()
        def side_gen():
            for h in range(4):
                yield from gla_proj(h, hT, wget(B_GLA + h))
            for c in range(2):
                yield from gla_r(c, hT, wget(B_R + c))
            for s in range(NS):
                yield from gla_core(s, obT)
            for i in range(4):
                yield from gates_proj(i, hT, wget(B_GATE + i))

        side_it = side_gen()
        for s in range(NS):
            for g4 in range(2):
                for _ in gdn_unit(s, g4, oaT):
                    next(side_it, None)
        chk("A_gdncore")
        for _ in side_it:
            pass
        chk("A_glacore")
        return oaT, obT

    def resid_proj(srcT, s, blk0, after=None):
        pass

    def stage_B(q_, st, oaT, obT):
        t0 = st * T
        P.phase = "B_branch"
        for c in range(2):
            wt = wget(B_BG + c)
            for s in range(NS):
                pb = psA.next()
                for kc in range(8):
                    P.mm(pb, oaT[:, kc, s * 128:(s + 1) * 128], wt[:, kc, :], start=(kc == 0), stop=(kc == 7))
                P.tt("vector", yam[:, s, c * 512:(c + 1) * 512], pb, sa[:, s, c * 512:(c + 1) * 512], ALU.mult)
        mTt = ttiles.next()
        for c in range(2):
            wt = wget(B_BL + c)
            for s in range(NS):
                pb = psA.next()
                for kc in range(8):
                    P.mm(pb, obT[:, kc, s * 128:(s + 1) * 128], wt[:, kc, :], start=(kc == 0), stop=(kc == 7))
                m2 = wide_f.next()
                P.tt("vector", m2, pb, sb_[:, s, c * 512:(c + 1) * 512], ALU.mult)
                P.tt("gpsimd", tokb[s][:, c * 512:(c + 1) * 512], m2, yam[:, s, c * 512:(c + 1) * 512], ALU.add)
        for s in range(NS):
            to_T(tokb[s], mTt, s * 128)
        for c in range(2):
            wt = wget(B_OUT + c)
            for s in range(NS):
                pb = psA.next()
                for kc in range(8):
                    P.mm(pb, mTt[:, kc, s * 128:(s + 1) * 128], wt[:, kc, :], start=(kc == 0), stop=(kc == 7))
                P.tt("vector", xt[:, s, c * 512:(c + 1) * 512], pb, xt[:, s, c * 512:(c + 1) * 512], ALU.add)
        tap("x1", xt, [128, NS, 1024])
        chk("B_x1")
        h2T = ttiles.next()
        for s in range(NS):
            norm_T(xt[:, s, :], h2T, s * 128)
        qT = ttiles.next()
        for c2 in range(2):
            wt = wget(B_WQ + c2)
            for cc in range(4):
                pb = psA.next()
                for kc in range(8):
                    P.mm(pb[:, 0:T], wt[:, kc, cc * 128:(cc + 1) * 128], h2T[:, kc, :], start=(kc == 0), stop=(kc == 7))
                P.act(qT[:, c2 * 4 + cc, :], pb[:, 0:T], AF.Identity, scale=float(256 ** -0.5))
        oxT = ttiles.next()
        for s in range(NS):
            cols = slice(s * 128, (s + 1) * 128)
            psc = [psA.next(), psA.next()]
            for h in range(4):
                o_ = psc[h // 2][:, (h % 2) * 256:(h % 2 + 1) * 256]
                for c in range(2):
                    P.mm(o_, qT[:, 2 * h + c, cols], KT[:, 2 * h + c, :], start=(c == 0), stop=(c == 1))
            mx = smalls.next()
            for half in range(2):
                P.reduce("vector", mx[:, half * 2:half * 2 + 2], psc[half].re("p (a b) -> p a b", a=2), ALU.max)
            P.ts("vector", mx[:, 0:4], mx[:, 0:4], -1.0, None, ALU.mult)
            sm4 = smalls.next()
            for h in range(4):
                P.act(pt[:, h, :], psc[h // 2][:, (h % 2) * 256:(h % 2 + 1) * 256], AF.Exp, bias=mx[:, h:h + 1], accum_out=sm4[:, h:h + 1])
            rs4 = smalls.next()
            P.recip("vector", rs4[:, 0:4], sm4[:, 0:4])
            pbt = psA.next().bc(BF16)
            for h in range(4):
                for mc in range(2):
                    P.tr(pbt[:, (2 * h + mc) * 128:(2 * h + mc + 1) * 128], pt[:, h, mc * 128:(mc + 1) * 128], ident_b)
            P.copy("scalar", pT.re("p a b -> p (a b)"), pbt)
            pov = [psA.next(), psA.next()]
            for h in range(4):
                o_ = pov[h // 2][:, (h % 2) * 256:(h % 2 + 1) * 256]
                for mc in range(2):
                    P.mm(o_, pT[:, 2 * h + mc, :], Vt[:, mc, h * 256:(h + 1) * 256], start=(mc == 0), stop=(mc == 1))
            for h in range(4):
                P.ts("vector", ox[:, h * 256:(h + 1) * 256], pov[h // 2][:, (h % 2) * 256:(h % 2 + 1) * 256], rs4[:, h:h + 1], None, ALU.mult)
            to_T(ox, oxT, s * 128)
        for c in range(2):
            wt = wget(B_WO + c)
            for s in range(NS):
                pb = psA.next()
                for kc in range(8):
                    P.mm(pb, oxT[:, kc, s * 128:(s + 1) * 128], wt[:, kc, :], start=(kc == 0), stop=(kc == 7))
                P.tt("vector", xt[:, s, c * 512:(c + 1) * 512], pb, xt[:, s, c * 512:(c + 1) * 512], ALU.add)
        tap("x2", xt, [128, NS, 1024])
        chk("B_x2")
        P.phase = "B_mlp"
        h3T = ttiles.next()
        for s in range(NS):
            norm_T(xt[:, s, :], h3T, s * 128)
        acc = {}
        for fg in range(4):
            hq = hidq[fg % 2]
            for c2 in range(2):
                wt = wget(B_W1 + fg * 2 + c2)
                for cc in range(4):
                    fi = c2 * 4 + cc
                    pb = psA.next()
                    for kc in range(8):
                        P.mm(pb[:, 0:T], wt[:, kc, cc * 128:(cc + 1) * 128], h3T[:, kc, :], start=(kc == 0), stop=(kc == 7))
                    r_ = wide_f.next()
                    P.act(r_[:, 0:T], pb[:, 0:T], AF.Relu)
                    P.tt("vector" if fi % 2 == 0 else "gpsimd", hq[:, fi, :], r_[:, 0:T], r_[:, 0:T], ALU.mult)
            if fg == 0:
                for s in range(NS):
                    for c in range(2):
                        acc[(s, c)] = psA.next()
                        psA.hold(acc[(s, c)])
            for c in range(2):
                wt = wget(B_W2 + fg * 2 + c)
                for s in range(NS):
                    for kc in range(8):
                        P.mm(acc[(s, c)], hq[:, kc, s * 128:(s + 1) * 128], wt[:, kc, :],
                             start=(fg == 0 and kc == 0), stop=(fg == 3 and kc == 7))
        for s in range(NS):
            for c in range(2):
                P.tt("vector", xt[:, s, c * 512:(c + 1) * 512], acc[(s, c)], xt[:, s, c * 512:(c + 1) * 512], ALU.add)
                psA.release(acc[(s, c)])
        tap("x3", xt, [128, NS, 1024])
        chk("B_x3")
        P.phase = "B_final"
        for s in range(NS):
            rs = rstd_of(xt[:, s, :], 1024)
            P.stt("vector", xt[:, s, :], xt[:, s, :], rs, gfin, ALU.mult, ALU.mult)
        d = P.dma("sync", y_d[q_, t0:t0 + T, :].rearrange("(s p) d -> p s d", p=128), xt)
        P.final_deps.append(d)

    try:
      chk("prep")
      for q_ in range(nseq):
        P.memset("vector", Sg.re("p a b -> p (a b)"), 0.0)
        for g in range(2):
            P.memset("gpsimd", Sgb[g].re("p a b -> p (a b)"), 0.0)
        P.memset("vector", Sl.re("p a b -> p (a b)"), 0.0)
        P.memset("gpsimd", Slb.re("p a b -> p (a b)"), 0.0)
        P.memset("gpsimd", halo.re("p a b -> p (a b)"), 0.0)
        kv_prep(q_)
        chk("kv")
        for st in range(nst):
            oaT, obT = stage_A(q_, st)
            chk("A")
            stage_B(q_, st, oaT, obT)
    except _Stop:
        pass

    bes = ExitStack()
    finals = P.finalize(bes)
    bes.enter_context(nc.allow_low_precision(reason="bf16 matmul operands by design; fp32 accumulation"))
    block = bes.enter_context(nc.Block())
    P.emit(block, finals)
    bes.close()
    es.close()
    ninst = {e: len(P.ops[e]) for e in ENGS}
    return nc, tap_outs, dict(ninst=ninst, nwaits=P.nwaits, nsems=P.nsems, op_phase=P.op_phase)


N_CORES = 8


def kernel(**inputs):
    x = np.asarray(inputs["x"], dtype=np.float32)
    mem = np.asarray(inputs["mem"], dtype=np.float32)
    B, S, _ = x.shape
    nseq = B // N_CORES
    lay = host_layout(inputs)
    nc, _, _ = build(nseq, S)
    in_maps = []
    for c in range(N_CORES):
        m = dict(lay)
        m["x"] = np.ascontiguousarray(x[c * nseq:(c + 1) * nseq])
        m["mem"] = np.ascontiguousarray(mem[c * nseq:(c + 1) * nseq])
        in_maps.append(m)
    res = run_bass_kernel_spmd(nc, in_maps, core_ids=list(range(N_CORES)))
    out = np.concatenate([np.asarray(r["y"], dtype=np.float32) for r in res.results], axis=0)
    return out
```

```python
from contextlib import ExitStack
from concourse.bass_utils import run_bass_kernel_spmd
import numpy as np
import concourse.bass as bass
import concourse.mybir as mybir

F32 = mybir.dt.float32
BF16 = mybir.dt.bfloat16
AF = mybir.ActivationFunctionType
ALU = mybir.AluOpType
AX = mybir.AxisListType

ENGS = ("sync", "scalar", "vector", "gpsimd", "tensor")


class Buf:
    __slots__ = ("name", "last_w", "readers", "excl")

    def __init__(self, name):
        self.name = name
        self.last_w = None
        self.readers = []
        self.excl = False


class V:
    __slots__ = ("ap", "buf", "dram")

    def __init__(self, ap, buf, dram=False):
        self.ap = ap
        self.buf = buf
        self.dram = dram

    def __getitem__(self, idx):
        return V(self.ap[idx], self.buf, self.dram)

    def re(self, pat, **kw):
        return V(self.ap.rearrange(pat, **kw), self.buf, self.dram)

    def bc(self, dt):
        return V(self.ap.bitcast(dt), self.buf, self.dram)


class Tile(V):
    def __init__(self, t, name):
        V.__init__(self, t[:], Buf(name))
        self.t = t


class Op:
    __slots__ = ("eng", "fn", "deps", "is_dma", "sem", "sigval", "signal", "idx", "waits")

    def __init__(self, eng, fn, deps, is_dma):
        self.eng = eng
        self.fn = fn
        self.deps = deps
        self.is_dma = is_dma
        self.sem = None
        self.sigval = 0
        self.signal = False
        self.waits = []


class Prog:
    def __init__(self, nc, es, same_engine_sync=False):
        self.nc = nc
        self.es = es
        self.ops = {e: [] for e in ENGS}
        self.same_engine_sync = same_engine_sync
        self.dma_sems = {}
        self.final_deps = []
        self.nsb = 0
        self.phase = ""
        self.op_phase = {e: [] for e in ENGS}

    def sb(self, name, shape, dt):
        t = self.es.enter_context(self.nc.sbuf_tensor("sb_" + name, list(shape), dt))
        return Tile(t, name)

    def dram(self, ap, name):
        return V(ap, Buf(name), True)

    def ps(self, name, shape, dt):
        t = self.es.enter_context(self.nc.psum_tensor("ps_" + name, list(shape), dt))
        tl = Tile(t, name)
        tl.buf.excl = True
        return tl

    def rec(self, eng, fn, reads=(), writes=(), is_dma=False, dma_key=None):
        deps = set()
        rb = [v.buf for v in reads if isinstance(v, V)]
        wb = [v.buf for v in writes if isinstance(v, V)]
        for b in rb:
            if b.last_w is not None:
                deps.add(b.last_w)
            if b.excl:
                for r in b.readers:
                    if r.eng != eng:
                        deps.add(r)
        for b in wb:
            lw = b.last_w
            if lw is not None and (lw.eng != eng or lw.is_dma or is_dma):
                deps.add(lw)
            for r in b.readers:
                if r.eng != eng or r.is_dma or is_dma:
                    deps.add(r)
        op = Op(eng, fn, deps, is_dma)
        if is_dma:
            op.sem = dma_key
        for b in wb:
            b.last_w = op
            b.readers = []
        for b in rb:
            if b not in wb:
                b.readers.append(op)
        op.idx = len(self.ops[eng])
        self.ops[eng].append(op)
        self.op_phase[eng].append(self.phase)
        return op

    def finalize(self, block_es):
        nc = self.nc
        for e in ENGS:
            for op in self.ops[e]:
                if op.is_dma:
                    op.signal = True
                for d in op.deps:
                    if d.is_dma:
                        continue
                    d.signal = True
        last = {}
        for e in ENGS:
            for op in self.ops[e]:
                if op.is_dma:
                    last[op.sem] = op
        for op in last.values():
            if op not in self.final_deps:
                self.final_deps.append(op)
        for d in self.final_deps:
            d.signal = True
        eng_sem = {e: block_es.enter_context(nc.semaphore("sem_" + e)) for e in ENGS}
        dma_sem = {}
        for e in ENGS:
            cnt = 0
            for op in self.ops[e]:
                if op.is_dma:
                    key = op.sem
                    if key not in dma_sem:
                        dma_sem[key] = [block_es.enter_context(nc.semaphore("dsem_%d" % len(dma_sem))), 0]
                    ent = dma_sem[key]
                    ent[1] += 16
                    op.sem = ent[0]
                    op.sigval = ent[1]
                elif op.signal:
                    cnt += 1
                    op.sem = eng_sem[e]
                    op.sigval = cnt
        nwaits = 0
        for e in ENGS:
            seen = {}
            for op in self.ops[e]:
                need = {}
                for d in op.deps:
                    if d is op:
                        continue
                    k = id(d.sem)
                    if seen.get(k, 0) >= d.sigval:
                        continue
                    if k not in need or need[k][1] < d.sigval:
                        need[k] = (d.sem, d.sigval)
                for k, (s, v) in need.items():
                    seen[k] = v
                    op.waits.append((s, v))
                    nwaits += 1
        self.nwaits = nwaits
        self.nsems = len(dma_sem) + len(ENGS)
        finals = [(d.sem, d.sigval) for d in self.final_deps]
        return finals

    def emit(self, block, finals, final_eng="gpsimd"):
        P = self

        def run(ename, e):
            for op in P.ops[ename]:
                for (s, v) in op.waits:
                    e.wait_ge(s, v)
                ins = op.fn(e)
                if op.signal:
                    ins.then_inc(op.sem, 16 if op.is_dma else 1)
            if ename == final_eng:
                for (s, v) in finals:
                    e.wait_ge(s, v)

        @block.sync
        def _(e):
            run("sync", e)

        @block.scalar
        def _(e):
            run("scalar", e)

        @block.vector
        def _(e):
            run("vector", e)

        @block.gpsimd
        def _(e):
            run("gpsimd", e)

        @block.tensor
        def _(e):
            run("tensor", e)

    def dma(self, eng, out, in_, key=None):
        k = key
        if k is None:
            if isinstance(out, V) and not out.dram:
                k = out.buf
            elif isinstance(in_, V) and not in_.dram:
                k = in_.buf
            else:
                k = out.buf if isinstance(out, V) else in_.buf
        o = out.ap if isinstance(out, V) else out
        i = in_.ap if isinstance(in_, V) else in_
        return self.rec(eng, lambda e: e.dma_start(out=o, in_=i),
                        reads=[in_], writes=[out], is_dma=True, dma_key=k)

    def mm(self, out, lhsT, rhs, start=True, stop=True):
        return self.rec("tensor", lambda e: e.matmul(out.ap, lhsT.ap, rhs.ap, start=start, stop=stop),
                        reads=[lhsT, rhs], writes=[out])

    def tr(self, out, in_, ident):
        return self.rec("tensor", lambda e: e.transpose(out.ap, in_.ap, ident.ap),
                        reads=[in_, ident], writes=[out])

    def act(self, out, in_, func, bias=None, scale=None, accum_out=None, eng="scalar"):
        kw = {}
        reads = [in_]
        if bias is not None:
            kw["bias"] = bias.ap if isinstance(bias, V) else bias
            reads.append(bias)
        if scale is not None:
            kw["scale"] = scale.ap if isinstance(scale, V) else scale
            reads.append(scale)
        writes = [out]
        if accum_out is not None:
            kw["accum_out"] = accum_out.ap
            writes.append(accum_out)
        return self.rec(eng, lambda e: e.activation(out.ap, in_.ap, func, **kw), reads=reads, writes=writes)

    def ts(self, eng, out, in0, s1, s2, op0, op1=None, accum_out=None):
        reads = [in0, s1, s2]
        a1 = s1.ap if isinstance(s1, V) else s1
        a2 = s2.ap if isinstance(s2, V) else s2
        kw = {}
        writes = [out]
        if op1 is not None:
            kw["op1"] = op1
        if accum_out is not None:
            kw["accum_out"] = accum_out.ap
            writes.append(accum_out)
        return self.rec(eng, lambda e: e.tensor_scalar(out=out.ap, in0=in0.ap, scalar1=a1, scalar2=a2, op0=op0, **kw),
                        reads=reads, writes=writes)

    def stt(self, eng, out, in0, scalar, in1, op0, op1):
        sc = scalar.ap if isinstance(scalar, V) else scalar
        eng = "vector"
        return self.rec(eng, lambda e: e.scalar_tensor_tensor(out=out.ap, in0=in0.ap, scalar=sc, in1=in1.ap, op0=op0, op1=op1),
                        reads=[in0, scalar, in1], writes=[out])

    def tt(self, eng, out, in0, in1, op):
        return self.rec(eng, lambda e: e.tensor_tensor(out=out.ap, in0=in0.ap, in1=in1.ap, op=op),
                        reads=[in0, in1], writes=[out])

    def copy(self, eng, out, in_):
        if eng == "scalar":
            return self.rec(eng, lambda e: e.copy(out=out.ap, in_=in_.ap), reads=[in_], writes=[out])
        return self.rec(eng, lambda e: e.tensor_copy(out=out.ap, in_=in_.ap), reads=[in_], writes=[out])

    def reduce(self, eng, out, in_, op, axis=AX.X):
        return self.rec(eng, lambda e: e.tensor_reduce(out=out.ap, in_=in_.ap, axis=axis, op=op),
                        reads=[in_], writes=[out])

    def recip(self, eng, out, in_):
        return self.rec(eng, lambda e: e.reciprocal(out=out.ap, in_=in_.ap), reads=[in_], writes=[out])

    def scan(self, out, d0, d1, initial, op0, op1):
        return self.rec("vector", lambda e: e.tensor_tensor_scan(out=out.ap, data0=d0.ap, data1=d1.ap, initial=initial, op0=op0, op1=op1),
                        reads=[d0, d1], writes=[out])

    def memset(self, eng, out, val):
        return self.rec(eng, lambda e: e.memset(out.ap, val), reads=[], writes=[out])


D = 1024
NH = 8
HD = 128
LH = 4
LKD = 128
LVD = 256
NMEM = 256
DFF = 4096
NS = 2
T = NS * 128
EPS = 1e-6
NBLK = 48
BIG = 1.0e30
DGE_SCRATCH = 1024

B_GDN = 0
B_GLA = 8
B_R = 12
B_GATE = 14
B_BG = 18
B_BL = 20
B_OUT = 22
B_WQ = 24
B_WO = 26
B_W1 = 28
B_W2 = 36
B_WK = 44
B_WV = 46

IN_SIZES = (1024, 1024, 1024, 1024, 8, 8, 512, 512, 1024, 1024, 16, 1024, 1024)


def host_layout(inp):
    f = lambda a: np.ascontiguousarray(np.asarray(a, dtype=np.float32))
    w_in = f(inp["w_in"][0])
    offs = np.cumsum((0,) + IN_SIZES)
    gq, gk, gv, gz, ga, gb, lq, lk, lv, lr, lgate, gate_a, gate_b = [w_in[:, offs[i]:offs[i + 1]] for i in range(13)]
    blocks = []

    def blk(mat):
        assert mat.shape == (1024, 512), mat.shape
        return mat.reshape(8, 128, 512).transpose(1, 0, 2)

    for h in range(8):
        s = slice(h * 128, (h + 1) * 128)
        blocks.append(blk(np.concatenate([gq[:, s], gk[:, s], gv[:, s], gz[:, s]], axis=1)))
    for h in range(4):
        blocks.append(blk(np.concatenate([lq[:, h * 128:(h + 1) * 128], lk[:, h * 128:(h + 1) * 128],
                                          lv[:, h * 256:(h + 1) * 256]], axis=1)))
    for c in range(2):
        blocks.append(blk(lr[:, c * 512:(c + 1) * 512]))
    for g in (gate_a, gate_b):
        for c in range(2):
            blocks.append(blk(g[:, c * 512:(c + 1) * 512]))
    for name in ("w_branch_gdn", "w_branch_gla", "w_out", "xattn_wq", "xattn_wo"):
        w = f(inp[name][0])
        for c in range(2):
            blocks.append(blk(w[:, c * 512:(c + 1) * 512]))
    w1 = f(inp["mlp_w1"][0])
    for c in range(8):
        blocks.append(blk(w1[:, c * 512:(c + 1) * 512]))
    w2 = f(inp["mlp_w2"][0])
    for fg in range(4):
        for c in range(2):
            blocks.append(blk(w2[fg * 1024:(fg + 1) * 1024, c * 512:(c + 1) * 512]))
    for name in ("xattn_wk", "xattn_wv"):
        w = f(inp[name][0])
        for c in range(2):
            blocks.append(blk(w[:, c * 512:(c + 1) * 512]))
    wblk = np.ascontiguousarray(np.stack(blocks, 0)).reshape(NBLK, 128, 4096)
    wsm = np.ascontiguousarray(np.concatenate([ga, gb, lgate], axis=1).reshape(8, 128, 32).transpose(1, 0, 2)).reshape(128, 256)

    def gcol(g):
        return np.ascontiguousarray(f(g).reshape(8, 128).T)

    rep = lambda v: np.ascontiguousarray(np.broadcast_to(f(v).reshape(1, -1), (128, f(v).size)))
    gcols = np.concatenate([gcol(inp["norm_mix_g"][0]), gcol(inp["norm_xattn_g"][0]),
                            gcol(inp["norm_mlp_g"][0]), gcol(inp["norm_mem_g"][0])], axis=1)
    cwt = f(inp["gdn_conv_w"][0])
    cw = np.ascontiguousarray(cwt.reshape(4, 24, 128).transpose(2, 1, 0)).reshape(128, 96)
    small = np.concatenate([
        gcols,
        cw,
        rep(inp["gdn_a_log"][0]),
        rep(inp["gdn_dt_bias"][0]),
        np.ascontiguousarray(f(inp["gla_b_gate"][0]).reshape(4, 128).T),
    ], axis=1)
    small = np.ascontiguousarray(small)
    wide = np.concatenate([
        rep(inp["norm_final_g"]),
        rep(np.tile(f(inp["gdn_norm_g"][0]), 8)),
        rep(np.tile(f(inp["gla_norm_g"][0]), 4)),
    ], axis=1)
    wide = np.ascontiguousarray(wide)
    wg2 = f(inp["gla_w_gate2"][0])
    p = np.arange(128)[:, None]
    q = np.arange(128)[None, :]
    ident = (p == q).astype(np.float32)
    tri = (p <= q).astype(np.float32)
    same32 = (p // 32) == (q // 32)
    pm_d = np.where((p > q) & same32, 0.0, BIG).astype(np.float32)
    pm_r = np.where((p > q) & (~same32), 0.0, BIG).astype(np.float32)
    nm_t = np.where(q >= p, 0.0, -BIG).astype(np.float32)
    m01t = (q >= p).astype(np.float32)
    ones = np.ones((128, 128), np.float32)
    consts = np.ascontiguousarray(np.concatenate([ident, tri, pm_d, pm_r, nm_t, m01t, ones], axis=1))
    return dict(wblk=wblk, wsm=wsm, small=small, wide=wide, wg2=wg2, consts=consts)


class RR:
    def __init__(self, items):
        self.items = items
        self.i = 0
        self.held = set()

    def next(self):
        for _ in range(len(self.items) + 1):
            k = self.i % len(self.items)
            self.i += 1
            if k not in self.held:
                return self.items[k]
        raise RuntimeError("all held")

    def hold(self, it):
        self.held.add(self.items.index(it))

    def release(self, it):
        self.held.discard(self.items.index(it))


class _Stop(Exception):
    pass


def build(nseq, S, taps=None, stop=None):
    assert S % T == 0
    nst = S // T
    nc = bass.Bass("TRN2", target_bir_lowering=False, dynamic_dma_scratch_size=DGE_SCRATCH)
    dr = lambda name, shape, dt=F32, kind="ExternalInput": nc.dram_tensor(name, list(shape), dt, kind=kind).ap()
    x_d = dr("x", [nseq, S, D])
    mem_d = dr("mem", [nseq, NMEM, D])
    wblk_d = dr("wblk", [NBLK, 128, 4096])
    wsm_d = dr("wsm", [128, 256])
    small_d = dr("small", [128, 148])
    wide_d = dr("wide", [128, 3072])
    wg2_d = dr("wg2", [16, 512])
    consts_d = dr("consts", [128, 7 * 128])
    y_d = dr("y", [nseq, S, D], F32, "ExternalOutput")
    wbf_ap = dr("wbf", [NBLK, 128, 4096], BF16, "Internal")
    tap_outs = {}

    es = ExitStack()
    P = Prog(nc, es)
    wbf = [P.dram(wbf_ap[b], "wbf%d" % b) for b in range(NBLK)]

    def chk(name):
        P.phase = "after_" + name
        if stop == name:
            raise _Stop()

    def tap(name, v, shape, dt=F32):
        if taps is None or name not in taps or name in tap_outs:
            return
        o = dr("tap_" + name, shape, dt, "ExternalOutput")
        tap_outs[name] = o
        stg = P.sb("tapstg_" + name, shape, dt)
        P.copy("vector", stg, v)
        d = P.dma("sync", o, stg)
        P.final_deps.append(d)

    cst = P.sb("cst", [128, 7 * 128], F32)
    ident_f = cst[:, 0:128]
    tri_f = cst[:, 128:256]
    pm_d = cst[:, 256:384]
    pm_r = cst[:, 384:512]
    nm_t = cst[:, 512:640]
    m01_f = cst[:, 640:768]
    ones_f = cst[:, 768:896]
    ident_b = P.sb("ident_b", [128, 128], BF16)
    ident4_b = P.sb("ident4_b", [128, 4, 128], BF16)
    ones_b = P.sb("ones_b", [128, 128], BF16)
    m01t4 = P.sb("m01t4", [128, 4, 128], BF16)
    small = P.sb("small", [128, 148], F32)
    gcols = small[:, 0:32]
    cw = small[:, 32:128]
    alog = small[:, 128:136]
    dtb = small[:, 136:144]
    bgate = small[:, 144:148]
    negA = P.sb("negA", [128, 8], F32)
    negb = P.sb("negb", [128, 4], F32)
    gfin = P.sb("gfin", [128, 1024], F32)
    wide_b = P.sb("wide_b", [128, 2048], BF16)
    gng = wide_b[:, 0:1024]
    lng = wide_b[:, 1024:2048]
    wsm = P.sb("wsm", [128, 8, 32], BF16)
    wg2 = P.sb("wg2", [16, 512], BF16)

    Sg = P.sb("Sg", [128, 8, 128], F32)
    Sgb = [P.sb("Sgb%d" % g, [128, 4, 128], BF16) for g in range(2)]
    Sl = P.sb("Sl", [128, 4, 256], F32)
    Slb = P.sb("Slb", [128, 4, 256], BF16)
    halo = P.sb("halo", [128, 24, 3], F32)
    KT = P.sb("KT", [128, 8, 256], BF16)
    Vt = P.sb("Vt", [128, 2, 1024], BF16)

    NSLOT = 5
    slots = [P.sb("wslot%d" % i, [128, 8, 512], BF16) for i in range(NSLOT)]
    xt = P.sb("xt", [128, NS, 1024], F32)
    ttiles = RR([P.sb("tt%d" % i, [128, 8, T], BF16) for i in range(4)])
    banks = [P.ps("bank%d" % i, [128, 512], F32) for i in range(8)]
    psA = RR(banks)

    hb = P.sb("hb", [128, 1024], BF16)
    smalls = RR([P.sb("sm%d" % i, [128, 8], F32) for i in range(24)])

    raws = RR([P.sb("raw%d" % i, [128, 3 + T], F32) for i in range(6)])
    convy = RR([P.sb("convy%d" % i, [128, T], F32) for i in range(6)])
    sc_q = RR([P.sb("scq%d" % i, [128, T], BF16) for i in range(2)])
    qn = [P.sb("qn%d" % g, [128, 4, T], BF16) for g in range(2)]
    kn = [P.sb("kn%d" % g, [128, 4, T], BF16) for g in range(2)]
    vs = [P.sb("vs%d" % g, [128, 4, T], BF16) for g in range(2)]
    zg = P.sb("zg", [128, NS, 1024], BF16)
    gab = P.sb("gab", [128, NS, 32], F32)
    lgT = P.sb("lgT", [16, T], BF16)
    vl = P.sb("vl", [128, NS, 1024], BF16)
    rg = P.sb("rg", [128, NS, 1024], BF16)
    sa = P.sb("sa", [128, NS, 1024], BF16)
    sb_ = P.sb("sb_", [128, NS, 1024], BF16)
    wide_f = RR([P.sb("widef%d" % i, [128, 512], F32) for i in range(2)])

    GT = P.sb("GT", [128, 4, 128], F32)
    Gs = P.sb("Gs", [128, 4, 128], F32)
    EG = P.sb("EG", [128, 4, 128], BF16)
    args = RR([P.sb("arg%d" % i, [128, 4, 128], F32) for i in range(2)])
    mk = lambda n: P.sb(n, [128, 4, 128], BF16)
    Fd, Fr, Dt = mk("Fd"), mk("Fr"), mk("Dt")
    Ld, Rr, AqkT, LdT, qdT, Rw, Ru, kdec = [mk(n) for n in ("Ld", "Rr", "AqkT", "LdT", "qdT", "Rw", "Ru", "kdec")]
    Mp = [mk("Mp0"), mk("Mp1")]
    MTp = [mk("MTp0"), mk("MTp1")]
    Yp = [mk("Yp0"), mk("Yp1")]
    Dp = [mk("Dp0"), mk("Dp1")]
    Zt, wTn, dlt = [mk(n) for n in ("Zt", "wTn", "dlt")]
    Wn, Nb, T1 = Dt, Fr, Fd
    tokb = [P.sb("tokb%d" % i, [128, 1024], BF16) for i in range(2)]
    oa = tokb[0]

    cs = P.sb("cs", [128, 4, T], F32)
    glp = [P.sb("glp%d" % i, [128, 4, T], BF16) for i in range(4)]
    smallsB = RR([P.sb("smB%d" % i, [128, 8], F32) for i in range(12)])
    lsc = RR([P.sb("lsc%d" % i, [128, T], F32) for i in range(2)])
    efac = RR([P.sb("efac%d" % i, [128, 128], F32) for i in range(4)])
    ATm = mk("ATm")
    kdl = mk("kdl")
    ob = tokb[1]

    pt = vl[:, 0, :].re("p (a b) -> p a b", a=4)
    pT = vl[:, 1, :].re("p (a b) -> p a b", a=8)
    ox = tokb[1]
    hidq = [P.sb("hidq%d" % i, [128, 8, T], BF16) for i in range(2)]
    yam = zg


    if taps is not None and "probe" in taps:
        for kb in range(64, 0, -1):
            try:
                es2 = ExitStack()
                es2.enter_context(nc.sbuf_tensor("probe%d" % kb, [128, kb * 256], F32))
                print("SBUF slack >= %d KB" % kb)
                es2.close()
                break
            except Exception as ex:
                pass
    order = []
    for q_ in range(nseq):
        order += [B_WK, B_WK + 1, B_WV, B_WV + 1]
        for st in range(nst):
            order += list(range(0, 28)) + [28, 29, 36, 37, 30, 31, 38, 39, 32, 33, 40, 41, 34, 35, 42, 43]
    wstate = {"issued": 0, "cur": 0}

    def wissue():
        i = wstate["issued"]
        if i < len(order):
            prep_block(order[i])
            P.dma("sync", slots[i % NSLOT].re("p a b -> p (a b)"), wbf[order[i]])
            wstate["issued"] += 1

    def wget(expect):
        c = wstate["cur"]
        assert order[c] == expect, (c, order[c], expect)
        while wstate["issued"] <= min(c + NSLOT - 1, len(order) - 1):
            wissue()
        wstate["cur"] += 1
        return slots[c % NSLOT]

    P.dma("sync", cst, consts_d)
    P.dma("sync", small, small_d)
    P.dma("sync", gfin, wide_d[:, 0:1024])
    P.copy("vector", ident_b, ident_f)
    P.copy("vector", ones_b, ones_f)
    for h in range(4):
        P.copy("vector", ident4_b[:, h, :], ident_f)
        P.copy("vector", m01t4[:, h, :], m01_f)
    P.dma("sync", wide_f.items[1][0:16, :], wg2_d)
    P.copy("vector", wg2, wide_f.items[1][0:16, :])
    P.act(negA, alog, AF.Exp)
    P.ts("vector", negA, negA, -1.0, None, ALU.mult)
    P.ts("vector", negb, bgate, -1.0, None, ALU.mult)
    for c in range(4):
        sf_ = wide_f.items[c % 2]
        P.dma("sync", sf_, wide_d[:, 1024 + c * 512:1024 + (c + 1) * 512])
        P.copy("vector", wide_b[:, c * 512:(c + 1) * 512], sf_)
    gain_of = {}
    for b in range(0, 18):
        gain_of[b] = 0
    gain_of[B_WQ] = gain_of[B_WQ + 1] = 1
    for b in range(B_W1, B_W1 + 8):
        gain_of[b] = 2
    for b in range(B_WK, B_WK + 4):
        gain_of[b] = 3
    pstf = [P.sb("pstf%d" % i, [128, 512], F32) for i in range(2)]
    pstb = [P.sb("pstb%d" % i, [128, 512], BF16) for i in range(2)]
    pstate = {"qi": 0, "done": set()}

    def prep_block(b):
        if b in pstate["done"]:
            return
        pstate["done"].add(b)
        gi = gain_of.get(b, None)
        for kc in range(8):
            qi = pstate["qi"]
            sf = pstf[qi % 2]
            sbf = pstb[qi % 2]
            eng = "vector" if qi % 2 == 0 else "scalar"
            P.dma("sync", sf, wblk_d[b][:, kc * 512:(kc + 1) * 512])
            if gi is None:
                P.copy(eng, sbf, sf)
            elif eng == "scalar":
                P.act(sbf, sf, AF.Identity, scale=gcols[:, gi * 8 + kc:gi * 8 + kc + 1])
            else:
                P.ts(eng, sbf, sf, gcols[:, gi * 8 + kc:gi * 8 + kc + 1], None, ALU.mult)
            P.dma("sync", wbf[b][:, kc * 512:(kc + 1) * 512], sbf)
            pstate["qi"] += 1

    stf = [wide_f.items[0]]
    sf = stf[0]
    P.dma("sync", sf[:, 0:256], wsm_d)
    for kc in range(8):
        P.ts("vector", wsm[:, kc, :], sf[:, kc * 32:(kc + 1) * 32], gcols[:, kc:kc + 1], None, ALU.mult)

    def rstd_of(xrow, ncols):
        ss = smalls.next()
        P.act(hb[:, 0:ncols], xrow, AF.Square, accum_out=ss[:, 0:1])
        rs = smalls.next()
        P.ts("vector", rs[:, 0:1], ss[:, 0:1], 1.0 / ncols, EPS, ALU.mult, ALU.add)
        P.act(rs[:, 1:2], rs[:, 0:1], AF.Ln)
        P.act(rs[:, 2:3], rs[:, 1:2], AF.Exp, scale=-0.5)
        return rs[:, 2:3]

    def to_T(src_b, dstT, col0, ncol=128, evac="scalar"):
        pb = psA.next().bc(BF16)
        for c in range(8):
            P.tr(pb[:, c * 128:(c + 1) * 128], src_b[:, c * 128:(c + 1) * 128], ident_b)
        P.copy(evac, dstT[:, :, col0:col0 + 128], pb.re("p (a b) -> p a b", a=8))

    def norm_T(xrow, dstT, col0):
        rs = rstd_of(xrow, 1024)
        P.ts("vector", hb, xrow, rs, None, ALU.mult)
        to_T(hb, dstT, col0)

    def kv_prep(q_):
        mT = ttiles.next()
        for mc in range(2):
            P.dma("sync", xt[:, 0, :], mem_d[q_, mc * 128:(mc + 1) * 128, :])
            norm_T(xt[:, 0, :], mT, mc * 128)
        for c2 in range(2):
            wt = wget(B_WK + c2)
            for cc in range(4):
                pb = psA.next()
                for kc in range(8):
                    P.mm(pb[:, 0:256], wt[:, kc, cc * 128:(cc + 1) * 128], mT[:, kc, 0:256], start=(kc == 0), stop=(kc == 7))
                P.copy("scalar", KT[:, c2 * 4 + cc, :], pb[:, 0:256])
        for c2 in range(2):
            wt = wget(B_WV + c2)
            for mc in range(2):
                pb = psA.next()
                for kc in range(8):
                    P.mm(pb, mT[:, kc, mc * 128:(mc + 1) * 128], wt[:, kc, :], start=(kc == 0), stop=(kc == 7))
                P.copy("scalar", Vt[:, mc, c2 * 512:(c2 + 1) * 512], pb)

    def gdn_front(h, hT, wt):
        pbs, rws, ys = [], [], []
        for j in range(3):
            pb = psA.next()
            for kc in range(8):
                P.mm(pb[:, 0:T], wt[:, kc, j * 128:(j + 1) * 128], hT[:, kc, :], start=(kc == 0), stop=(kc == 7))
            pbs.append(pb)
        for j in range(3):
            g = j * 8 + h
            raw = raws.next()
            P.copy("gpsimd", raw[:, 0:3], halo[:, g, :])
            P.copy("scalar", raw[:, 3:3 + T], pbs[j][:, 0:T])
            P.copy("gpsimd", halo[:, g, :], raw[:, T:T + 3])
            rws.append(raw)
        for j in range(3):
            g = j * 8 + h
            y = convy.next()
            P.ts("vector", y, rws[j][:, 3:3 + T], cw[:, g * 4 + 3:g * 4 + 4], None, ALU.mult)
            ys.append(y)
        for jj in (2, 1, 0):
            for j in range(3):
                g = j * 8 + h
                P.stt("vector", ys[j], rws[j][:, jj:jj + T], cw[:, g * 4 + jj:g * 4 + jj + 1], ys[j], ALU.mult, ALU.add)
        return ys, rws

    def gdn_back(h, ys, rws):
        g4, hh = divmod(h, 4)
        es = [rws[j][:, 3:3 + T] for j in range(3)]
        for j in range(3):
            P.act(es[j], ys[j], AF.Exp, scale=-1.0)
        for j in range(3):
            P.act(es[j], es[j], AF.Ln, bias=1.0)
        for j in range(3):
            P.act(es[j], es[j], AF.Exp, scale=-1.0)
        for j in range(2):
            P.tt("vector" if j == 0 else "gpsimd", ys[j], ys[j], es[j], ALU.mult)
        P.tt("gpsimd", vs[g4][:, hh, :], ys[2], es[2], ALU.mult)
        pns = []
        for j in range(2):
            sq = sc_q.next()
            P.tt("gpsimd", sq, ys[j], ys[j], ALU.mult)
            pn = psA.next()
            P.mm(pn[:, 0:T], ones_b, sq)
            pns.append(pn)
        for j in range(2):
            P.act(es[j], pns[j][:, 0:T], AF.Ln, bias=EPS)
        for j in range(2):
            if j == 0:
                P.act(es[j], es[j], AF.Exp, scale=-0.5, bias=float(np.log(HD ** -0.5)))
            else:
                P.act(es[j], es[j], AF.Exp, scale=-0.5)
        for j in range(2):
            dst = (qn if j == 0 else kn)[g4][:, hh, :]
            P.tt("vector", dst, ys[j], es[j], ALU.mult)

    zbank = {}

    def gdn_z(h, hT, wt):
        g4, hh = divmod(h, 4)
        for s in range(NS):
            if hh == 0:
                zbank[s] = psA.next()
                psA.hold(zbank[s])
            pb = zbank[s]
            for kc in range(8):
                P.mm(pb[:, hh * 128:(hh + 1) * 128], hT[:, kc, s * 128:(s + 1) * 128], wt[:, kc, 384:512], start=(kc == 0), stop=(kc == 7))
            if hh == 3:
                e = wide_f.next()
                P.act(e, pb, AF.Exp, scale=-1.0)
                P.act(e, e, AF.Ln, bias=1.0)
                P.act(e, e, AF.Exp, scale=-1.0)
                P.tt("vector", e, pb, e, ALU.mult)
                P.tt("gpsimd", zg[:, s, g4 * 512:(g4 + 1) * 512], e, gng[:, g4 * 512:(g4 + 1) * 512], ALU.mult)
                psA.release(pb)

    dsc = {}

    def gdn_scalars(s):
        t8 = smalls.next()
        P.tt("vector", t8, gab[:, s, 0:8], dtb, ALU.add)
        P.act(t8, t8, AF.Exp)
        sp8 = smalls.next()
        P.act(sp8, t8, AF.Ln, bias=1.0)
        g8 = smalls.next()
        P.tt("vector", g8, sp8, negA, ALU.mult)
        eb8 = smalls.next()
        P.act(eb8, gab[:, s, 8:16], AF.Exp, scale=-1.0)
        lb8 = smalls.next()
        P.act(lb8, eb8, AF.Ln, bias=1.0)
        pb = psA.next()
        P.mm(pb[:, 0:8], tri_f, g8)
        gc8 = smalls.next()
        P.copy("vector", gc8, pb[:, 0:8])
        gcb8 = smalls.next()
        P.tt("vector", gcb8, gc8, lb8, ALU.subtract)
        beta8 = smalls.next()
        P.act(beta8, lb8, AF.Exp, scale=-1.0)
        bg8 = smalls.next()
        P.act(bg8, gcb8, AF.Exp)
        tap("g8", g8, [128, 8]); tap("gc8", gc8, [128, 8]); tap("beta8", beta8, [128, 8])
        chk("G1")
        dsc[s] = (g8, gc8, gcb8, beta8, bg8)

    def gdn_unit(s, g4, oaT):
        c0 = s * 128
        cols = slice(c0, c0 + 128)
        if g4 == 0:
            gdn_scalars(s)
        g8, gc8, gcb8, beta8, bg8 = dsc[s]
        hs = [g4 * 4 + i for i in range(4)]
        for hh, h in enumerate(hs):
            P.act(GT[:, hh, :], tri_f, AF.Identity, scale=g8[:, h:h + 1])
        pb = psA.next()
        P.mm(pb, ones_f, GT.re("p a b -> p (a b)"))
        yield
        P.copy("scalar", Gs.re("p a b -> p (a b)"), pb)
        P.act(EG.re("p a b -> p (a b)"), Gs.re("p a b -> p (a b)"), AF.Exp)
        gl4 = smalls.next()
        P.copy("vector", gl4[:, 0:4], Gs[:, :, 127])
        gend4 = smalls.next()
        P.act(gend4[:, 0:4], gl4[:, 0:4], AF.Exp)
        ekd4 = smalls.next()
        P.tt("vector", ekd4[:, 0:4], gl4[:, 0:4], gc8[:, g4 * 4:g4 * 4 + 4], ALU.subtract)
        P.act(ekd4[:, 0:4], ekd4[:, 0:4], AF.Exp)
        chk("G2")
        yield
        pbk = psA.next().bc(BF16)
        pbv = psA.next().bc(BF16)
        for hh, h in enumerate(hs):
            P.tr(pbk[:, hh * 128:(hh + 1) * 128], kn[g4][:, hh, cols], ident_b)
        for hh, h in enumerate(hs):
            P.tr(pbv[:, hh * 128:(hh + 1) * 128], vs[g4][:, hh, cols], ident_b)
        yield
        for hh, h in enumerate(hs):
            P.ts("vector", Rw[:, hh, :], pbk[:, hh * 128:(hh + 1) * 128], bg8[:, h:h + 1], None, ALU.mult)
            P.act(kdec[:, hh, :], pbk[:, hh * 128:(hh + 1) * 128], AF.Identity, scale=ekd4[:, hh:hh + 1])
            P.act(Ru[:, hh, :], pbv[:, hh * 128:(hh + 1) * 128], AF.Identity, scale=beta8[:, h:h + 1])
        chk("G3")
        yield
        pkk = psA.next()
        pqk = psA.next()
        for hh, h in enumerate(hs):
            P.mm(pkk[:, hh * 128:(hh + 1) * 128], kn[g4][:, hh, cols], kn[g4][:, hh, cols])
        for hh, h in enumerate(hs):
            P.mm(pqk[:, hh * 128:(hh + 1) * 128], kn[g4][:, hh, cols], qn[g4][:, hh, cols])
        yield
        a_d = args.next()
        for hh, h in enumerate(hs):
            P.stt("vector" if hh % 2 == 0 else "gpsimd", a_d[:, hh, :], Gs[:, hh, :], gcb8[:, h:h + 1], pm_d, ALU.subtract, ALU.max)
        P.act(Fd.re("p a b -> p (a b)"), a_d.re("p a b -> p (a b)"), AF.Exp, scale=-1.0)
        a_r = args.next()
        for hh, h in enumerate(hs):
            P.stt("vector" if hh % 2 == 0 else "gpsimd", a_r[:, hh, :], Gs[:, hh, :], gcb8[:, h:h + 1], pm_r, ALU.subtract, ALU.max)
        P.act(Fr.re("p a b -> p (a b)"), a_r.re("p a b -> p (a b)"), AF.Exp, scale=-1.0)
        a_t = args.next()
        for hh, h in enumerate(hs):
            P.stt("vector" if hh % 2 == 0 else "gpsimd", a_t[:, hh, :], Gs[:, hh, :], gc8[:, h:h + 1], nm_t, ALU.subtract, ALU.min)
        P.act(Dt.re("p a b -> p (a b)"), a_t.re("p a b -> p (a b)"), AF.Exp)
        fl = lambda t_: t_.re("p a b -> p (a b)")
        P.tt("vector", fl(Ld), pkk, fl(Fd), ALU.mult)
        P.tt("vector", fl(Rr), pkk, fl(Fr), ALU.mult)
        P.tt("vector", fl(AqkT), pqk, fl(Dt), ALU.mult)
        for hh in range(4):
            P.tt("gpsimd", qdT[:, hh, :], qn[g4][:, hh, cols], EG[:, hh, :], ALU.mult)
        chk("G4")
        yield
        pbt = psA.next().bc(BF16)
        for hh in range(4):
            P.tr(pbt[:, hh * 128:(hh + 1) * 128], Ld[:, hh, :], ident_b)
        yield
        P.copy("scalar", fl(LdT), pbt[:, 0:512])
        chk("G5")
        P.tt("vector", fl(Yp[0]), fl(ident4_b), fl(LdT), ALU.subtract)
        P.tt("gpsimd", fl(Dp[0]), fl(ident4_b), fl(Ld), ALU.subtract)

        def squares(Mc, MTc, Mn, MTn):
            p1 = psA.next()
            p2 = psA.next()
            for hh in range(4):
                P.mm(p1[:, hh * 128:(hh + 1) * 128], MTc[:, hh, :], Mc[:, hh, :])
            for hh in range(4):
                P.mm(p2[:, hh * 128:(hh + 1) * 128], Mc[:, hh, :], MTc[:, hh, :])
            return p1, p2

        Mc, MTc = Ld, LdT
        Mn, MTn = Mp[1], MTp[1]
        yield
        p1, p2 = squares(Mc, MTc, Mn, MTn)
        yield
        P.copy("scalar", fl(Mn), p1)
        P.copy("vector", fl(MTn), p2)
        yi = 0
        for m in range(1, 5):
            Mc, MTc = Mn, MTn
            yield
            p3 = psA.next()
            p4 = psA.next()
            for hh in range(4):
                P.mm(p3[:, hh * 128:(hh + 1) * 128], Mc[:, hh, :], Yp[yi][:, hh, :], start=True, stop=False)
                P.mm(p3[:, hh * 128:(hh + 1) * 128], ident_b, Yp[yi][:, hh, :], start=False, stop=True)
            for hh in range(4):
                P.mm(p4[:, hh * 128:(hh + 1) * 128], MTc[:, hh, :], Dp[yi][:, hh, :], start=True, stop=False)
                P.mm(p4[:, hh * 128:(hh + 1) * 128], ident_b, Dp[yi][:, hh, :], start=False, stop=True)
            if m < 4:
                Mn, MTn = Mp[(m + 1) % 2], MTp[(m + 1) % 2]
                p1, p2 = squares(Mc, MTc, Mn, MTn)
            yield
            P.copy("scalar", fl(Yp[1 - yi]), p3)
            P.copy("vector", fl(Dp[1 - yi]), p4)
            if m < 4:
                P.copy("scalar", fl(Mn), p1)
                P.copy("vector", fl(MTn), p2)
            yi = 1 - yi
        Yd, Dd = Yp[yi], Dp[yi]
        chk("G6")
        DRu, DRw = Mp[0], MTp[0]
        yield
        p1 = psA.next()
        p2 = psA.next()
        p3 = psA.next()
        p4 = psA.next()
        for hh in range(4):
            P.mm(p1[:, hh * 128:(hh + 1) * 128], Rr[:, hh, :], Yd[:, hh, :])
        for hh in range(4):
            P.mm(p2[:, hh * 128:(hh + 1) * 128], Yd[:, hh, :], Rr[:, hh, :])
        for hh in range(4):
            P.mm(p3[:, hh * 128:(hh + 1) * 128], Yd[:, hh, :], Ru[:, hh, :])
        for hh in range(4):
            P.mm(p4[:, hh * 128:(hh + 1) * 128], Yd[:, hh, :], Rw[:, hh, :])
        yield
        P.tt("vector", fl(Wn), fl(ident4_b), p1, ALU.subtract)
        P.copy("scalar", fl(Nb), p2)
        P.copy("scalar", fl(DRu), p3)
        P.copy("vector", fl(DRw), p4)
        yield
        p3 = psA.next()
        for hh in range(4):
            P.mm(p3[:, hh * 128:(hh + 1) * 128], Nb[:, hh, :], Wn[:, hh, :])
        yield
        P.copy("scalar", fl(T1), p3)
        yield
        p4 = psA.next()
        for hh in range(4):
            P.mm(p4[:, hh * 128:(hh + 1) * 128], Nb[:, hh, :], T1[:, hh, :], start=True, stop=False)
            P.mm(p4[:, hh * 128:(hh + 1) * 128], ident_b, Wn[:, hh, :], start=False, stop=True)
        yield
        P.copy("scalar", fl(Zt), p4)
        chk("G7")
        yield
        p6 = psA.next()
        for hh in range(4):
            P.mm(p6[:, hh * 128:(hh + 1) * 128], DRw[:, hh, :], Zt[:, hh, :])
        yield
        P.ts("vector", fl(wTn), p6, -1.0, None, ALU.mult)
        yield
        p7 = psA.next()
        for hh in range(4):
            P.mm(p7[:, hh * 128:(hh + 1) * 128], Zt[:, hh, :], DRu[:, hh, :], start=True, stop=False)
            P.mm(p7[:, hh * 128:(hh + 1) * 128], wTn[:, hh, :], Sgb[g4][:, hh, :], start=False, stop=True)
        yield
        P.copy("scalar", fl(dlt), p7)
        yield
        p8 = psA.next()
        for hh in range(4):
            P.mm(p8[:, hh * 128:(hh + 1) * 128], qdT[:, hh, :], Sgb[g4][:, hh, :], start=True, stop=False)
            P.mm(p8[:, hh * 128:(hh + 1) * 128], AqkT[:, hh, :], dlt[:, hh, :], start=False, stop=True)
        p9 = psA.next()
        for hh in range(4):
            P.mm(p9[:, hh * 128:(hh + 1) * 128], kdec[:, hh, :], dlt[:, hh, :])
        yield
        for hh, h in enumerate(hs):
            P.stt("vector", Sg[:, h, :], Sg[:, h, :], gend4[:, hh:hh + 1], p9[:, hh * 128:(hh + 1) * 128], ALU.mult, ALU.add)
        P.copy("scalar", fl(Sgb[g4]), Sg[:, g4 * 4:(g4 + 1) * 4, :].re("p a b -> p (a b)"))
        sq = wide_f.next()
        P.act(sq, p8, AF.Square)
        ss4 = smalls.next()
        P.reduce("vector", ss4[:, 0:4], sq.re("p (a b) -> p a b", a=4), ALU.add)
        P.ts("vector", ss4[:, 0:4], ss4[:, 0:4], 1.0 / HD, EPS, ALU.mult, ALU.add)
        P.act(ss4[:, 0:4], ss4[:, 0:4], AF.Ln)
        rs4 = smalls.next()
        P.act(rs4[:, 0:4], ss4[:, 0:4], AF.Exp, scale=-0.5)
        for hh, h in enumerate(hs):
            P.stt("vector", oa[:, h * 128:(h + 1) * 128], p8[:, hh * 128:(hh + 1) * 128], rs4[:, hh:hh + 1],
                  zg[:, s, h * 128:(h + 1) * 128], ALU.mult, ALU.mult)
        if g4 == 1:
            tap("oa", oa, [128, 1024], BF16)
            to_T(oa, oaT, c0)
        yield

    def gla_proj(h, hT, wt):
        pz = psA.next()
        P.mm(pz[:, 0:T], wg2[:, h * 128:(h + 1) * 128], lgT)
        pq = psA.next()
        pk = psA.next()
        for kc in range(8):
            P.mm(pq[:, 0:T], wt[:, kc, 0:128], hT[:, kc, :], start=(kc == 0), stop=(kc == 7))
        for kc in range(8):
            P.mm(pk[:, 0:T], wt[:, kc, 128:256], hT[:, kc, :], start=(kc == 0), stop=(kc == 7))
        yield
        l_ = lsc.next()
        P.act(l_, pz[:, 0:T], AF.Exp, scale=-1.0, bias=negb[:, h:h + 1])
        P.act(l_, l_, AF.Ln, bias=1.0)
        for s in range(NS):
            P.scan(cs[:, h, s * 128:(s + 1) * 128], ones_f, l_[:, s * 128:(s + 1) * 128], 0.0, ALU.mult, ALU.add)
        for s in range(NS):
            cols = slice(s * 128, (s + 1) * 128)
            cv = cs[:, h, cols]
            c2 = smallsB.next()
            P.ts("vector", c2[:, 0:1], cv[:, 64:65], 1.0 / 16, None, ALU.mult)
            P.ts("vector", c2[:, 1:2], cv[:, 64:65], -1.0 / 16, None, ALU.mult)
            P.ts("vector", c2[:, 2:3], cv[:, 127:128], -1.0 / 16, None, ALU.mult)
            eq = efac.next()
            P.act(eq, cv, AF.Exp, scale=-1.0 / 16, bias=c2[:, 0:1])
            ek = efac.next()
            P.act(ek, cv, AF.Exp, scale=1.0 / 16, bias=c2[:, 1:2])
            ed = efac.next()
            P.act(ed, cv, AF.Exp, scale=-1.0 / 16)
            ekd = efac.next()
            P.act(ekd, cv, AF.Exp, scale=1.0 / 16, bias=c2[:, 2:3])
            sc = float(LKD ** -0.5)
            P.stt("vector", glp[0][:, h, cols], pq[:, cols], sc, eq, ALU.mult, ALU.mult)
            P.tt("vector", glp[1][:, h, cols], pk[:, cols], ek, ALU.mult)
            P.stt("vector", glp[2][:, h, cols], pq[:, cols], sc, ed, ALU.mult, ALU.mult)
            P.tt("vector", glp[3][:, h, cols], pk[:, cols], ekd, ALU.mult)
            P.copy("gpsimd", gendl[:, s, h:h + 1], ed[:, 127:128])
        yield
        for s in range(NS):
            pv = psA.next()
            for kc in range(8):
                P.mm(pv[:, 0:256], hT[:, kc, s * 128:(s + 1) * 128], wt[:, kc, 256:512], start=(kc == 0), stop=(kc == 7))
            yield
            P.copy("scalar", vl[:, s, h * 256:(h + 1) * 256], pv[:, 0:256])
            yield

    gendl = P.sb("gendl", [128, NS, 4], F32)

    def gla_r(c, hT, wt):
        for s in range(NS):
            pb = psA.next()
            for kc in range(8):
                P.mm(pb, hT[:, kc, s * 128:(s + 1) * 128], wt[:, kc, :], start=(kc == 0), stop=(kc == 7))
            yield
            e = wide_f.next()
            P.act(e, pb, AF.Exp, scale=-1.0)
            P.act(e, e, AF.Ln, bias=1.0)
            P.act(e, e, AF.Exp, scale=-1.0)
            P.tt("vector", e, pb, e, ALU.mult)
            P.tt("gpsimd", rg[:, s, c * 512:(c + 1) * 512], e, lng[:, c * 512:(c + 1) * 512], ALU.mult)
            yield

    def gla_core(s, obT):
        c0 = s * 128
        cols = slice(c0, c0 + 128)
        fl = lambda t_: t_.re("p a b -> p (a b)")
        pa = psA.next()
        for h in range(4):
            P.mm(pa[:, h * 128:(h + 1) * 128], glp[1][:, h, cols], glp[0][:, h, cols])
        pbt = psA.next().bc(BF16)
        for h in range(4):
            P.tr(pbt[:, h * 128:(h + 1) * 128], glp[3][:, h, cols], ident_b)
        yield
        P.tt("vector", fl(ATm), pa, fl(m01t4), ALU.mult)
        P.copy("scalar", fl(kdl), pbt[:, 0:512])
        yield
        po = [psA.next(), psA.next()]
        for h in range(4):
            o_ = po[h // 2][:, (h % 2) * 256:(h % 2 + 1) * 256]
            P.mm(o_, glp[2][:, h, cols], Slb[:, h, :], start=True, stop=False)
            P.mm(o_, ATm[:, h, :], vl[:, s, h * 256:(h + 1) * 256], start=False, stop=True)
        yield
        ss4 = smallsB.next()
        for half in range(2):
            sq = wide_f.next()
            P.act(sq, po[half], AF.Square)
            P.reduce("vector", ss4[:, half * 2:half * 2 + 2], sq.re("p (a b) -> p a b", a=2), ALU.add)
        P.ts("vector", ss4[:, 0:4], ss4[:, 0:4], 1.0 / LVD, EPS, ALU.mult, ALU.add)
        P.act(ss4[:, 0:4], ss4[:, 0:4], AF.Ln)
        rs4 = smallsB.next()
        P.act(rs4[:, 0:4], ss4[:, 0:4], AF.Exp, scale=-0.5)
        for h in range(4):
            P.stt("vector", ob[:, h * 256:(h + 1) * 256], po[h // 2][:, (h % 2) * 256:(h % 2 + 1) * 256], rs4[:, h:h + 1],
                  rg[:, s, h * 256:(h + 1) * 256], ALU.mult, ALU.mult)
        yield
        pu = [psA.next(), psA.next()]
        for h in range(4):
            P.mm(pu[h // 2][:, (h % 2) * 256:(h % 2 + 1) * 256], kdl[:, h, :], vl[:, s, h * 256:(h + 1) * 256])
        yield
        for h in range(4):
            P.stt("vector", Sl[:, h, :], Sl[:, h, :], gendl[:, s, h:h + 1], pu[h // 2][:, (h % 2) * 256:(h % 2 + 1) * 256], ALU.mult, ALU.add)
        P.copy("scalar", fl(Slb), fl(Sl))
        yield
        tap("ob", ob, [128, 1024], BF16)
        to_T(ob, obT, c0)
        yield
        yield

    def gates_proj(i, hT, wt):
        dst = sa if i < 2 else sb_
        c = i % 2
        for s in range(NS):
            pb = psA.next()
            for kc in range(8):
                P.mm(pb, hT[:, kc, s * 128:(s + 1) * 128], wt[:, kc, :], start=(kc == 0), stop=(kc == 7))
            yield
            e = wide_f.next()
            P.act(e, pb, AF.Exp, scale=-1.0)
            P.act(e, e, AF.Ln, bias=1.0)
            P.act(dst[:, s, c * 512:(c + 1) * 512], e, AF.Exp, scale=-1.0)
            yield

    def stage_A(q_, st):
        t0 = st * T
        P.phase = "A_start"
        P.dma("sync", xt, x_d[q_, t0:t0 + T, :].rearrange("(s p) d -> p s d", p=128))
        hT = ttiles.next()
        for s in range(NS):
            norm_T(xt[:, s, :], hT, s * 128)
        tap("hT", hT, [128, 8, T], BF16)
        chk("A_norm")
        for s in range(NS):
            pb = psA.next()
            for kc in range(8):
                P.mm(pb[:, 0:32], hT[:, kc, s * 128:(s + 1) * 128], wsm[:, kc, :], start=(kc == 0), stop=(kc == 7))
            P.copy("scalar", gab[:, s, :], pb[:, 0:32])
        pb = psA.next()
        for kc in range(8):
            P.mm(pb[0:16, 0:T], wsm[:, kc, 16:32], hT[:, kc, :], start=(kc == 0), stop=(kc == 7))
        P.copy("scalar", lgT, pb[0:16, 0:T])
        chk("A_small")
        prev = None
        for h in range(8):
            wt = wget(B_GDN + h)
            cur = gdn_front(h, hT, wt)
            gdn_z(h, hT, wt)
            if prev is not None:
                gdn_back(*prev)
            prev = (h,) + cur
        gdn_back(*prev)
        tap("qn0", qn[0], [128, 4, T], BF16); tap("kn0", kn[0], [128, 4, T], BF16); tap("vs0", vs[0], [128, 4, T], BF16)
        tap("zg", zg, [128, NS, 1024], BF16)
        chk("A_gdnproj")
        oaT = ttiles.next()
        obT = ttiles.next()
        def side_gen():
            for h in range(4):
                yield from gla_proj(h, hT, wget(B_GLA + h))
            for c in range(2):
                yield from gla_r(c, hT, wget(B_R + c))
            for s in range(NS):
                yield from gla_core(s, obT)
            for i in range(4):
                yield from gates_proj(i, hT, wget(B_GATE + i))

        side_it = side_gen()
        for s in range(NS):
            for g4 in range(2):
                for _ in gdn_unit(s, g4, oaT):
                    next(side_it, None)
        chk("A_gdncore")
        for _ in side_it:
            pass
        chk("A_glacore")
        return oaT, obT

    def resid_proj(srcT, s, blk0, after=None):
        pass

    def stage_B(q_, st, oaT, obT):
        t0 = st * T
        P.phase = "B_branch"
        for c in range(2):
            wt = wget(B_BG + c)
            for s in range(NS):
                pb = psA.next()
                for kc in range(8):
                    P.mm(pb, oaT[:, kc, s * 128:(s + 1) * 128], wt[:, kc, :], start=(kc == 0), stop=(kc == 7))
                P.tt("vector", yam[:, s, c * 512:(c + 1) * 512], pb, sa[:, s, c * 512:(c + 1) * 512], ALU.mult)
        mTt = ttiles.next()
        for c in range(2):
            wt = wget(B_BL + c)
            for s in range(NS):
                pb = psA.next()
                for kc in range(8):
                    P.mm(pb, obT[:, kc, s * 128:(s + 1) * 128], wt[:, kc, :], start=(kc == 0), stop=(kc == 7))
                m2 = wide_f.next()
                P.tt("vector", m2, pb, sb_[:, s, c * 512:(c + 1) * 512], ALU.mult)
                P.tt("gpsimd", tokb[s][:, c * 512:(c + 1) * 512], m2, yam[:, s, c * 512:(c + 1) * 512], ALU.add)
        for s in range(NS):
            to_T(tokb[s], mTt, s * 128)
        for c in range(2):
            wt = wget(B_OUT + c)
            for s in range(NS):
                pb = psA.next()
                for kc in range(8):
                    P.mm(pb, mTt[:, kc, s * 128:(s + 1) * 128], wt[:, kc, :], start=(kc == 0), stop=(kc == 7))
                P.tt("vector", xt[:, s, c * 512:(c + 1) * 512], pb, xt[:, s, c * 512:(c + 1) * 512], ALU.add)
        tap("x1", xt, [128, NS, 1024])
        chk("B_x1")
        h2T = ttiles.next()
        for s in range(NS):
            norm_T(xt[:, s, :], h2T, s * 128)
        qT = ttiles.next()
        for c2 in range(2):
            wt = wget(B_WQ + c2)
            for cc in range(4):
                pb = psA.next()
                for kc in range(8):
                    P.mm(pb[:, 0:T], wt[:, kc, cc * 128:(cc + 1) * 128], h2T[:, kc, :], start=(kc == 0), stop=(kc == 7))
                P.act(qT[:, c2 * 4 + cc, :], pb[:, 0:T], AF.Identity, scale=float(256 ** -0.5))
        oxT = ttiles.next()
        for s in range(NS):
            cols = slice(s * 128, (s + 1) * 128)
            psc = [psA.next(), psA.next()]
            for h in range(4):
                o_ = psc[h // 2][:, (h % 2) * 256:(h % 2 + 1) * 256]
                for c in range(2):
                    P.mm(o_, qT[:, 2 * h + c, cols], KT[:, 2 * h + c, :], start=(c == 0), stop=(c == 1))
            mx = smalls.next()
            for half in range(2):
                P.reduce("vector", mx[:, half * 2:half * 2 + 2], psc[half].re("p (a b) -> p a b", a=2), ALU.max)
            P.ts("vector", mx[:, 0:4], mx[:, 0:4], -1.0, None, ALU.mult)
            sm4 = smalls.next()
            for h in range(4):
                P.act(pt[:, h, :], psc[h // 2][:, (h % 2) * 256:(h % 2 + 1) * 256], AF.Exp, bias=mx[:, h:h + 1], accum_out=sm4[:, h:h + 1])
            rs4 = smalls.next()
            P.recip("vector", rs4[:, 0:4], sm4[:, 0:4])
            pbt = psA.next().bc(BF16)
            for h in range(4):
                for mc in range(2):
                    P.tr(pbt[:, (2 * h + mc) * 128:(2 * h + mc + 1) * 128], pt[:, h, mc * 128:(mc + 1) * 128], ident_b)
            P.copy("scalar", pT.re("p a b -> p (a b)"), pbt)
            pov = [psA.next(), psA.next()]
            for h in range(4):
                o_ = pov[h // 2][:, (h % 2) * 256:(h % 2 + 1) * 256]
                for mc in range(2):
                    P.mm(o_, pT[:, 2 * h + mc, :], Vt[:, mc, h * 256:(h + 1) * 256], start=(mc == 0), stop=(mc == 1))
            for h in range(4):
                P.ts("vector", ox[:, h * 256:(h + 1) * 256], pov[h // 2][:, (h % 2) * 256:(h % 2 + 1) * 256], rs4[:, h:h + 1], None, ALU.mult)
            to_T(ox, oxT, s * 128)
        for c in range(2):
            wt = wget(B_WO + c)
            for s in range(NS):
                pb = psA.next()
                for kc in range(8):
                    P.mm(pb, oxT[:, kc, s * 128:(s + 1) * 128], wt[:, kc, :], start=(kc == 0), stop=(kc == 7))
                P.tt("vector", xt[:, s, c * 512:(c + 1) * 512], pb, xt[:, s, c * 512:(c + 1) * 512], ALU.add)
        tap("x2", xt, [128, NS, 1024])
        chk("B_x2")
        P.phase = "B_mlp"
        h3T = ttiles.next()
        for s in range(NS):
            norm_T(xt[:, s, :], h3T, s * 128)
        acc = {}
        for fg in range(4):
            hq = hidq[fg % 2]
            for c2 in range(2):
                wt = wget(B_W1 + fg * 2 + c2)
                for cc in range(4):
                    fi = c2 * 4 + cc
                    pb = psA.next()
                    for kc in range(8):
                        P.mm(pb[:, 0:T], wt[:, kc, cc * 128:(cc + 1) * 128], h3T[:, kc, :], start=(kc == 0), stop=(kc == 7))
                    r_ = wide_f.next()
                    P.act(r_[:, 0:T], pb[:, 0:T], AF.Relu)
                    P.tt("vector" if fi % 2 == 0 else "gpsimd", hq[:, fi, :], r_[:, 0:T], r_[:, 0:T], ALU.mult)
            if fg == 0:
                for s in range(NS):
                    for c in range(2):
                        acc[(s, c)] = psA.next()
                        psA.hold(acc[(s, c)])
            for c in range(2):
                wt = wget(B_W2 + fg * 2 + c)
                for s in range(NS):
                    for kc in range(8):
                        P.mm(acc[(s, c)], hq[:, kc, s * 128:(s + 1) * 128], wt[:, kc, :],
                             start=(fg == 0 and kc == 0), stop=(fg == 3 and kc == 7))
        for s in range(NS):
            for c in range(2):
                P.tt("vector", xt[:, s, c * 512:(c + 1) * 512], acc[(s, c)], xt[:, s, c * 512:(c + 1) * 512], ALU.add)
                psA.release(acc[(s, c)])
        tap("x3", xt, [128, NS, 1024])
        chk("B_x3")
        P.phase = "B_final"
        for s in range(NS):
            rs = rstd_of(xt[:, s, :], 1024)
            P.stt("vector", xt[:, s, :], xt[:, s, :], rs, gfin, ALU.mult, ALU.mult)
        d = P.dma("sync", y_d[q_, t0:t0 + T, :].rearrange("(s p) d -> p s d", p=128), xt)
        P.final_deps.append(d)

    try:
      chk("prep")
      for q_ in range(nseq):
        P.memset("vector", Sg.re("p a b -> p (a b)"), 0.0)
        for g in range(2):
            P.memset("gpsimd", Sgb[g].re("p a b -> p (a b)"), 0.0)
        P.memset("vector", Sl.re("p a b -> p (a b)"), 0.0)
        P.memset("gpsimd", Slb.re("p a b -> p (a b)"), 0.0)
        P.memset("gpsimd", halo.re("p a b -> p (a b)"), 0.0)
        kv_prep(q_)
        chk("kv")
        for st in range(nst):
            oaT, obT = stage_A(q_, st)
            chk("A")
            stage_B(q_, st, oaT, obT)
    except _Stop:
        pass

    bes = ExitStack()
    finals = P.finalize(bes)
    bes.enter_context(nc.allow_low_precision(reason="bf16 matmul operands by design; fp32 accumulation"))
    block = bes.enter_context(nc.Block())
    P.emit(block, finals)
    bes.close()
    es.close()
    ninst = {e: len(P.ops[e]) for e in ENGS}
    return nc, tap_outs, dict(ninst=ninst, nwaits=P.nwaits, nsems=P.nsems, op_phase=P.op_phase)


N_CORES = 8


def kernel(**inputs):
    x = np.asarray(inputs["x"], dtype=np.float32)
    mem = np.asarray(inputs["mem"], dtype=np.float32)
    B, S, _ = x.shape
    nseq = B // N_CORES
    lay = host_layout(inputs)
    nc, _, _ = build(nseq, S)
    in_maps = []
    for c in range(N_CORES):
        m = dict(lay)
        m["x"] = np.ascontiguousarray(x[c * nseq:(c + 1) * nseq])
        m["mem"] = np.ascontiguousarray(mem[c * nseq:(c + 1) * nseq])
        in_maps.append(m)
    res = run_bass_kernel_spmd(nc, in_maps, core_ids=list(range(N_CORES)))
    out = np.concatenate([np.asarray(r["y"], dtype=np.float32) for r in res.results], axis=0)
    return out
```

```python
from contextlib import ExitStack
from concourse.bass_utils import run_bass_kernel_spmd
import numpy as np
import concourse.bass as bass
import concourse.mybir as mybir

F32 = mybir.dt.float32
BF16 = mybir.dt.bfloat16
AF = mybir.ActivationFunctionType
ALU = mybir.AluOpType
AX = mybir.AxisListType

ENGS = ("sync", "scalar", "vector", "gpsimd", "tensor")


class Buf:
    __slots__ = ("name", "last_w", "readers", "excl")

    def __init__(self, name):
        self.name = name
        self.last_w = None
        self.readers = []
        self.excl = False


class V:
    __slots__ = ("ap", "buf", "dram")

    def __init__(self, ap, buf, dram=False):
        self.ap = ap
        self.buf = buf
        self.dram = dram

    def __getitem__(self, idx):
        return V(self.ap[idx], self.buf, self.dram)

    def re(self, pat, **kw):
        return V(self.ap.rearrange(pat, **kw), self.buf, self.dram)

    def bc(self, dt):
        return V(self.ap.bitcast(dt), self.buf, self.dram)


class Tile(V):
    def __init__(self, t, name):
        V.__init__(self, t[:], Buf(name))
        self.t = t


class Op:
    __slots__ = ("eng", "fn", "deps", "is_dma", "sem", "sigval", "signal", "idx", "waits")

    def __init__(self, eng, fn, deps, is_dma):
        self.eng = eng
        self.fn = fn
        self.deps = deps
        self.is_dma = is_dma
        self.sem = None
        self.sigval = 0
        self.signal = False
        self.waits = []


class Prog:
    def __init__(self, nc, es, same_engine_sync=False):
        self.nc = nc
        self.es = es
        self.ops = {e: [] for e in ENGS}
        self.same_engine_sync = same_engine_sync
        self.dma_sems = {}
        self.final_deps = []
        self.nsb = 0
        self.phase = ""
        self.op_phase = {e: [] for e in ENGS}

    def sb(self, name, shape, dt):
        t = self.es.enter_context(self.nc.sbuf_tensor("sb_" + name, list(shape), dt))
        return Tile(t, name)

    def dram(self, ap, name):
        return V(ap, Buf(name), True)

    def ps(self, name, shape, dt):
        t = self.es.enter_context(self.nc.psum_tensor("ps_" + name, list(shape), dt))
        tl = Tile(t, name)
        tl.buf.excl = True
        return tl

    def rec(self, eng, fn, reads=(), writes=(), is_dma=False, dma_key=None):
        deps = set()
        rb = [v.buf for v in reads if isinstance(v, V)]
        wb = [v.buf for v in writes if isinstance(v, V)]
        for b in rb:
            if b.last_w is not None:
                deps.add(b.last_w)
            if b.excl:
                for r in b.readers:
                    if r.eng != eng:
                        deps.add(r)
        for b in wb:
            lw = b.last_w
            if lw is not None and (lw.eng != eng or lw.is_dma or is_dma):
                deps.add(lw)
            for r in b.readers:
                if r.eng != eng or r.is_dma or is_dma:
                    deps.add(r)
        op = Op(eng, fn, deps, is_dma)
        if is_dma:
            op.sem = dma_key
        for b in wb:
            b.last_w = op
            b.readers = []
        for b in rb:
            if b not in wb:
                b.readers.append(op)
        op.idx = len(self.ops[eng])
        self.ops[eng].append(op)
        self.op_phase[eng].append(self.phase)
        return op

    def finalize(self, block_es):
        nc = self.nc
        for e in ENGS:
            for op in self.ops[e]:
                if op.is_dma:
                    op.signal = True
                for d in op.deps:
                    if d.is_dma:
                        continue
                    d.signal = True
        last = {}
        for e in ENGS:
            for op in self.ops[e]:
                if op.is_dma:
                    last[op.sem] = op
        for op in last.values():
            if op not in self.final_deps:
                self.final_deps.append(op)
        for d in self.final_deps:
            d.signal = True
        eng_sem = {e: block_es.enter_context(nc.semaphore("sem_" + e)) for e in ENGS}
        dma_sem = {}
        for e in ENGS:
            cnt = 0
            for op in self.ops[e]:
                if op.is_dma:
                    key = op.sem
                    if key not in dma_sem:
                        dma_sem[key] = [block_es.enter_context(nc.semaphore("dsem_%d" % len(dma_sem))), 0]
                    ent = dma_sem[key]
                    ent[1] += 16
                    op.sem = ent[0]
                    op.sigval = ent[1]
                elif op.signal:
                    cnt += 1
                    op.sem = eng_sem[e]
                    op.sigval = cnt
        nwaits = 0
        for e in ENGS:
            seen = {}
            for op in self.ops[e]:
                need = {}
                for d in op.deps:
                    if d is op:
                        continue
                    k = id(d.sem)
                    if seen.get(k, 0) >= d.sigval:
                        continue
                    if k not in need or need[k][1] < d.sigval:
                        need[k] = (d.sem, d.sigval)
                for k, (s, v) in need.items():
                    seen[k] = v
                    op.waits.append((s, v))
                    nwaits += 1
        self.nwaits = nwaits
        self.nsems = len(dma_sem) + len(ENGS)
        finals = [(d.sem, d.sigval) for d in self.final_deps]
        return finals

    def emit(self, block, finals, final_eng="gpsimd"):
        P = self

        def run(ename, e):
            for op in P.ops[ename]:
                for (s, v) in op.waits:
                    e.wait_ge(s, v)
                ins = op.fn(e)
                if op.signal:
                    ins.then_inc(op.sem, 16 if op.is_dma else 1)
            if ename == final_eng:
                for (s, v) in finals:
                    e.wait_ge(s, v)

        @block.sync
        def _(e):
            run("sync", e)

        @block.scalar
        def _(e):
            run("scalar", e)

        @block.vector
        def _(e):
            run("vector", e)

        @block.gpsimd
        def _(e):
            run("gpsimd", e)

        @block.tensor
        def _(e):
            run("tensor", e)

    def dma(self, eng, out, in_, key=None):
        k = key
        if k is None:
            if isinstance(out, V) and not out.dram:
                k = out.buf
            elif isinstance(in_, V) and not in_.dram:
                k = in_.buf
            else:
                k = out.buf if isinstance(out, V) else in_.buf
        o = out.ap if isinstance(out, V) else out
        i = in_.ap if isinstance(in_, V) else in_
        return self.rec(eng, lambda e: e.dma_start(out=o, in_=i),
                        reads=[in_], writes=[out], is_dma=True, dma_key=k)

    def mm(self, out, lhsT, rhs, start=True, stop=True):
        return self.rec("tensor", lambda e: e.matmul(out.ap, lhsT.ap, rhs.ap, start=start, stop=stop),
                        reads=[lhsT, rhs], writes=[out])

    def tr(self, out, in_, ident):
        return self.rec("tensor", lambda e: e.transpose(out.ap, in_.ap, ident.ap),
                        reads=[in_, ident], writes=[out])

    def act(self, out, in_, func, bias=None, scale=None, accum_out=None, eng="scalar"):
        kw = {}
        reads = [in_]
        if bias is not None:
            kw["bias"] = bias.ap if isinstance(bias, V) else bias
            reads.append(bias)
        if scale is not None:
            kw["scale"] = scale.ap if isinstance(scale, V) else scale
            reads.append(scale)
        writes = [out]
        if accum_out is not None:
            kw["accum_out"] = accum_out.ap
            writes.append(accum_out)
        return self.rec(eng, lambda e: e.activation(out.ap, in_.ap, func, **kw), reads=reads, writes=writes)

    def ts(self, eng, out, in0, s1, s2, op0, op1=None, accum_out=None):
        reads = [in0, s1, s2]
        a1 = s1.ap if isinstance(s1, V) else s1
        a2 = s2.ap if isinstance(s2, V) else s2
        kw = {}
        writes = [out]
        if op1 is not None:
            kw["op1"] = op1
        if accum_out is not None:
            kw["accum_out"] = accum_out.ap
            writes.append(accum_out)
        return self.rec(eng, lambda e: e.tensor_scalar(out=out.ap, in0=in0.ap, scalar1=a1, scalar2=a2, op0=op0, **kw),
                        reads=reads, writes=writes)

    def stt(self, eng, out, in0, scalar, in1, op0, op1):
        sc = scalar.ap if isinstance(scalar, V) else scalar
        eng = "vector"
        return self.rec(eng, lambda e: e.scalar_tensor_tensor(out=out.ap, in0=in0.ap, scalar=sc, in1=in1.ap, op0=op0, op1=op1),
                        reads=[in0, scalar, in1], writes=[out])

    def tt(self, eng, out, in0, in1, op):
        return self.rec(eng, lambda e: e.tensor_tensor(out=out.ap, in0=in0.ap, in1=in1.ap, op=op),
                        reads=[in0, in1], writes=[out])

    def copy(self, eng, out, in_):
        if eng == "scalar":
            return self.rec(eng, lambda e: e.copy(out=out.ap, in_=in_.ap), reads=[in_], writes=[out])
        return self.rec(eng, lambda e: e.tensor_copy(out=out.ap, in_=in_.ap), reads=[in_], writes=[out])

    def reduce(self, eng, out, in_, op, axis=AX.X):
        return self.rec(eng, lambda e: e.tensor_reduce(out=out.ap, in_=in_.ap, axis=axis, op=op),
                        reads=[in_], writes=[out])

    def recip(self, eng, out, in_):
        return self.rec(eng, lambda e: e.reciprocal(out=out.ap, in_=in_.ap), reads=[in_], writes=[out])

    def scan(self, out, d0, d1, initial, op0, op1):
        return self.rec("vector", lambda e: e.tensor_tensor_scan(out=out.ap, data0=d0.ap, data1=d1.ap, initial=initial, op0=op0, op1=op1),
                        reads=[d0, d1], writes=[out])

    def memset(self, eng, out, val):
        return self.rec(eng, lambda e: e.memset(out.ap, val), reads=[], writes=[out])


D = 1024
NH = 8
HD = 128
LH = 4
LKD = 128
LVD = 256
NMEM = 256
DFF = 4096
NS = 2
T = NS * 128
EPS = 1e-6
NBLK = 48
BIG = 1.0e30
DGE_SCRATCH = 1024

B_GDN = 0
B_GLA = 8
B_R = 12
B_GATE = 14
B_BG = 18
B_BL = 20
B_OUT = 22
B_WQ = 24
B_WO = 26
B_W1 = 28
B_W2 = 36
B_WK = 44
B_WV = 46

IN_SIZES = (1024, 1024, 1024, 1024, 8, 8, 512, 512, 1024, 1024, 16, 1024, 1024)


def host_layout(inp):
    f = lambda a: np.ascontiguousarray(np.asarray(a, dtype=np.float32))
    w_in = f(inp["w_in"][0])
    offs = np.cumsum((0,) + IN_SIZES)
    gq, gk, gv, gz, ga, gb, lq, lk, lv, lr, lgate, gate_a, gate_b = [w_in[:, offs[i]:offs[i + 1]] for i in range(13)]
    blocks = []

    def blk(mat):
        assert mat.shape == (1024, 512), mat.shape
        return mat.reshape(8, 128, 512).transpose(1, 0, 2)

    for h in range(8):
        s = slice(h * 128, (h + 1) * 128)
        blocks.append(blk(np.concatenate([gq[:, s], gk[:, s], gv[:, s], gz[:, s]], axis=1)))
    for h in range(4):
        blocks.append(blk(np.concatenate([lq[:, h * 128:(h + 1) * 128], lk[:, h * 128:(h + 1) * 128],
                                          lv[:, h * 256:(h + 1) * 256]], axis=1)))
    for c in range(2):
        blocks.append(blk(lr[:, c * 512:(c + 1) * 512]))
    for g in (gate_a, gate_b):
        for c in range(2):
            blocks.append(blk(g[:, c * 512:(c + 1) * 512]))
    for name in ("w_branch_gdn", "w_branch_gla", "w_out", "xattn_wq", "xattn_wo"):
        w = f(inp[name][0])
        for c in range(2):
            blocks.append(blk(w[:, c * 512:(c + 1) * 512]))
    w1 = f(inp["mlp_w1"][0])
    for c in range(8):
        blocks.append(blk(w1[:, c * 512:(c + 1) * 512]))
    w2 = f(inp["mlp_w2"][0])
    for fg in range(4):
        for c in range(2):
            blocks.append(blk(w2[fg * 1024:(fg + 1) * 1024, c * 512:(c + 1) * 512]))
    for name in ("xattn_wk", "xattn_wv"):
        w = f(inp[name][0])
        for c in range(2):
            blocks.append(blk(w[:, c * 512:(c + 1) * 512]))
    wblk = np.ascontiguousarray(np.stack(blocks, 0)).reshape(NBLK, 128, 4096)
    wsm = np.ascontiguousarray(np.concatenate([ga, gb, lgate], axis=1).reshape(8, 128, 32).transpose(1, 0, 2)).reshape(128, 256)

    def gcol(g):
        return np.ascontiguousarray(f(g).reshape(8, 128).T)

    rep = lambda v: np.ascontiguousarray(np.broadcast_to(f(v).reshape(1, -1), (128, f(v).size)))
    gcols = np.concatenate([gcol(inp["norm_mix_g"][0]), gcol(inp["norm_xattn_g"][0]),
                            gcol(inp["norm_mlp_g"][0]), gcol(inp["norm_mem_g"][0])], axis=1)
    cwt = f(inp["gdn_conv_w"][0])
    cw = np.ascontiguousarray(cwt.reshape(4, 24, 128).transpose(2, 1, 0)).reshape(128, 96)
    small = np.concatenate([
        gcols,
        cw,
        rep(inp["gdn_a_log"][0]),
        rep(inp["gdn_dt_bias"][0]),
        np.ascontiguousarray(f(inp["gla_b_gate"][0]).reshape(4, 128).T),
    ], axis=1)
    small = np.ascontiguousarray(small)
    wide = np.concatenate([
        rep(inp["norm_final_g"]),
        rep(np.tile(f(inp["gdn_norm_g"][0]), 8)),
        rep(np.tile(f(inp["gla_norm_g"][0]), 4)),
    ], axis=1)
    wide = np.ascontiguousarray(wide)
    wg2 = f(inp["gla_w_gate2"][0])
    p = np.arange(128)[:, None]
    q = np.arange(128)[None, :]
    ident = (p == q).astype(np.float32)
    tri = (p <= q).astype(np.float32)
    same32 = (p // 32) == (q // 32)
    pm_d = np.where((p > q) & same32, 0.0, BIG).astype(np.float32)
    pm_r = np.where((p > q) & (~same32), 0.0, BIG).astype(np.float32)
    nm_t = np.where(q >= p, 0.0, -BIG).astype(np.float32)
    m01t = (q >= p).astype(np.float32)
    ones = np.ones((128, 128), np.float32)
    consts = np.ascontiguousarray(np.concatenate([ident, tri, pm_d, pm_r, nm_t, m01t, ones], axis=1))
    return dict(wblk=wblk, wsm=wsm, small=small, wide=wide, wg2=wg2, consts=consts)


class RR:
    def __init__(self, items):
        self.items = items
        self.i = 0
        self.held = set()

    def next(self):
        for _ in range(len(self.items) + 1):
            k = self.i % len(self.items)
            self.i += 1
            if k not in self.held:
                return self.items[k]
        raise RuntimeError("all held")

    def hold(self, it):
        self.held.add(self.items.index(it))

    def release(self, it):
        self.held.discard(self.items.index(it))


class _Stop(Exception):
    pass


def build(nseq, S, taps=None, stop=None):
    assert S % T == 0
    nst = S // T
    nc = bass.Bass("TRN2", target_bir_lowering=False, dynamic_dma_scratch_size=DGE_SCRATCH)
    dr = lambda name, shape, dt=F32, kind="ExternalInput": nc.dram_tensor(name, list(shape), dt, kind=kind).ap()
    x_d = dr("x", [nseq, S, D])
    mem_d = dr("mem", [nseq, NMEM, D])
    wblk_d = dr("wblk", [NBLK, 128, 4096])
    wsm_d = dr("wsm", [128, 256])
    small_d = dr("small", [128, 148])
    wide_d = dr("wide", [128, 3072])
    wg2_d = dr("wg2", [16, 512])
    consts_d = dr("consts", [128, 7 * 128])
    y_d = dr("y", [nseq, S, D], F32, "ExternalOutput")
    wbf_ap = dr("wbf", [NBLK, 128, 4096], BF16, "Internal")
    tap_outs = {}

    es = ExitStack()
    P = Prog(nc, es)
    wbf = [P.dram(wbf_ap[b], "wbf%d" % b) for b in range(NBLK)]

    def chk(name):
        P.phase = "after_" + name
        if stop == name:
            raise _Stop()

    def tap(name, v, shape, dt=F32):
        if taps is None or name not in taps or name in tap_outs:
            return
        o = dr("tap_" + name, shape, dt, "ExternalOutput")
        tap_outs[name] = o
        stg = P.sb("tapstg_" + name, shape, dt)
        P.copy("vector", stg, v)
        d = P.dma("sync", o, stg)
        P.final_deps.append(d)

    cst = P.sb("cst", [128, 7 * 128], F32)
    ident_f = cst[:, 0:128]
    tri_f = cst[:, 128:256]
    pm_d = cst[:, 256:384]
    pm_r = cst[:, 384:512]
    nm_t = cst[:, 512:640]
    m01_f = cst[:, 640:768]
    ones_f = cst[:, 768:896]
    ident_b = P.sb("ident_b", [128, 128], BF16)
    ident4_b = P.sb("ident4_b", [128, 4, 128], BF16)
    ones_b = P.sb("ones_b", [128, 128], BF16)
    m01t4 = P.sb("m01t4", [128, 4, 128], BF16)
    small = P.sb("small", [128, 148], F32)
    gcols = small[:, 0:32]
    cw = small[:, 32:128]
    alog = small[:, 128:136]
    dtb = small[:, 136:144]
    bgate = small[:, 144:148]
    negA = P.sb("negA", [128, 8], F32)
    negb = P.sb("negb", [128, 4], F32)
    gfin = P.sb("gfin", [128, 1024], F32)
    wide_b = P.sb("wide_b", [128, 2048], BF16)
    gng = wide_b[:, 0:1024]
    lng = wide_b[:, 1024:2048]
    wsm = P.sb("wsm", [128, 8, 32], BF16)
    wg2 = P.sb("wg2", [16, 512], BF16)

    Sg = P.sb("Sg", [128, 8, 128], F32)
    Sgb = [P.sb("Sgb%d" % g, [128, 4, 128], BF16) for g in range(2)]
    Sl = P.sb("Sl", [128, 4, 256], F32)
    Slb = P.sb("Slb", [128, 4, 256], BF16)
    halo = P.sb("halo", [128, 24, 3], F32)
    KT = P.sb("KT", [128, 8, 256], BF16)
    Vt = P.sb("Vt", [128, 2, 1024], BF16)

    NSLOT = 3
    slots = [P.sb("wslot%d" % i, [128, 8, 512], BF16) for i in range(NSLOT)]
    xt = P.sb("xt", [128, NS, 1024], F32)
    ttiles = RR([P.sb("tt%d" % i, [128, 8, T], BF16) for i in range(4)])
    banks = [P.ps("bank%d" % i, [128, 512], F32) for i in range(8)]
    psA = RR(banks)

    hb = P.sb("hb", [128, 1024], BF16)
    smalls = RR([P.sb("sm%d" % i, [128, 8], F32) for i in range(24)])

    raws = RR([P.sb("raw%d" % i, [128, 3 + T], F32) for i in range(12)])
    convy = RR([P.sb("convy%d" % i, [128, T], F32) for i in range(12)])
    sc_q = RR([P.sb("scq%d" % i, [128, T], BF16) for i in range(2)])
    qn = [P.sb("qn%d" % g, [128, 4, T], BF16) for g in range(2)]
    kn = [P.sb("kn%d" % g, [128, 4, T], BF16) for g in range(2)]
    vs = [P.sb("vs%d" % g, [128, 4, T], BF16) for g in range(2)]
    zg = P.sb("zg", [128, NS, 1024], BF16)
    gab = P.sb("gab", [128, NS, 32], F32)
    lgT = P.sb("lgT", [16, T], BF16)
    vl = P.sb("vl", [128, NS, 1024], BF16)
    rg = P.sb("rg", [128, NS, 1024], BF16)
    sa = P.sb("sa", [128, NS, 1024], BF16)
    sb_ = P.sb("sb_", [128, NS, 1024], BF16)
    wide_f = RR([P.sb("widef%d" % i, [128, 512], F32) for i in range(2)])

    GT = P.sb("GT", [128, 4, 128], F32)
    Gs = P.sb("Gs", [128, 4, 128], F32)
    EG = P.sb("EG", [128, 4, 128], BF16)
    args = RR([P.sb("arg%d" % i, [128, 4, 128], F32) for i in range(2)])
    mk = lambda n: P.sb(n, [128, 4, 128], BF16)
    Fd, Fr, Dt = mk("Fd"), mk("Fr"), mk("Dt")
    Ld, Rr, AqkT, LdT, qdT, Rw, Ru, kdec = [mk(n) for n in ("Ld", "Rr", "AqkT", "LdT", "qdT", "Rw", "Ru", "kdec")]
    Mp = [mk("Mp0"), mk("Mp1")]
    MTp = [mk("MTp0"), mk("MTp1")]
    Yp = [mk("Yp0"), mk("Yp1")]
    Dp = [mk("Dp0"), mk("Dp1")]
    Zt, wTn, dlt = [mk(n) for n in ("Zt", "wTn", "dlt")]
    Wn, Nb, T1 = Dt, Fr, Fd
    tokb = [P.sb("tokb%d" % i, [128, 1024], BF16) for i in range(2)]
    oa = tokb[0]

    cs = P.sb("cs", [128, 4, T], F32)
    glp = [P.sb("glp%d" % i, [128, 4, T], BF16) for i in range(4)]
    smallsB = RR([P.sb("smB%d" % i, [128, 8], F32) for i in range(12)])
    lsc = RR([P.sb("lsc%d" % i, [128, T], F32) for i in range(2)])
    efac = RR([P.sb("efac%d" % i, [128, 128], F32) for i in range(4)])
    ATm = mk("ATm")
    kdl = mk("kdl")
    ob = tokb[1]

    pt = vl[:, 0, :].re("p (a b) -> p a b", a=4)
    pT = vl[:, 1, :].re("p (a b) -> p a b", a=8)
    ox = tokb[1]
    hidq = [P.sb("hidq%d" % i, [128, 8, T], BF16) for i in range(2)]
    yam = zg


    if taps is not None and "probe" in taps:
        for kb in range(64, 0, -1):
            try:
                es2 = ExitStack()
                es2.enter_context(nc.sbuf_tensor("probe%d" % kb, [128, kb * 256], F32))
                print("SBUF slack >= %d KB" % kb)
                es2.close()
                break
            except Exception as ex:
                pass
    order = []
    for q_ in range(nseq):
        order += [B_WK, B_WK + 1, B_WV, B_WV + 1]
        for st in range(nst):
            order += list(range(0, 28)) + [28, 29, 36, 37, 30, 31, 38, 39, 32, 33, 40, 41, 34, 35, 42, 43]
    wstate = {"issued": 0, "cur": 0}

    def wissue():
        i = wstate["issued"]
        if i < len(order):
            prep_block(order[i])
            P.dma("sync", slots[i % NSLOT].re("p a b -> p (a b)"), wbf[order[i]])
            wstate["issued"] += 1

    def wget(expect):
        c = wstate["cur"]
        assert order[c] == expect, (c, order[c], expect)
        while wstate["issued"] <= min(c + NSLOT - 1, len(order) - 1):
            wissue()
        wstate["cur"] += 1
        return slots[c % NSLOT]

    P.dma("sync", cst, consts_d)
    P.dma("sync", small, small_d)
    P.dma("sync", gfin, wide_d[:, 0:1024])
    P.copy("vector", ident_b, ident_f)
    P.copy("vector", ones_b, ones_f)
    for h in range(4):
        P.copy("vector", ident4_b[:, h, :], ident_f)
        P.copy("vector", m01t4[:, h, :], m01_f)
    P.dma("sync", wide_f.items[1][0:16, :], wg2_d)
    P.copy("vector", wg2, wide_f.items[1][0:16, :])
    P.act(negA, alog, AF.Exp)
    P.ts("vector", negA, negA, -1.0, None, ALU.mult)
    P.ts("vector", negb, bgate, -1.0, None, ALU.mult)
    for c in range(4):
        sf_ = wide_f.items[c % 2]
        P.dma("sync", sf_, wide_d[:, 1024 + c * 512:1024 + (c + 1) * 512])
        P.copy("vector", wide_b[:, c * 512:(c + 1) * 512], sf_)
    gain_of = {}
    for b in range(0, 18):
        gain_of[b] = 0
    gain_of[B_WQ] = gain_of[B_WQ + 1] = 1
    for b in range(B_W1, B_W1 + 8):
        gain_of[b] = 2
    for b in range(B_WK, B_WK + 4):
        gain_of[b] = 3
    pstf = [P.sb("pstf%d" % i, [128, 512], F32) for i in range(2)]
    pstb = [P.sb("pstb%d" % i, [128, 512], BF16) for i in range(2)]
    pstate = {"qi": 0, "done": set()}

    def prep_block(b):
        if b in pstate["done"]:
            return
        pstate["done"].add(b)
        gi = gain_of.get(b, None)
        for kc in range(8):
            qi = pstate["qi"]
            sf = pstf[qi % 2]
            sbf = pstb[qi % 2]
            eng = "vector" if qi % 2 == 0 else "scalar"
            P.dma("sync", sf, wblk_d[b][:, kc * 512:(kc + 1) * 512])
            if gi is None:
                P.copy(eng, sbf, sf)
            elif eng == "scalar":
                P.act(sbf, sf, AF.Identity, scale=gcols[:, gi * 8 + kc:gi * 8 + kc + 1])
            else:
                P.ts(eng, sbf, sf, gcols[:, gi * 8 + kc:gi * 8 + kc + 1], None, ALU.mult)
            P.dma("sync", wbf[b][:, kc * 512:(kc + 1) * 512], sbf)
            pstate["qi"] += 1

    stf = [wide_f.items[0]]
    sf = stf[0]
    P.dma("sync", sf[:, 0:256], wsm_d)
    for kc in range(8):
        P.ts("vector", wsm[:, kc, :], sf[:, kc * 32:(kc + 1) * 32], gcols[:, kc:kc + 1], None, ALU.mult)

    def rstd_of(xrow, ncols):
        ss = smalls.next()
        P.act(hb[:, 0:ncols], xrow, AF.Square, accum_out=ss[:, 0:1])
        rs = smalls.next()
        P.ts("vector", rs[:, 0:1], ss[:, 0:1], 1.0 / ncols, EPS, ALU.mult, ALU.add)
        P.act(rs[:, 1:2], rs[:, 0:1], AF.Ln)
        P.act(rs[:, 2:3], rs[:, 1:2], AF.Exp, scale=-0.5)
        return rs[:, 2:3]

    def to_T(src_b, dstT, col0, ncol=128, evac="scalar"):
        pb = psA.next().bc(BF16)
        for c in range(8):
            P.tr(pb[:, c * 128:(c + 1) * 128], src_b[:, c * 128:(c + 1) * 128], ident_b)
        P.copy(evac, dstT[:, :, col0:col0 + 128], pb.re("p (a b) -> p a b", a=8))

    def norm_T(xrow, dstT, col0):
        rs = rstd_of(xrow, 1024)
        P.ts("vector", hb, xrow, rs, None, ALU.mult)
        to_T(hb, dstT, col0)

    def kv_prep(q_):
        mT = ttiles.next()
        for mc in range(2):
            P.dma("sync", xt[:, 0, :], mem_d[q_, mc * 128:(mc + 1) * 128, :])
            norm_T(xt[:, 0, :], mT, mc * 128)
        for c2 in range(2):
            wt = wget(B_WK + c2)
            for cc in range(4):
                pb = psA.next()
                for kc in range(8):
                    P.mm(pb[:, 0:256], wt[:, kc, cc * 128:(cc + 1) * 128], mT[:, kc, 0:256], start=(kc == 0), stop=(kc == 7))
                P.copy("scalar", KT[:, c2 * 4 + cc, :], pb[:, 0:256])
        for c2 in range(2):
            wt = wget(B_WV + c2)
            for mc in range(2):
                pb = psA.next()
                for kc in range(8):
                    P.mm(pb, mT[:, kc, mc * 128:(mc + 1) * 128], wt[:, kc, :], start=(kc == 0), stop=(kc == 7))
                P.copy("scalar", Vt[:, mc, c2 * 512:(c2 + 1) * 512], pb)

    def gdn_front(h, hT, wt):
        pbs, rws, ys = [], [], []
        for j in range(3):
            pb = psA.next()
            for kc in range(8):
                P.mm(pb[:, 0:T], wt[:, kc, j * 128:(j + 1) * 128], hT[:, kc, :], start=(kc == 0), stop=(kc == 7))
            pbs.append(pb)
        for j in range(3):
            g = j * 8 + h
            raw = raws.next()
            P.copy("gpsimd", raw[:, 0:3], halo[:, g, :])
            P.copy("scalar", raw[:, 3:3 + T], pbs[j][:, 0:T])
            P.copy("gpsimd", halo[:, g, :], raw[:, T:T + 3])
            rws.append(raw)
        for j in range(3):
            g = j * 8 + h
            y = convy.next()
            P.ts("vector", y, rws[j][:, 3:3 + T], cw[:, g * 4 + 3:g * 4 + 4], None, ALU.mult)
            ys.append(y)
        for jj in (2, 1, 0):
            for j in range(3):
                g = j * 8 + h
                P.stt("vector", ys[j], rws[j][:, jj:jj + T], cw[:, g * 4 + jj:g * 4 + jj + 1], ys[j], ALU.mult, ALU.add)
        return ys, rws

    def gdn_back(h, ys, rws):
        g4, hh = divmod(h, 4)
        es = [rws[j][:, 3:3 + T] for j in range(3)]
        for j in range(3):
            P.act(es[j], ys[j], AF.Exp, scale=-1.0)
        for j in range(3):
            P.act(es[j], es[j], AF.Ln, bias=1.0)
        for j in range(3):
            P.act(es[j], es[j], AF.Exp, scale=-1.0)
        for j in range(2):
            P.tt("vector" if j == 0 else "gpsimd", ys[j], ys[j], es[j], ALU.mult)
        P.tt("gpsimd", vs[g4][:, hh, :], ys[2], es[2], ALU.mult)
        pns = []
        for j in range(2):
            sq = sc_q.next()
            P.tt("gpsimd", sq, ys[j], ys[j], ALU.mult)
            pn = psA.next()
            P.mm(pn[:, 0:T], ones_b, sq)
            pns.append(pn)
        for j in range(2):
            P.act(es[j], pns[j][:, 0:T], AF.Ln, bias=EPS)
        for j in range(2):
            if j == 0:
                P.act(es[j], es[j], AF.Exp, scale=-0.5, bias=float(np.log(HD ** -0.5)))
            else:
                P.act(es[j], es[j], AF.Exp, scale=-0.5)
        for j in range(2):
            dst = (qn if j == 0 else kn)[g4][:, hh, :]
            P.tt("vector", dst, ys[j], es[j], ALU.mult)

    zbank = {}

    def gdn_z(h, hT, wt):
        g4, hh = divmod(h, 4)
        for s in range(NS):
            if hh == 0:
                zbank[s] = psA.next()
                psA.hold(zbank[s])
            pb = zbank[s]
            for kc in range(8):
                P.mm(pb[:, hh * 128:(hh + 1) * 128], hT[:, kc, s * 128:(s + 1) * 128], wt[:, kc, 384:512], start=(kc == 0), stop=(kc == 7))
            if hh == 3:
                e = wide_f.next()
                P.act(e, pb, AF.Exp, scale=-1.0)
                P.act(e, e, AF.Ln, bias=1.0)
                P.act(e, e, AF.Exp, scale=-1.0)
                P.tt("vector", e, pb, e, ALU.mult)
                P.tt("gpsimd", zg[:, s, g4 * 512:(g4 + 1) * 512], e, gng[:, g4 * 512:(g4 + 1) * 512], ALU.mult)
                psA.release(pb)

    dsc = {}

    def gdn_scalars(s):
        t8 = smalls.next()
        P.tt("vector", t8, gab[:, s, 0:8], dtb, ALU.add)
        P.act(t8, t8, AF.Exp)
        sp8 = smalls.next()
        P.act(sp8, t8, AF.Ln, bias=1.0)
        g8 = smalls.next()
        P.tt("vector", g8, sp8, negA, ALU.mult)
        eb8 = smalls.next()
        P.act(eb8, gab[:, s, 8:16], AF.Exp, scale=-1.0)
        lb8 = smalls.next()
        P.act(lb8, eb8, AF.Ln, bias=1.0)
        pb = psA.next()
        P.mm(pb[:, 0:8], tri_f, g8)
        gc8 = smalls.next()
        P.copy("vector", gc8, pb[:, 0:8])
        gcb8 = smalls.next()
        P.tt("vector", gcb8, gc8, lb8, ALU.subtract)
        beta8 = smalls.next()
        P.act(beta8, lb8, AF.Exp, scale=-1.0)
        bg8 = smalls.next()
        P.act(bg8, gcb8, AF.Exp)
        tap("g8", g8, [128, 8]); tap("gc8", gc8, [128, 8]); tap("beta8", beta8, [128, 8])
        chk("G1")
        dsc[s] = (g8, gc8, gcb8, beta8, bg8)

    def gdn_unit(s, g4, oaT):
        c0 = s * 128
        cols = slice(c0, c0 + 128)
        if g4 == 0:
            gdn_scalars(s)
        g8, gc8, gcb8, beta8, bg8 = dsc[s]
        hs = [g4 * 4 + i for i in range(4)]
        for hh, h in enumerate(hs):
            P.act(GT[:, hh, :], tri_f, AF.Identity, scale=g8[:, h:h + 1])
        pb = psA.next()
        P.mm(pb, ones_f, GT.re("p a b -> p (a b)"))
        yield
        P.copy("scalar", Gs.re("p a b -> p (a b)"), pb)
        P.act(EG.re("p a b -> p (a b)"), Gs.re("p a b -> p (a b)"), AF.Exp)
        gl4 = smalls.next()
        P.copy("vector", gl4[:, 0:4], Gs[:, :, 127])
        gend4 = smalls.next()
        P.act(gend4[:, 0:4], gl4[:, 0:4], AF.Exp)
        ekd4 = smalls.next()
        P.tt("vector", ekd4[:, 0:4], gl4[:, 0:4], gc8[:, g4 * 4:g4 * 4 + 4], ALU.subtract)
        P.act(ekd4[:, 0:4], ekd4[:, 0:4], AF.Exp)
        chk("G2")
        yield
        pbk = psA.next().bc(BF16)
        pbv = psA.next().bc(BF16)
        for hh, h in enumerate(hs):
            P.tr(pbk[:, hh * 128:(hh + 1) * 128], kn[g4][:, hh, cols], ident_b)
        for hh, h in enumerate(hs):
            P.tr(pbv[:, hh * 128:(hh + 1) * 128], vs[g4][:, hh, cols], ident_b)
        yield
        for hh, h in enumerate(hs):
            P.ts("vector", Rw[:, hh, :], pbk[:, hh * 128:(hh + 1) * 128], bg8[:, h:h + 1], None, ALU.mult)
            P.act(kdec[:, hh, :], pbk[:, hh * 128:(hh + 1) * 128], AF.Identity, scale=ekd4[:, hh:hh + 1])
            P.act(Ru[:, hh, :], pbv[:, hh * 128:(hh + 1) * 128], AF.Identity, scale=beta8[:, h:h + 1])
        chk("G3")
        yield
        pkk = psA.next()
        pqk = psA.next()
        for hh, h in enumerate(hs):
            P.mm(pkk[:, hh * 128:(hh + 1) * 128], kn[g4][:, hh, cols], kn[g4][:, hh, cols])
        for hh, h in enumerate(hs):
            P.mm(pqk[:, hh * 128:(hh + 1) * 128], kn[g4][:, hh, cols], qn[g4][:, hh, cols])
        yield
        a_d = args.next()
        for hh, h in enumerate(hs):
            P.stt("vector" if hh % 2 == 0 else "gpsimd", a_d[:, hh, :], Gs[:, hh, :], gcb8[:, h:h + 1], pm_d, ALU.subtract, ALU.max)
        P.act(Fd.re("p a b -> p (a b)"), a_d.re("p a b -> p (a b)"), AF.Exp, scale=-1.0)
        a_r = args.next()
        for hh, h in enumerate(hs):
            P.stt("vector" if hh % 2 == 0 else "gpsimd", a_r[:, hh, :], Gs[:, hh, :], gcb8[:, h:h + 1], pm_r, ALU.subtract, ALU.max)
        P.act(Fr.re("p a b -> p (a b)"), a_r.re("p a b -> p (a b)"), AF.Exp, scale=-1.0)
        a_t = args.next()
        for hh, h in enumerate(hs):
            P.stt("vector" if hh % 2 == 0 else "gpsimd", a_t[:, hh, :], Gs[:, hh, :], gc8[:, h:h + 1], nm_t, ALU.subtract, ALU.min)
        P.act(Dt.re("p a b -> p (a b)"), a_t.re("p a b -> p (a b)"), AF.Exp)
        fl = lambda t_: t_.re("p a b -> p (a b)")
        P.tt("vector", fl(Ld), pkk, fl(Fd), ALU.mult)
        P.tt("vector", fl(Rr), pkk, fl(Fr), ALU.mult)
        P.tt("vector", fl(AqkT), pqk, fl(Dt), ALU.mult)
        for hh in range(4):
            P.tt("gpsimd", qdT[:, hh, :], qn[g4][:, hh, cols], EG[:, hh, :], ALU.mult)
        chk("G4")
        yield
        pbt = psA.next().bc(BF16)
        for hh in range(4):
            P.tr(pbt[:, hh * 128:(hh + 1) * 128], Ld[:, hh, :], ident_b)
        yield
        P.copy("scalar", fl(LdT), pbt[:, 0:512])
        chk("G5")
        P.tt("vector", fl(Yp[0]), fl(ident4_b), fl(LdT), ALU.subtract)
        P.tt("gpsimd", fl(Dp[0]), fl(ident4_b), fl(Ld), ALU.subtract)

        def squares(Mc, MTc, Mn, MTn):
            p1 = psA.next()
            p2 = psA.next()
            for hh in range(4):
                P.mm(p1[:, hh * 128:(hh + 1) * 128], MTc[:, hh, :], Mc[:, hh, :])
            for hh in range(4):
                P.mm(p2[:, hh * 128:(hh + 1) * 128], Mc[:, hh, :], MTc[:, hh, :])
            return p1, p2

        Mc, MTc = Ld, LdT
        Mn, MTn = Mp[1], MTp[1]
        yield
        p1, p2 = squares(Mc, MTc, Mn, MTn)
        yield
        P.copy("scalar", fl(Mn), p1)
        P.copy("vector", fl(MTn), p2)
        yi = 0
        for m in range(1, 5):
            Mc, MTc = Mn, MTn
            yield
            p3 = psA.next()
            p4 = psA.next()
            for hh in range(4):
                P.mm(p3[:, hh * 128:(hh + 1) * 128], Mc[:, hh, :], Yp[yi][:, hh, :], start=True, stop=False)
                P.mm(p3[:, hh * 128:(hh + 1) * 128], ident_b, Yp[yi][:, hh, :], start=False, stop=True)
            for hh in range(4):
                P.mm(p4[:, hh * 128:(hh + 1) * 128], MTc[:, hh, :], Dp[yi][:, hh, :], start=True, stop=False)
                P.mm(p4[:, hh * 128:(hh + 1) * 128], ident_b, Dp[yi][:, hh, :], start=False, stop=True)
            if m < 4:
                Mn, MTn = Mp[(m + 1) % 2], MTp[(m + 1) % 2]
                p1, p2 = squares(Mc, MTc, Mn, MTn)
            yield
            P.copy("scalar", fl(Yp[1 - yi]), p3)
            P.copy("vector", fl(Dp[1 - yi]), p4)
            if m < 4:
                P.copy("scalar", fl(Mn), p1)
                P.copy("vector", fl(MTn), p2)
            yi = 1 - yi
        Yd, Dd = Yp[yi], Dp[yi]
        chk("G6")
        DRu, DRw = Mp[0], MTp[0]
        yield
        p1 = psA.next()
        p2 = psA.next()
        p3 = psA.next()
        p4 = psA.next()
        for hh in range(4):
            P.mm(p1[:, hh * 128:(hh + 1) * 128], Rr[:, hh, :], Yd[:, hh, :])
        for hh in range(4):
            P.mm(p2[:, hh * 128:(hh + 1) * 128], Yd[:, hh, :], Rr[:, hh, :])
        for hh in range(4):
            P.mm(p3[:, hh * 128:(hh + 1) * 128], Yd[:, hh, :], Ru[:, hh, :])
        for hh in range(4):
            P.mm(p4[:, hh * 128:(hh + 1) * 128], Yd[:, hh, :], Rw[:, hh, :])
        yield
        P.tt("vector", fl(Wn), fl(ident4_b), p1, ALU.subtract)
        P.copy("scalar", fl(Nb), p2)
        P.copy("scalar", fl(DRu), p3)
        P.copy("vector", fl(DRw), p4)
        yield
        p3 = psA.next()
        for hh in range(4):
            P.mm(p3[:, hh * 128:(hh + 1) * 128], Nb[:, hh, :], Wn[:, hh, :])
        yield
        P.copy("scalar", fl(T1), p3)
        yield
        p4 = psA.next()
        for hh in range(4):
            P.mm(p4[:, hh * 128:(hh + 1) * 128], Nb[:, hh, :], T1[:, hh, :], start=True, stop=False)
            P.mm(p4[:, hh * 128:(hh + 1) * 128], ident_b, Wn[:, hh, :], start=False, stop=True)
        yield
        P.copy("scalar", fl(Zt), p4)
        chk("G7")
        yield
        p6 = psA.next()
        for hh in range(4):
            P.mm(p6[:, hh * 128:(hh + 1) * 128], DRw[:, hh, :], Zt[:, hh, :])
        yield
        P.ts("vector", fl(wTn), p6, -1.0, None, ALU.mult)
        yield
        p7 = psA.next()
        for hh in range(4):
            P.mm(p7[:, hh * 128:(hh + 1) * 128], Zt[:, hh, :], DRu[:, hh, :], start=True, stop=False)
            P.mm(p7[:, hh * 128:(hh + 1) * 128], wTn[:, hh, :], Sgb[g4][:, hh, :], start=False, stop=True)
        yield
        P.copy("scalar", fl(dlt), p7)
        yield
        p8 = psA.next()
        for hh in range(4):
            P.mm(p8[:, hh * 128:(hh + 1) * 128], qdT[:, hh, :], Sgb[g4][:, hh, :], start=True, stop=False)
            P.mm(p8[:, hh * 128:(hh + 1) * 128], AqkT[:, hh, :], dlt[:, hh, :], start=False, stop=True)
        p9 = psA.next()
        for hh in range(4):
            P.mm(p9[:, hh * 128:(hh + 1) * 128], kdec[:, hh, :], dlt[:, hh, :])
        yield
        for hh, h in enumerate(hs):
            P.stt("vector", Sg[:, h, :], Sg[:, h, :], gend4[:, hh:hh + 1], p9[:, hh * 128:(hh + 1) * 128], ALU.mult, ALU.add)
        P.copy("scalar", fl(Sgb[g4]), Sg[:, g4 * 4:(g4 + 1) * 4, :].re("p a b -> p (a b)"))
        sq = wide_f.next()
        P.act(sq, p8, AF.Square)
        ss4 = smalls.next()
        P.reduce("vector", ss4[:, 0:4], sq.re("p (a b) -> p a b", a=4), ALU.add)
        P.ts("vector", ss4[:, 0:4], ss4[:, 0:4], 1.0 / HD, EPS, ALU.mult, ALU.add)
        P.act(ss4[:, 0:4], ss4[:, 0:4], AF.Ln)
        rs4 = smalls.next()
        P.act(rs4[:, 0:4], ss4[:, 0:4], AF.Exp, scale=-0.5)
        for hh, h in enumerate(hs):
            P.stt("vector", oa[:, h * 128:(h + 1) * 128], p8[:, hh * 128:(hh + 1) * 128], rs4[:, hh:hh + 1],
                  zg[:, s, h * 128:(h + 1) * 128], ALU.mult, ALU.mult)
        if g4 == 1:
            tap("oa", oa, [128, 1024], BF16)
            to_T(oa, oaT, c0)
        yield

    def gla_proj(h, hT, wt):
        pz = psA.next()
        P.mm(pz[:, 0:T], wg2[:, h * 128:(h + 1) * 128], lgT)
        pq = psA.next()
        pk = psA.next()
        for kc in range(8):
            P.mm(pq[:, 0:T], wt[:, kc, 0:128], hT[:, kc, :], start=(kc == 0), stop=(kc == 7))
        for kc in range(8):
            P.mm(pk[:, 0:T], wt[:, kc, 128:256], hT[:, kc, :], start=(kc == 0), stop=(kc == 7))
        yield
        l_ = lsc.next()
        P.act(l_, pz[:, 0:T], AF.Exp, scale=-1.0, bias=negb[:, h:h + 1])
        P.act(l_, l_, AF.Ln, bias=1.0)
        for s in range(NS):
            P.scan(cs[:, h, s * 128:(s + 1) * 128], ones_f, l_[:, s * 128:(s + 1) * 128], 0.0, ALU.mult, ALU.add)
        for s in range(NS):
            cols = slice(s * 128, (s + 1) * 128)
            cv = cs[:, h, cols]
            c2 = smallsB.next()
            P.ts("vector", c2[:, 0:1], cv[:, 64:65], 1.0 / 16, None, ALU.mult)
            P.ts("vector", c2[:, 1:2], cv[:, 64:65], -1.0 / 16, None, ALU.mult)
            P.ts("vector", c2[:, 2:3], cv[:, 127:128], -1.0 / 16, None, ALU.mult)
            eq = efac.next()
            P.act(eq, cv, AF.Exp, scale=-1.0 / 16, bias=c2[:, 0:1])
            ek = efac.next()
            P.act(ek, cv, AF.Exp, scale=1.0 / 16, bias=c2[:, 1:2])
            ed = efac.next()
            P.act(ed, cv, AF.Exp, scale=-1.0 / 16)
            ekd = efac.next()
            P.act(ekd, cv, AF.Exp, scale=1.0 / 16, bias=c2[:, 2:3])
            sc = float(LKD ** -0.5)
            P.stt("vector", glp[0][:, h, cols], pq[:, cols], sc, eq, ALU.mult, ALU.mult)
            P.tt("vector", glp[1][:, h, cols], pk[:, cols], ek, ALU.mult)
            P.stt("vector", glp[2][:, h, cols], pq[:, cols], sc, ed, ALU.mult, ALU.mult)
            P.tt("vector", glp[3][:, h, cols], pk[:, cols], ekd, ALU.mult)
            P.copy("gpsimd", gendl[:, s, h:h + 1], ed[:, 127:128])
        yield
        for s in range(NS):
            pv = psA.next()
            for kc in range(8):
                P.mm(pv[:, 0:256], hT[:, kc, s * 128:(s + 1) * 128], wt[:, kc, 256:512], start=(kc == 0), stop=(kc == 7))
            yield
            P.copy("scalar", vl[:, s, h * 256:(h + 1) * 256], pv[:, 0:256])
            yield

    gendl = P.sb("gendl", [128, NS, 4], F32)

    def gla_r(c, hT, wt):
        for s in range(NS):
            pb = psA.next()
            for kc in range(8):
                P.mm(pb, hT[:, kc, s * 128:(s + 1) * 128], wt[:, kc, :], start=(kc == 0), stop=(kc == 7))
            yield
            e = wide_f.next()
            P.act(e, pb, AF.Exp, scale=-1.0)
            P.act(e, e, AF.Ln, bias=1.0)
            P.act(e, e, AF.Exp, scale=-1.0)
            P.tt("vector", e, pb, e, ALU.mult)
            P.tt("gpsimd", rg[:, s, c * 512:(c + 1) * 512], e, lng[:, c * 512:(c + 1) * 512], ALU.mult)
            yield

    def gla_core(s, obT):
        c0 = s * 128
        cols = slice(c0, c0 + 128)
        fl = lambda t_: t_.re("p a b -> p (a b)")
        pa = psA.next()
        for h in range(4):
            P.mm(pa[:, h * 128:(h + 1) * 128], glp[1][:, h, cols], glp[0][:, h, cols])
        pbt = psA.next().bc(BF16)
        for h in range(4):
            P.tr(pbt[:, h * 128:(h + 1) * 128], glp[3][:, h, cols], ident_b)
        yield
        P.tt("vector", fl(ATm), pa, fl(m01t4), ALU.mult)
        P.copy("scalar", fl(kdl), pbt[:, 0:512])
        yield
        po = [psA.next(), psA.next()]
        for h in range(4):
            o_ = po[h // 2][:, (h % 2) * 256:(h % 2 + 1) * 256]
            P.mm(o_, glp[2][:, h, cols], Slb[:, h, :], start=True, stop=False)
            P.mm(o_, ATm[:, h, :], vl[:, s, h * 256:(h + 1) * 256], start=False, stop=True)
        yield
        ss4 = smallsB.next()
        for half in range(2):
            sq = wide_f.next()
            P.act(sq, po[half], AF.Square)
            P.reduce("vector", ss4[:, half * 2:half * 2 + 2], sq.re("p (a b) -> p a b", a=2), ALU.add)
        P.ts("vector", ss4[:, 0:4], ss4[:, 0:4], 1.0 / LVD, EPS, ALU.mult, ALU.add)
        P.act(ss4[:, 0:4], ss4[:, 0:4], AF.Ln)
        rs4 = smallsB.next()
        P.act(rs4[:, 0:4], ss4[:, 0:4], AF.Exp, scale=-0.5)
        for h in range(4):
            P.stt("vector", ob[:, h * 256:(h + 1) * 256], po[h // 2][:, (h % 2) * 256:(h % 2 + 1) * 256], rs4[:, h:h + 1],
                  rg[:, s, h * 256:(h + 1) * 256], ALU.mult, ALU.mult)
        yield
        pu = [psA.next(), psA.next()]
        for h in range(4):
            P.mm(pu[h // 2][:, (h % 2) * 256:(h % 2 + 1) * 256], kdl[:, h, :], vl[:, s, h * 256:(h + 1) * 256])
        yield
        for h in range(4):
            P.stt("vector", Sl[:, h, :], Sl[:, h, :], gendl[:, s, h:h + 1], pu[h // 2][:, (h % 2) * 256:(h % 2 + 1) * 256], ALU.mult, ALU.add)
        P.copy("scalar", fl(Slb), fl(Sl))
        yield
        tap("ob", ob, [128, 1024], BF16)
        to_T(ob, obT, c0)
        yield
        yield

    def gates_proj(i, hT, wt):
        dst = sa if i < 2 else sb_
        c = i % 2
        for s in range(NS):
            pb = psA.next()
            for kc in range(8):
                P.mm(pb, hT[:, kc, s * 128:(s + 1) * 128], wt[:, kc, :], start=(kc == 0), stop=(kc == 7))
            yield
            e = wide_f.next()
            P.act(e, pb, AF.Exp, scale=-1.0)
            P.act(e, e, AF.Ln, bias=1.0)
            P.act(dst[:, s, c * 512:(c + 1) * 512], e, AF.Exp, scale=-1.0)
            yield

    def stage_A(q_, st):
        t0 = st * T
        P.phase = "A_start"
        P.dma("sync", xt, x_d[q_, t0:t0 + T, :].rearrange("(s p) d -> p s d", p=128))
        hT = ttiles.next()
        for s in range(NS):
            norm_T(xt[:, s, :], hT, s * 128)
        tap("hT", hT, [128, 8, T], BF16)
        chk("A_norm")
        for s in range(NS):
            pb = psA.next()
            for kc in range(8):
                P.mm(pb[:, 0:32], hT[:, kc, s * 128:(s + 1) * 128], wsm[:, kc, :], start=(kc == 0), stop=(kc == 7))
            P.copy("scalar", gab[:, s, :], pb[:, 0:32])
        pb = psA.next()
        for kc in range(8):
            P.mm(pb[0:16, 0:T], wsm[:, kc, 16:32], hT[:, kc, :], start=(kc == 0), stop=(kc == 7))
        P.copy("scalar", lgT, pb[0:16, 0:T])
        chk("A_small")
        pend = []
        for h in range(8):
            wt = wget(B_GDN + h)
            cur = gdn_front(h, hT, wt)
            gdn_z(h, hT, wt)
            pend.append((h,) + cur)
            if len(pend) > 3:
                gdn_back(*pend.pop(0))
        while pend:
            gdn_back(*pend.pop(0))
        tap("qn0", qn[0], [128, 4, T], BF16); tap("kn0", kn[0], [128, 4, T], BF16); tap("vs0", vs[0], [128, 4, T], BF16)
        tap("zg", zg, [128, NS, 1024], BF16)
        chk("A_gdnproj")
        oaT = ttiles.next()
        obT = ttiles.next()
        def side_gen():
            for h in range(4):
                yield from gla_proj(h, hT, wget(B_GLA + h))
            for c in range(2):
                yield from gla_r(c, hT, wget(B_R + c))
            for s in range(NS):
                yield from gla_core(s, obT)
            for i in range(4):
                yield from gates_proj(i, hT, wget(B_GATE + i))

        side_it = side_gen()
        for s in range(NS):
            for g4 in range(2):
                for _ in gdn_unit(s, g4, oaT):
                    next(side_it, None)
        chk("A_gdncore")
        for _ in side_it:
            pass
        chk("A_glacore")
        return oaT, obT

    def resid_proj(srcT, s, blk0, after=None):
        pass

    def stage_B(q_, st, oaT, obT):
        t0 = st * T
        P.phase = "B_branch"
        for c in range(2):
            wt = wget(B_BG + c)
            for s in range(NS):
                pb = psA.next()
                for kc in range(8):
                    P.mm(pb, oaT[:, kc, s * 128:(s + 1) * 128], wt[:, kc, :], start=(kc == 0), stop=(kc == 7))
                P.tt("vector", yam[:, s, c * 512:(c + 1) * 512], pb, sa[:, s, c * 512:(c + 1) * 512], ALU.mult)
        mTt = ttiles.next()
        for c in range(2):
            wt = wget(B_BL + c)
            for s in range(NS):
                pb = psA.next()
                for kc in range(8):
                    P.mm(pb, obT[:, kc, s * 128:(s + 1) * 128], wt[:, kc, :], start=(kc == 0), stop=(kc == 7))
                m2 = wide_f.next()
                P.tt("vector", m2, pb, sb_[:, s, c * 512:(c + 1) * 512], ALU.mult)
                P.tt("gpsimd", tokb[s][:, c * 512:(c + 1) * 512], m2, yam[:, s, c * 512:(c + 1) * 512], ALU.add)
        for s in range(NS):
            to_T(tokb[s], mTt, s * 128)
        for c in range(2):
            wt = wget(B_OUT + c)
            for s in range(NS):
                pb = psA.next()
                for kc in range(8):
                    P.mm(pb, mTt[:, kc, s * 128:(s + 1) * 128], wt[:, kc, :], start=(kc == 0), stop=(kc == 7))
                P.tt("vector", xt[:, s, c * 512:(c + 1) * 512], pb, xt[:, s, c * 512:(c + 1) * 512], ALU.add)
        tap("x1", xt, [128, NS, 1024])
        chk("B_x1")
        h2T = ttiles.next()
        for s in range(NS):
            norm_T(xt[:, s, :], h2T, s * 128)
        qT = ttiles.next()
        for c2 in range(2):
            wt = wget(B_WQ + c2)
            for cc in range(4):
                pb = psA.next()
                for kc in range(8):
                    P.mm(pb[:, 0:T], wt[:, kc, cc * 128:(cc + 1) * 128], h2T[:, kc, :], start=(kc == 0), stop=(kc == 7))
                P.act(qT[:, c2 * 4 + cc, :], pb[:, 0:T], AF.Identity, scale=float(256 ** -0.5))
        oxT = ttiles.next()
        for s in range(NS):
            cols = slice(s * 128, (s + 1) * 128)
            psc = [psA.next(), psA.next()]
            for h in range(4):
                o_ = psc[h // 2][:, (h % 2) * 256:(h % 2 + 1) * 256]
                for c in range(2):
                    P.mm(o_, qT[:, 2 * h + c, cols], KT[:, 2 * h + c, :], start=(c == 0), stop=(c == 1))
            mx = smalls.next()
            for half in range(2):
                P.reduce("vector", mx[:, half * 2:half * 2 + 2], psc[half].re("p (a b) -> p a b", a=2), ALU.max)
            P.ts("vector", mx[:, 0:4], mx[:, 0:4], -1.0, None, ALU.mult)
            sm4 = smalls.next()
            for h in range(4):
                P.act(pt[:, h, :], psc[h // 2][:, (h % 2) * 256:(h % 2 + 1) * 256], AF.Exp, bias=mx[:, h:h + 1], accum_out=sm4[:, h:h + 1])
            rs4 = smalls.next()
            P.recip("vector", rs4[:, 0:4], sm4[:, 0:4])
            pbt = psA.next().bc(BF16)
            for h in range(4):
                for mc in range(2):
                    P.tr(pbt[:, (2 * h + mc) * 128:(2 * h + mc + 1) * 128], pt[:, h, mc * 128:(mc + 1) * 128], ident_b)
            P.copy("scalar", pT.re("p a b -> p (a b)"), pbt)
            pov = [psA.next(), psA.next()]
            for h in range(4):
                o_ = pov[h // 2][:, (h % 2) * 256:(h % 2 + 1) * 256]
                for mc in range(2):
                    P.mm(o_, pT[:, 2 * h + mc, :], Vt[:, mc, h * 256:(h + 1) * 256], start=(mc == 0), stop=(mc == 1))
            for h in range(4):
                P.ts("vector", ox[:, h * 256:(h + 1) * 256], pov[h // 2][:, (h % 2) * 256:(h % 2 + 1) * 256], rs4[:, h:h + 1], None, ALU.mult)
            to_T(ox, oxT, s * 128)
        for c in range(2):
            wt = wget(B_WO + c)
            for s in range(NS):
                pb = psA.next()
                for kc in range(8):
                    P.mm(pb, oxT[:, kc, s * 128:(s + 1) * 128], wt[:, kc, :], start=(kc == 0), stop=(kc == 7))
                P.tt("vector", xt[:, s, c * 512:(c + 1) * 512], pb, xt[:, s, c * 512:(c + 1) * 512], ALU.add)
        tap("x2", xt, [128, NS, 1024])
        chk("B_x2")
        P.phase = "B_mlp"
        h3T = ttiles.next()
        for s in range(NS):
            norm_T(xt[:, s, :], h3T, s * 128)
        acc = {}
        for fg in range(4):
            hq = hidq[fg % 2]
            for c2 in range(2):
                wt = wget(B_W1 + fg * 2 + c2)
                for cc in range(4):
                    fi = c2 * 4 + cc
                    pb = psA.next()
                    for kc in range(8):
                        P.mm(pb[:, 0:T], wt[:, kc, cc * 128:(cc + 1) * 128], h3T[:, kc, :], start=(kc == 0), stop=(kc == 7))
                    r_ = wide_f.next()
                    P.act(r_[:, 0:T], pb[:, 0:T], AF.Relu)
                    P.tt("vector" if fi % 2 == 0 else "gpsimd", hq[:, fi, :], r_[:, 0:T], r_[:, 0:T], ALU.mult)
            if fg == 0:
                for s in range(NS):
                    for c in range(2):
                        acc[(s, c)] = psA.next()
                        psA.hold(acc[(s, c)])
            for c in range(2):
                wt = wget(B_W2 + fg * 2 + c)
                for s in range(NS):
                    for kc in range(8):
                        P.mm(acc[(s, c)], hq[:, kc, s * 128:(s + 1) * 128], wt[:, kc, :],
                             start=(fg == 0 and kc == 0), stop=(fg == 3 and kc == 7))
        for s in range(NS):
            for c in range(2):
                P.tt("vector", xt[:, s, c * 512:(c + 1) * 512], acc[(s, c)], xt[:, s, c * 512:(c + 1) * 512], ALU.add)
                psA.release(acc[(s, c)])
        tap("x3", xt, [128, NS, 1024])
        chk("B_x3")
        P.phase = "B_final"
        for s in range(NS):
            rs = rstd_of(xt[:, s, :], 1024)
            P.stt("vector", xt[:, s, :], xt[:, s, :], rs, gfin, ALU.mult, ALU.mult)
        d = P.dma("sync", y_d[q_, t0:t0 + T, :].rearrange("(s p) d -> p s d", p=128), xt)
        P.final_deps.append(d)

    try:
      chk("prep")
      for q_ in range(nseq):
        P.memset("vector", Sg.re("p a b -> p (a b)"), 0.0)
        for g in range(2):
            P.memset("gpsimd", Sgb[g].re("p a b -> p (a b)"), 0.0)
        P.memset("vector", Sl.re("p a b -> p (a b)"), 0.0)
        P.memset("gpsimd", Slb.re("p a b -> p (a b)"), 0.0)
        P.memset("gpsimd", halo.re("p a b -> p (a b)"), 0.0)
        kv_prep(q_)
        chk("kv")
        for st in range(nst):
            oaT, obT = stage_A(q_, st)
            chk("A")
            stage_B(q_, st, oaT, obT)
    except _Stop:
        pass

    bes = ExitStack()
    finals = P.finalize(bes)
    bes.enter_context(nc.allow_low_precision(reason="bf16 matmul operands by design; fp32 accumulation"))
    block = bes.enter_context(nc.Block())
    P.emit(block, finals)
    bes.close()
    es.close()
    ninst = {e: len(P.ops[e]) for e in ENGS}
    return nc, tap_outs, dict(ninst=ninst, nwaits=P.nwaits, nsems=P.nsems, op_phase=P.op_phase)


N_CORES = 8


def kernel(**inputs):
    x = np.asarray(inputs["x"], dtype=np.float32)
    mem = np.asarray(inputs["mem"], dtype=np.float32)
    B, S, _ = x.shape
    nseq = B // N_CORES
    lay = host_layout(inputs)
    nc, _, _ = build(nseq, S)
    in_maps = []
    for c in range(N_CORES):
        m = dict(lay)
        m["x"] = np.ascontiguousarray(x[c * nseq:(c + 1) * nseq])
        m["mem"] = np.ascontiguousarray(mem[c * nseq:(c + 1) * nseq])
        in_maps.append(m)
    res = run_bass_kernel_spmd(nc, in_maps, core_ids=list(range(N_CORES)))
    out = np.concatenate([np.asarray(r["y"], dtype=np.float32) for r in res.results], axis=0)
    return out
```

```python
from contextlib import ExitStack
from concourse.bass_utils import run_bass_kernel_spmd
import numpy as np
import concourse.bass as bass
import concourse.mybir as mybir

F32 = mybir.dt.float32
BF16 = mybir.dt.bfloat16
AF = mybir.ActivationFunctionType
ALU = mybir.AluOpType
AX = mybir.AxisListType

ENGS = ("sync", "scalar", "vector", "gpsimd", "tensor")


class Buf:
    __slots__ = ("name", "last_w", "readers", "excl")

    def __init__(self, name):
        self.name = name
        self.last_w = None
        self.readers = []
        self.excl = False


class V:
    __slots__ = ("ap", "buf", "dram")

    def __init__(self, ap, buf, dram=False):
        self.ap = ap
        self.buf = buf
        self.dram = dram

    def __getitem__(self, idx):
        return V(self.ap[idx], self.buf, self.dram)

    def re(self, pat, **kw):
        return V(self.ap.rearrange(pat, **kw), self.buf, self.dram)

    def bc(self, dt):
        return V(self.ap.bitcast(dt), self.buf, self.dram)


class Tile(V):
    def __init__(self, t, name):
        V.__init__(self, t[:], Buf(name))
        self.t = t


class Op:
    __slots__ = ("eng", "fn", "deps", "is_dma", "sem", "sigval", "signal", "idx", "waits")

    def __init__(self, eng, fn, deps, is_dma):
        self.eng = eng
        self.fn = fn
        self.deps = deps
        self.is_dma = is_dma
        self.sem = None
        self.sigval = 0
        self.signal = False
        self.waits = []


class Prog:
    def __init__(self, nc, es, same_engine_sync=False):
        self.nc = nc
        self.es = es
        self.ops = {e: [] for e in ENGS}
        self.same_engine_sync = same_engine_sync
        self.dma_sems = {}
        self.final_deps = []
        self.nsb = 0
        self.phase = ""
        self.op_phase = {e: [] for e in ENGS}

    def sb(self, name, shape, dt):
        t = self.es.enter_context(self.nc.sbuf_tensor("sb_" + name, list(shape), dt))
        return Tile(t, name)

    def dram(self, ap, name):
        return V(ap, Buf(name), True)

    def ps(self, name, shape, dt):
        t = self.es.enter_context(self.nc.psum_tensor("ps_" + name, list(shape), dt))
        tl = Tile(t, name)
        tl.buf.excl = True
        return tl

    def rec(self, eng, fn, reads=(), writes=(), is_dma=False, dma_key=None):
        deps = set()
        rb = [v.buf for v in reads if isinstance(v, V)]
        wb = [v.buf for v in writes if isinstance(v, V)]
        for b in rb:
            if b.last_w is not None:
                deps.add(b.last_w)
            if b.excl:
                for r in b.readers:
                    if r.eng != eng:
                        deps.add(r)
        for b in wb:
            lw = b.last_w
            if lw is not None and (lw.eng != eng or lw.is_dma or is_dma):
                deps.add(lw)
            for r in b.readers:
                if r.eng != eng or r.is_dma or is_dma:
                    deps.add(r)
        op = Op(eng, fn, deps, is_dma)
        if is_dma:
            op.sem = dma_key
        for b in wb:
            b.last_w = op
            b.readers = []
        for b in rb:
            if b not in wb:
                b.readers.append(op)
        op.idx = len(self.ops[eng])
        self.ops[eng].append(op)
        self.op_phase[eng].append(self.phase)
        return op

    def finalize(self, block_es):
        nc = self.nc
        for e in ENGS:
            for op in self.ops[e]:
                if op.is_dma:
                    op.signal = True
                for d in op.deps:
                    if d.is_dma:
                        continue
                    d.signal = True
        last = {}
        for e in ENGS:
            for op in self.ops[e]:
                if op.is_dma:
                    last[op.sem] = op
        for op in last.values():
            if op not in self.final_deps:
                self.final_deps.append(op)
        for d in self.final_deps:
            d.signal = True
        eng_sem = {e: block_es.enter_context(nc.semaphore("sem_" + e)) for e in ENGS}
        dma_sem = {}
        for e in ENGS:
            cnt = 0
            for op in self.ops[e]:
                if op.is_dma:
                    key = op.sem
                    if key not in dma_sem:
                        dma_sem[key] = [block_es.enter_context(nc.semaphore("dsem_%d" % len(dma_sem))), 0]
                    ent = dma_sem[key]
                    ent[1] += 16
                    op.sem = ent[0]
                    op.sigval = ent[1]
                elif op.signal:
                    cnt += 1
                    op.sem = eng_sem[e]
                    op.sigval = cnt
        nwaits = 0
        for e in ENGS:
            seen = {}
            for op in self.ops[e]:
                need = {}
                for d in op.deps:
                    if d is op:
                        continue
                    k = id(d.sem)
                    if seen.get(k, 0) >= d.sigval:
                        continue
                    if k not in need or need[k][1] < d.sigval:
                        need[k] = (d.sem, d.sigval)
                for k, (s, v) in need.items():
                    seen[k] = v
                    op.waits.append((s, v))
                    nwaits += 1
        self.nwaits = nwaits
        self.nsems = len(dma_sem) + len(ENGS)
        finals = [(d.sem, d.sigval) for d in self.final_deps]
        return finals

    def emit(self, block, finals, final_eng="gpsimd"):
        P = self

        def run(ename, e):
            for op in P.ops[ename]:
                for (s, v) in op.waits:
                    e.wait_ge(s, v)
                ins = op.fn(e)
                if op.signal:
                    ins.then_inc(op.sem, 16 if op.is_dma else 1)
            if ename == final_eng:
                for (s, v) in finals:
                    e.wait_ge(s, v)

        @block.sync
        def _(e):
            run("sync", e)

        @block.scalar
        def _(e):
            run("scalar", e)

        @block.vector
        def _(e):
            run("vector", e)

        @block.gpsimd
        def _(e):
            run("gpsimd", e)

        @block.tensor
        def _(e):
            run("tensor", e)

    def dma(self, eng, out, in_, key=None):
        k = key
        if k is None:
            if isinstance(out, V) and not out.dram:
                k = out.buf
            elif isinstance(in_, V) and not in_.dram:
                k = in_.buf
            else:
                k = out.buf if isinstance(out, V) else in_.buf
        o = out.ap if isinstance(out, V) else out
        i = in_.ap if isinstance(in_, V) else in_
        return self.rec(eng, lambda e: e.dma_start(out=o, in_=i),
                        reads=[in_], writes=[out], is_dma=True, dma_key=k)

    def mm(self, out, lhsT, rhs, start=True, stop=True):
        return self.rec("tensor", lambda e: e.matmul(out.ap, lhsT.ap, rhs.ap, start=start, stop=stop),
                        reads=[lhsT, rhs], writes=[out])

    def tr(self, out, in_, ident):
        return self.rec("tensor", lambda e: e.transpose(out.ap, in_.ap, ident.ap),
                        reads=[in_, ident], writes=[out])

    def act(self, out, in_, func, bias=None, scale=None, accum_out=None, eng="scalar"):
        kw = {}
        reads = [in_]
        if bias is not None:
            kw["bias"] = bias.ap if isinstance(bias, V) else bias
            reads.append(bias)
        if scale is not None:
            kw["scale"] = scale.ap if isinstance(scale, V) else scale
            reads.append(scale)
        writes = [out]
        if accum_out is not None:
            kw["accum_out"] = accum_out.ap
            writes.append(accum_out)
        return self.rec(eng, lambda e: e.activation(out.ap, in_.ap, func, **kw), reads=reads, writes=writes)

    def ts(self, eng, out, in0, s1, s2, op0, op1=None, accum_out=None):
        reads = [in0, s1, s2]
        a1 = s1.ap if isinstance(s1, V) else s1
        a2 = s2.ap if isinstance(s2, V) else s2
        kw = {}
        writes = [out]
        if op1 is not None:
            kw["op1"] = op1
        if accum_out is not None:
            kw["accum_out"] = accum_out.ap
            writes.append(accum_out)
        return self.rec(eng, lambda e: e.tensor_scalar(out=out.ap, in0=in0.ap, scalar1=a1, scalar2=a2, op0=op0, **kw),
                        reads=reads, writes=writes)

    def stt(self, eng, out, in0, scalar, in1, op0, op1):
        sc = scalar.ap if isinstance(scalar, V) else scalar
        eng = "vector"
        return self.rec(eng, lambda e: e.scalar_tensor_tensor(out=out.ap, in0=in0.ap, scalar=sc, in1=in1.ap, op0=op0, op1=op1),
                        reads=[in0, scalar, in1], writes=[out])

    def tt(self, eng, out, in0, in1, op):
        return self.rec(eng, lambda e: e.tensor_tensor(out=out.ap, in0=in0.ap, in1=in1.ap, op=op),
                        reads=[in0, in1], writes=[out])

    def copy(self, eng, out, in_):
        if eng == "scalar":
            return self.rec(eng, lambda e: e.copy(out=out.ap, in_=in_.ap), reads=[in_], writes=[out])
        return self.rec(eng, lambda e: e.tensor_copy(out=out.ap, in_=in_.ap), reads=[in_], writes=[out])

    def reduce(self, eng, out, in_, op, axis=AX.X):
        return self.rec(eng, lambda e: e.tensor_reduce(out=out.ap, in_=in_.ap, axis=axis, op=op),
                        reads=[in_], writes=[out])

    def recip(self, eng, out, in_):
        return self.rec(eng, lambda e: e.reciprocal(out=out.ap, in_=in_.ap), reads=[in_], writes=[out])

    def scan(self, out, d0, d1, initial, op0, op1):
        return self.rec("vector", lambda e: e.tensor_tensor_scan(out=out.ap, data0=d0.ap, data1=d1.ap, initial=initial, op0=op0, op1=op1),
                        reads=[d0, d1], writes=[out])

    def memset(self, eng, out, val):
        return self.rec(eng, lambda e: e.memset(out.ap, val), reads=[], writes=[out])


D = 1024
NH = 8
HD = 128
LH = 4
LKD = 128
LVD = 256
NMEM = 256
DFF = 4096
NS = 2
T = NS * 128
EPS = 1e-6
NBLK = 48
BIG = 1.0e30
DGE_SCRATCH = 1024

B_GDN = 0
B_GLA = 8
B_R = 12
B_GATE = 14
B_BG = 18
B_BL = 20
B_OUT = 22
B_WQ = 24
B_WO = 26
B_W1 = 28
B_W2 = 36
B_WK = 44
B_WV = 46

IN_SIZES = (1024, 1024, 1024, 1024, 8, 8, 512, 512, 1024, 1024, 16, 1024, 1024)


def host_layout(inp):
    f = lambda a: np.ascontiguousarray(np.asarray(a, dtype=np.float32))
    w_in = f(inp["w_in"][0])
    offs = np.cumsum((0,) + IN_SIZES)
    gq, gk, gv, gz, ga, gb, lq, lk, lv, lr, lgate, gate_a, gate_b = [w_in[:, offs[i]:offs[i + 1]] for i in range(13)]
    blocks = []

    def blk(mat):
        assert mat.shape == (1024, 512), mat.shape
        return mat.reshape(8, 128, 512).transpose(1, 0, 2)

    for h in range(8):
        s = slice(h * 128, (h + 1) * 128)
        blocks.append(blk(np.concatenate([gq[:, s], gk[:, s], gv[:, s], gz[:, s]], axis=1)))
    for h in range(4):
        blocks.append(blk(np.concatenate([lq[:, h * 128:(h + 1) * 128], lk[:, h * 128:(h + 1) * 128],
                                          lv[:, h * 256:(h + 1) * 256]], axis=1)))
    for c in range(2):
        blocks.append(blk(lr[:, c * 512:(c + 1) * 512]))
    for g in (gate_a, gate_b):
        for c in range(2):
            blocks.append(blk(g[:, c * 512:(c + 1) * 512]))
    for name in ("w_branch_gdn", "w_branch_gla", "w_out", "xattn_wq", "xattn_wo"):
        w = f(inp[name][0])
        for c in range(2):
            blocks.append(blk(w[:, c * 512:(c + 1) * 512]))
    w1 = f(inp["mlp_w1"][0])
    for c in range(8):
        blocks.append(blk(w1[:, c * 512:(c + 1) * 512]))
    w2 = f(inp["mlp_w2"][0])
    for fg in range(4):
        for c in range(2):
            blocks.append(blk(w2[fg * 1024:(fg + 1) * 1024, c * 512:(c + 1) * 512]))
    for name in ("xattn_wk", "xattn_wv"):
        w = f(inp[name][0])
        for c in range(2):
            blocks.append(blk(w[:, c * 512:(c + 1) * 512]))
    wblk = np.ascontiguousarray(np.stack(blocks, 0)).reshape(NBLK, 128, 4096)
    wsm = np.ascontiguousarray(np.concatenate([ga, gb, lgate], axis=1).reshape(8, 128, 32).transpose(1, 0, 2)).reshape(128, 256)

    def gcol(g):
        return np.ascontiguousarray(f(g).reshape(8, 128).T)

    rep = lambda v: np.ascontiguousarray(np.broadcast_to(f(v).reshape(1, -1), (128, f(v).size)))
    gcols = np.concatenate([gcol(inp["norm_mix_g"][0]), gcol(inp["norm_xattn_g"][0]),
                            gcol(inp["norm_mlp_g"][0]), gcol(inp["norm_mem_g"][0])], axis=1)
    cwt = f(inp["gdn_conv_w"][0])
    cw = np.ascontiguousarray(cwt.reshape(4, 24, 128).transpose(2, 1, 0)).reshape(128, 96)
    small = np.concatenate([
        gcols,
        cw,
        rep(inp["gdn_a_log"][0]),
        rep(inp["gdn_dt_bias"][0]),
        np.ascontiguousarray(f(inp["gla_b_gate"][0]).reshape(4, 128).T),
    ], axis=1)
    small = np.ascontiguousarray(small)
    wide = np.concatenate([
        rep(inp["norm_final_g"]),
        rep(np.tile(f(inp["gdn_norm_g"][0]), 8)),
        rep(np.tile(f(inp["gla_norm_g"][0]), 4)),
    ], axis=1)
    wide = np.ascontiguousarray(wide)
    wg2 = f(inp["gla_w_gate2"][0])
    p = np.arange(128)[:, None]
    q = np.arange(128)[None, :]
    ident = (p == q).astype(np.float32)
    tri = (p <= q).astype(np.float32)
    same32 = (p // 32) == (q // 32)
    pm_d = np.where((p > q) & same32, 0.0, BIG).astype(np.float32)
    pm_r = np.where((p > q) & (~same32), 0.0, BIG).astype(np.float32)
    nm_t = np.where(q >= p, 0.0, -BIG).astype(np.float32)
    m01t = (q >= p).astype(np.float32)
    ones = np.ones((128, 128), np.float32)
    consts = np.ascontiguousarray(np.concatenate([ident, tri, pm_d, pm_r, nm_t, m01t, ones], axis=1))
    return dict(wblk=wblk, wsm=wsm, small=small, wide=wide, wg2=wg2, consts=consts)


class RR:
    def __init__(self, items):
        self.items = items
        self.i = 0
        self.held = set()

    def next(self):
        for _ in range(len(self.items) + 1):
            k = self.i % len(self.items)
            self.i += 1
            if k not in self.held:
                return self.items[k]
        raise RuntimeError("all held")

    def hold(self, it):
        self.held.add(self.items.index(it))

    def release(self, it):
        self.held.discard(self.items.index(it))


class _Stop(Exception):
    pass


def build(nseq, S, taps=None, stop=None):
    assert S % T == 0
    nst = S // T
    nc = bass.Bass("TRN2", target_bir_lowering=False, dynamic_dma_scratch_size=DGE_SCRATCH)
    dr = lambda name, shape, dt=F32, kind="ExternalInput": nc.dram_tensor(name, list(shape), dt, kind=kind).ap()
    x_d = dr("x", [nseq, S, D])
    mem_d = dr("mem", [nseq, NMEM, D])
    wblk_d = dr("wblk", [NBLK, 128, 4096])
    wsm_d = dr("wsm", [128, 256])
    small_d = dr("small", [128, 148])
    wide_d = dr("wide", [128, 3072])
    wg2_d = dr("wg2", [16, 512])
    consts_d = dr("consts", [128, 7 * 128])
    y_d = dr("y", [nseq, S, D], F32, "ExternalOutput")
    wbf_ap = dr("wbf", [NBLK, 128, 4096], BF16, "Internal")
    tap_outs = {}

    es = ExitStack()
    P = Prog(nc, es)
    wbf = [P.dram(wbf_ap[b], "wbf%d" % b) for b in range(NBLK)]

    def chk(name):
        P.phase = "after_" + name
        if stop == name:
            raise _Stop()

    def tap(name, v, shape, dt=F32):
        if taps is None or name not in taps or name in tap_outs:
            return
        o = dr("tap_" + name, shape, dt, "ExternalOutput")
        tap_outs[name] = o
        stg = P.sb("tapstg_" + name, shape, dt)
        P.copy("vector", stg, v)
        d = P.dma("sync", o, stg)
        P.final_deps.append(d)

    cst = P.sb("cst", [128, 7 * 128], F32)
    ident_f = cst[:, 0:128]
    tri_f = cst[:, 128:256]
    pm_d = cst[:, 256:384]
    pm_r = cst[:, 384:512]
    nm_t = cst[:, 512:640]
    m01_f = cst[:, 640:768]
    ones_f = cst[:, 768:896]
    ident_b = P.sb("ident_b", [128, 128], BF16)
    ident4_b = P.sb("ident4_b", [128, 4, 128], BF16)
    ones_b = P.sb("ones_b", [128, 128], BF16)
    m01t4 = P.sb("m01t4", [128, 4, 128], BF16)
    small = P.sb("small", [128, 148], F32)
    gcols = small[:, 0:32]
    cw = small[:, 32:128]
    alog = small[:, 128:136]
    dtb = small[:, 136:144]
    bgate = small[:, 144:148]
    negA = P.sb("negA", [128, 8], F32)
    negb = P.sb("negb", [128, 4], F32)
    gfin = P.sb("gfin", [128, 1024], F32)
    wide_b = P.sb("wide_b", [128, 2048], BF16)
    gng = wide_b[:, 0:1024]
    lng = wide_b[:, 1024:2048]
    wsm = P.sb("wsm", [128, 8, 32], BF16)
    wg2 = P.sb("wg2", [16, 512], BF16)

    Sg = P.sb("Sg", [128, 8, 128], F32)
    Sgb = [P.sb("Sgb%d" % g, [128, 4, 128], BF16) for g in range(2)]
    Sl = P.sb("Sl", [128, 4, 256], F32)
    Slb = P.sb("Slb", [128, 4, 256], BF16)
    halo = P.sb("halo", [128, 24, 3], F32)
    KT = P.sb("KT", [128, 8, 256], BF16)
    Vt = P.sb("Vt", [128, 2, 1024], BF16)

    NSLOT = 3
    slots = [P.sb("wslot%d" % i, [128, 8, 512], BF16) for i in range(NSLOT)]
    xt = P.sb("xt", [128, NS, 1024], F32)
    ttiles = RR([P.sb("tt%d" % i, [128, 8, T], BF16) for i in range(4)])
    banks = [P.ps("bank%d" % i, [128, 512], F32) for i in range(8)]
    psA = RR(banks)

    hb = P.sb("hb", [128, 1024], BF16)
    smalls = RR([P.sb("sm%d" % i, [128, 8], F32) for i in range(24)])

    raws = RR([P.sb("raw%d" % i, [128, 3 + T], F32) for i in range(6)])
    convy = RR([P.sb("convy%d" % i, [128, T], F32) for i in range(6)])
    sc_q = RR([P.sb("scq%d" % i, [128, T], BF16) for i in range(2)])
    qn = [P.sb("qn%d" % g, [128, 4, T], BF16) for g in range(2)]
    kn = [P.sb("kn%d" % g, [128, 4, T], BF16) for g in range(2)]
    vs = [P.sb("vs%d" % g, [128, 4, T], BF16) for g in range(2)]
    zg = P.sb("zg", [128, NS, 1024], BF16)
    gab = P.sb("gab", [128, NS, 32], F32)
    lgT = P.sb("lgT", [16, T], BF16)
    vl = P.sb("vl", [128, NS, 1024], BF16)
    rg = P.sb("rg", [128, NS, 1024], BF16)
    sa = P.sb("sa", [128, NS, 1024], BF16)
    sb_ = P.sb("sb_", [128, NS, 1024], BF16)
    wide_f = RR([P.sb("widef%d" % i, [128, 512], F32) for i in range(2)])

    GT = P.sb("GT", [128, 4, 128], F32)
    Gs = P.sb("Gs", [128, 4, 128], F32)
    EG = P.sb("EG", [128, 4, 128], BF16)
    args = RR([P.sb("arg%d" % i, [128, 4, 128], F32) for i in range(2)])
    mk = lambda n: P.sb(n, [128, 4, 128], BF16)
    Fd, Fr, Dt = mk("Fd"), mk("Fr"), mk("Dt")
    Ld, Rr, AqkT, LdT, qdT, Rw, Ru, kdec = [mk(n) for n in ("Ld", "Rr", "AqkT", "LdT", "qdT", "Rw", "Ru", "kdec")]
    Mp = [mk("Mp0"), mk("Mp1")]
    MTp = [mk("MTp0"), mk("MTp1")]
    Yp = [mk("Yp0"), mk("Yp1")]
    Dp = [mk("Dp0"), mk("Dp1")]
    Zt, wTn, dlt = [mk(n) for n in ("Zt", "wTn", "dlt")]
    Wn, Nb, T1 = Dt, Fr, Fd
    tokb = [P.sb("tokb%d" % i, [128, 1024], BF16) for i in range(2)]
    oa = tokb[0]

    cs = P.sb("cs", [128, 4, T], F32)
    glp = [P.sb("glp%d" % i, [128, 4, T], BF16) for i in range(4)]
    smallsB = RR([P.sb("smB%d" % i, [128, 8], F32) for i in range(12)])
    lsc = RR([P.sb("lsc%d" % i, [128, T], F32) for i in range(2)])
    efac = RR([P.sb("efac%d" % i, [128, 128], F32) for i in range(4)])
    ATm = mk("ATm")
    kdl = mk("kdl")
    ob = tokb[1]

    pt = vl[:, 0, :].re("p (a b) -> p a b", a=4)
    pT = vl[:, 1, :].re("p (a b) -> p a b", a=8)
    ox = tokb[1]
    hidq = [P.sb("hidq%d" % i, [128, 8, T], BF16) for i in range(2)]
    yam = zg


    if taps is not None and "probe" in taps:
        for kb in range(64, 0, -1):
            try:
                es2 = ExitStack()
                es2.enter_context(nc.sbuf_tensor("probe%d" % kb, [128, kb * 256], F32))
                print("SBUF slack >= %d KB" % kb)
                es2.close()
                break
            except Exception as ex:
                pass
    order = []
    for q_ in range(nseq):
        order += [B_WK, B_WK + 1, B_WV, B_WV + 1]
        for st in range(nst):
            order += list(range(0, 28)) + [28, 29, 36, 37, 30, 31, 38, 39, 32, 33, 40, 41, 34, 35, 42, 43]
    wstate = {"issued": 0, "cur": 0}

    def wissue():
        i = wstate["issued"]
        if i < len(order):
            prep_block(order[i])
            P.dma("sync", slots[i % NSLOT].re("p a b -> p (a b)"), wbf[order[i]])
            wstate["issued"] += 1

    def wget(expect):
        c = wstate["cur"]
        assert order[c] == expect, (c, order[c], expect)
        while wstate["issued"] <= min(c + NSLOT - 1, len(order) - 1):
            wissue()
        wstate["cur"] += 1
        return slots[c % NSLOT]

    P.dma("sync", cst, consts_d)
    P.dma("sync", small, small_d)
    P.dma("sync", gfin, wide_d[:, 0:1024])
    P.copy("vector", ident_b, ident_f)
    P.copy("vector", ones_b, ones_f)
    for h in range(4):
        P.copy("vector", ident4_b[:, h, :], ident_f)
        P.copy("vector", m01t4[:, h, :], m01_f)
    P.dma("sync", wide_f.items[1][0:16, :], wg2_d)
    P.copy("vector", wg2, wide_f.items[1][0:16, :])
    P.act(negA, alog, AF.Exp)
    P.ts("vector", negA, negA, -1.0, None, ALU.mult)
    P.ts("vector", negb, bgate, -1.0, None, ALU.mult)
    for c in range(4):
        sf_ = wide_f.items[c % 2]
        P.dma("sync", sf_, wide_d[:, 1024 + c * 512:1024 + (c + 1) * 512])
        P.copy("vector", wide_b[:, c * 512:(c + 1) * 512], sf_)
    gain_of = {}
    for b in range(0, 18):
        gain_of[b] = 0
    gain_of[B_WQ] = gain_of[B_WQ + 1] = 1
    for b in range(B_W1, B_W1 + 8):
        gain_of[b] = 2
    for b in range(B_WK, B_WK + 4):
        gain_of[b] = 3
    NPST = 3
    pstf = [P.sb("pstf%d" % i, [128, 1024], F32) for i in range(NPST)]
    pstb = [P.sb("pstb%d" % i, [128, 1024], BF16) for i in range(NPST)]
    pstate = {"qi": 0, "done": set()}

    def prep_block(b):
        if b in pstate["done"]:
            return
        pstate["done"].add(b)
        gi = gain_of.get(b, None)
        for q4 in range(4):
            qi = pstate["qi"]
            sf = pstf[qi % NPST]
            sbf = pstb[qi % NPST]
            P.dma("sync", sf, wblk_d[b][:, q4 * 1024:(q4 + 1) * 1024])
            for k2 in range(2):
                kc = q4 * 2 + k2
                eng = "vector" if (qi + k2) % 2 == 0 else "scalar"
                o_, i_ = sbf[:, k2 * 512:(k2 + 1) * 512], sf[:, k2 * 512:(k2 + 1) * 512]
                if gi is None:
                    P.copy(eng, o_, i_)
                elif eng == "scalar":
                    P.act(o_, i_, AF.Identity, scale=gcols[:, gi * 8 + kc:gi * 8 + kc + 1])
                else:
                    P.ts(eng, o_, i_, gcols[:, gi * 8 + kc:gi * 8 + kc + 1], None, ALU.mult)
            P.dma("sync", wbf[b][:, q4 * 1024:(q4 + 1) * 1024], sbf)
            pstate["qi"] += 1

    stf = [wide_f.items[0]]
    sf = stf[0]
    P.dma("sync", sf[:, 0:256], wsm_d)
    for kc in range(8):
        P.ts("vector", wsm[:, kc, :], sf[:, kc * 32:(kc + 1) * 32], gcols[:, kc:kc + 1], None, ALU.mult)

    def rstd_of(xrow, ncols):
        ss = smalls.next()
        P.act(hb[:, 0:ncols], xrow, AF.Square, accum_out=ss[:, 0:1])
        rs = smalls.next()
        P.ts("vector", rs[:, 0:1], ss[:, 0:1], 1.0 / ncols, EPS, ALU.mult, ALU.add)
        P.act(rs[:, 1:2], rs[:, 0:1], AF.Ln)
        P.act(rs[:, 2:3], rs[:, 1:2], AF.Exp, scale=-0.5)
        return rs[:, 2:3]

    def to_T(src_b, dstT, col0, ncol=128, evac="scalar"):
        pb = psA.next().bc(BF16)
        for c in range(8):
            P.tr(pb[:, c * 128:(c + 1) * 128], src_b[:, c * 128:(c + 1) * 128], ident_b)
        P.copy(evac, dstT[:, :, col0:col0 + 128], pb.re("p (a b) -> p a b", a=8))

    def norm_T(xrow, dstT, col0):
        rs = rstd_of(xrow, 1024)
        P.ts("vector", hb, xrow, rs, None, ALU.mult)
        to_T(hb, dstT, col0)

    def kv_prep(q_):
        mT = ttiles.next()
        for mc in range(2):
            P.dma("sync", xt[:, 0, :], mem_d[q_, mc * 128:(mc + 1) * 128, :])
            norm_T(xt[:, 0, :], mT, mc * 128)
        for c2 in range(2):
            wt = wget(B_WK + c2)
            for cc in range(4):
                pb = psA.next()
                for kc in range(8):
                    P.mm(pb[:, 0:256], wt[:, kc, cc * 128:(cc + 1) * 128], mT[:, kc, 0:256], start=(kc == 0), stop=(kc == 7))
                P.copy("scalar", KT[:, c2 * 4 + cc, :], pb[:, 0:256])
        for c2 in range(2):
            wt = wget(B_WV + c2)
            for mc in range(2):
                pb = psA.next()
                for kc in range(8):
                    P.mm(pb, mT[:, kc, mc * 128:(mc + 1) * 128], wt[:, kc, :], start=(kc == 0), stop=(kc == 7))
                P.copy("scalar", Vt[:, mc, c2 * 512:(c2 + 1) * 512], pb)

    def gdn_front(h, hT, wt):
        pbs, rws, ys = [], [], []
        for j in range(3):
            pb = psA.next()
            for kc in range(8):
                P.mm(pb[:, 0:T], wt[:, kc, j * 128:(j + 1) * 128], hT[:, kc, :], start=(kc == 0), stop=(kc == 7))
            pbs.append(pb)
        for j in range(3):
            g = j * 8 + h
            raw = raws.next()
            P.copy("gpsimd", raw[:, 0:3], halo[:, g, :])
            P.copy("scalar", raw[:, 3:3 + T], pbs[j][:, 0:T])
            P.copy("gpsimd", halo[:, g, :], raw[:, T:T + 3])
            rws.append(raw)
        for j in range(3):
            g = j * 8 + h
            y = convy.next()
            P.ts("vector", y, rws[j][:, 3:3 + T], cw[:, g * 4 + 3:g * 4 + 4], None, ALU.mult)
            ys.append(y)
        for jj in (2, 1, 0):
            for j in range(3):
                g = j * 8 + h
                P.stt("vector", ys[j], rws[j][:, jj:jj + T], cw[:, g * 4 + jj:g * 4 + jj + 1], ys[j], ALU.mult, ALU.add)
        return ys, rws

    def gdn_back(h, ys, rws):
        g4, hh = divmod(h, 4)
        es = [rws[j][:, 3:3 + T] for j in range(3)]
        for j in range(3):
            P.act(es[j], ys[j], AF.Exp, scale=-1.0)
        for j in range(3):
            P.act(es[j], es[j], AF.Ln, bias=1.0)
        for j in range(3):
            P.act(es[j], es[j], AF.Exp, scale=-1.0)
        for j in range(2):
            P.tt("vector" if j == 0 else "gpsimd", ys[j], ys[j], es[j], ALU.mult)
        P.tt("gpsimd", vs[g4][:, hh, :], ys[2], es[2], ALU.mult)
        pns = []
        for j in range(2):
            sq = sc_q.next()
            P.tt("gpsimd", sq, ys[j], ys[j], ALU.mult)
            pn = psA.next()
            P.mm(pn[:, 0:T], ones_b, sq)
            pns.append(pn)
        for j in range(2):
            P.act(es[j], pns[j][:, 0:T], AF.Ln, bias=EPS)
        for j in range(2):
            if j == 0:
                P.act(es[j], es[j], AF.Exp, scale=-0.5, bias=float(np.log(HD ** -0.5)))
            else:
                P.act(es[j], es[j], AF.Exp, scale=-0.5)
        for j in range(2):
            dst = (qn if j == 0 else kn)[g4][:, hh, :]
            P.tt("vector", dst, ys[j], es[j], ALU.mult)

    zbank = {}

    def gdn_z(h, hT, wt):
        g4, hh = divmod(h, 4)
        for s in range(NS):
            if hh == 0:
                zbank[s] = psA.next()
                psA.hold(zbank[s])
            pb = zbank[s]
            for kc in range(8):
                P.mm(pb[:, hh * 128:(hh + 1) * 128], hT[:, kc, s * 128:(s + 1) * 128], wt[:, kc, 384:512], start=(kc == 0), stop=(kc == 7))
            if hh == 3:
                e = wide_f.next()
                P.act(e, pb, AF.Exp, scale=-1.0)
                P.act(e, e, AF.Ln, bias=1.0)
                P.act(e, e, AF.Exp, scale=-1.0)
                P.tt("vector", e, pb, e, ALU.mult)
                P.tt("gpsimd", zg[:, s, g4 * 512:(g4 + 1) * 512], e, gng[:, g4 * 512:(g4 + 1) * 512], ALU.mult)
                psA.release(pb)

    dsc = {}

    def gdn_scalars(s):
        t8 = smalls.next()
        P.tt("vector", t8, gab[:, s, 0:8], dtb, ALU.add)
        P.act(t8, t8, AF.Exp)
        sp8 = smalls.next()
        P.act(sp8, t8, AF.Ln, bias=1.0)
        g8 = smalls.next()
        P.tt("vector", g8, sp8, negA, ALU.mult)
        eb8 = smalls.next()
        P.act(eb8, gab[:, s, 8:16], AF.Exp, scale=-1.0)
        lb8 = smalls.next()
        P.act(lb8, eb8, AF.Ln, bias=1.0)
        pb = psA.next()
        P.mm(pb[:, 0:8], tri_f, g8)
        gc8 = smalls.next()
        P.copy("vector", gc8, pb[:, 0:8])
        gcb8 = smalls.next()
        P.tt("vector", gcb8, gc8, lb8, ALU.subtract)
        beta8 = smalls.next()
        P.act(beta8, lb8, AF.Exp, scale=-1.0)
        bg8 = smalls.next()
        P.act(bg8, gcb8, AF.Exp)
        tap("g8", g8, [128, 8]); tap("gc8", gc8, [128, 8]); tap("beta8", beta8, [128, 8])
        chk("G1")
        dsc[s] = (g8, gc8, gcb8, beta8, bg8)

    def gdn_unit(s, g4, oaT):
        c0 = s * 128
        cols = slice(c0, c0 + 128)
        if g4 == 0:
            gdn_scalars(s)
        g8, gc8, gcb8, beta8, bg8 = dsc[s]
        hs = [g4 * 4 + i for i in range(4)]
        for hh, h in enumerate(hs):
            P.act(GT[:, hh, :], tri_f, AF.Identity, scale=g8[:, h:h + 1])
        pb = psA.next()
        P.mm(pb, ones_f, GT.re("p a b -> p (a b)"))
        yield
        P.copy("scalar", Gs.re("p a b -> p (a b)"), pb)
        P.act(EG.re("p a b -> p (a b)"), Gs.re("p a b -> p (a b)"), AF.Exp)
        gl4 = smalls.next()
        P.copy("vector", gl4[:, 0:4], Gs[:, :, 127])
        gend4 = smalls.next()
        P.act(gend4[:, 0:4], gl4[:, 0:4], AF.Exp)
        ekd4 = smalls.next()
        P.tt("vector", ekd4[:, 0:4], gl4[:, 0:4], gc8[:, g4 * 4:g4 * 4 + 4], ALU.subtract)
        P.act(ekd4[:, 0:4], ekd4[:, 0:4], AF.Exp)
        chk("G2")
        yield
        pbk = psA.next().bc(BF16)
        pbv = psA.next().bc(BF16)
        for hh, h in enumerate(hs):
            P.tr(pbk[:, hh * 128:(hh + 1) * 128], kn[g4][:, hh, cols], ident_b)
        for hh, h in enumerate(hs):
            P.tr(pbv[:, hh * 128:(hh + 1) * 128], vs[g4][:, hh, cols], ident_b)
        yield
        for hh, h in enumerate(hs):
            P.ts("vector", Rw[:, hh, :], pbk[:, hh * 128:(hh + 1) * 128], bg8[:, h:h + 1], None, ALU.mult)
            P.act(kdec[:, hh, :], pbk[:, hh * 128:(hh + 1) * 128], AF.Identity, scale=ekd4[:, hh:hh + 1])
            P.act(Ru[:, hh, :], pbv[:, hh * 128:(hh + 1) * 128], AF.Identity, scale=beta8[:, h:h + 1])
        chk("G3")
        yield
        pkk = psA.next()
        pqk = psA.next()
        for hh, h in enumerate(hs):
            P.mm(pkk[:, hh * 128:(hh + 1) * 128], kn[g4][:, hh, cols], kn[g4][:, hh, cols])
        for hh, h in enumerate(hs):
            P.mm(pqk[:, hh * 128:(hh + 1) * 128], kn[g4][:, hh, cols], qn[g4][:, hh, cols])
        yield
        a_d = args.next()
        for hh, h in enumerate(hs):
            P.stt("vector" if hh % 2 == 0 else "gpsimd", a_d[:, hh, :], Gs[:, hh, :], gcb8[:, h:h + 1], pm_d, ALU.subtract, ALU.max)
        P.act(Fd.re("p a b -> p (a b)"), a_d.re("p a b -> p (a b)"), AF.Exp, scale=-1.0)
        a_r = args.next()
        for hh, h in enumerate(hs):
            P.stt("vector" if hh % 2 == 0 else "gpsimd", a_r[:, hh, :], Gs[:, hh, :], gcb8[:, h:h + 1], pm_r, ALU.subtract, ALU.max)
        P.act(Fr.re("p a b -> p (a b)"), a_r.re("p a b -> p (a b)"), AF.Exp, scale=-1.0)
        a_t = args.next()
        for hh, h in enumerate(hs):
            P.stt("vector" if hh % 2 == 0 else "gpsimd", a_t[:, hh, :], Gs[:, hh, :], gc8[:, h:h + 1], nm_t, ALU.subtract, ALU.min)
        P.act(Dt.re("p a b -> p (a b)"), a_t.re("p a b -> p (a b)"), AF.Exp)
        fl = lambda t_: t_.re("p a b -> p (a b)")
        P.tt("vector", fl(Ld), pkk, fl(Fd), ALU.mult)
        P.tt("vector", fl(Rr), pkk, fl(Fr), ALU.mult)
        P.tt("vector", fl(AqkT), pqk, fl(Dt), ALU.mult)
        for hh in range(4):
            P.tt("gpsimd", qdT[:, hh, :], qn[g4][:, hh, cols], EG[:, hh, :], ALU.mult)
        chk("G4")
        yield
        pbt = psA.next().bc(BF16)
        for hh in range(4):
            P.tr(pbt[:, hh * 128:(hh + 1) * 128], Ld[:, hh, :], ident_b)
        yield
        P.copy("scalar", fl(LdT), pbt[:, 0:512])
        chk("G5")
        P.tt("vector", fl(Yp[0]), fl(ident4_b), fl(LdT), ALU.subtract)
        P.tt("gpsimd", fl(Dp[0]), fl(ident4_b), fl(Ld), ALU.subtract)

        def squares(Mc, MTc, Mn, MTn):
            p1 = psA.next()
            p2 = psA.next()
            for hh in range(4):
                P.mm(p1[:, hh * 128:(hh + 1) * 128], MTc[:, hh, :], Mc[:, hh, :])
            for hh in range(4):
                P.mm(p2[:, hh * 128:(hh + 1) * 128], Mc[:, hh, :], MTc[:, hh, :])
            return p1, p2

        Mc, MTc = Ld, LdT
        Mn, MTn = Mp[1], MTp[1]
        yield
        p1, p2 = squares(Mc, MTc, Mn, MTn)
        yield
        P.copy("scalar", fl(Mn), p1)
        P.copy("vector", fl(MTn), p2)
        yi = 0
        for m in range(1, 5):
            Mc, MTc = Mn, MTn
            yield
            p3 = psA.next()
            p4 = psA.next()
            for hh in range(4):
                P.mm(p3[:, hh * 128:(hh + 1) * 128], Mc[:, hh, :], Yp[yi][:, hh, :], start=True, stop=False)
                P.mm(p3[:, hh * 128:(hh + 1) * 128], ident_b, Yp[yi][:, hh, :], start=False, stop=True)
            for hh in range(4):
                P.mm(p4[:, hh * 128:(hh + 1) * 128], MTc[:, hh, :], Dp[yi][:, hh, :], start=True, stop=False)
                P.mm(p4[:, hh * 128:(hh + 1) * 128], ident_b, Dp[yi][:, hh, :], start=False, stop=True)
            if m < 4:
                Mn, MTn = Mp[(m + 1) % 2], MTp[(m + 1) % 2]
                p1, p2 = squares(Mc, MTc, Mn, MTn)
            yield
            P.copy("scalar", fl(Yp[1 - yi]), p3)
            P.copy("vector", fl(Dp[1 - yi]), p4)
            if m < 4:
                P.copy("scalar", fl(Mn), p1)
                P.copy("vector", fl(MTn), p2)
            yi = 1 - yi
        Yd, Dd = Yp[yi], Dp[yi]
        chk("G6")
        DRu, DRw = Mp[0], MTp[0]
        yield
        p1 = psA.next()
        p2 = psA.next()
        p3 = psA.next()
        p4 = psA.next()
        for hh in range(4):
            P.mm(p1[:, hh * 128:(hh + 1) * 128], Rr[:, hh, :], Yd[:, hh, :])
        for hh in range(4):
            P.mm(p2[:, hh * 128:(hh + 1) * 128], Yd[:, hh, :], Rr[:, hh, :])
        for hh in range(4):
            P.mm(p3[:, hh * 128:(hh + 1) * 128], Yd[:, hh, :], Ru[:, hh, :])
        for hh in range(4):
            P.mm(p4[:, hh * 128:(hh + 1) * 128], Yd[:, hh, :], Rw[:, hh, :])
        yield
        P.tt("vector", fl(Wn), fl(ident4_b), p1, ALU.subtract)
        P.copy("scalar", fl(Nb), p2)
        P.copy("scalar", fl(DRu), p3)
        P.copy("vector", fl(DRw), p4)
        yield
        p3 = psA.next()
        for hh in range(4):
            P.mm(p3[:, hh * 128:(hh + 1) * 128], Nb[:, hh, :], Wn[:, hh, :])
        yield
        P.copy("scalar", fl(T1), p3)
        yield
        p4 = psA.next()
        for hh in range(4):
            P.mm(p4[:, hh * 128:(hh + 1) * 128], Nb[:, hh, :], T1[:, hh, :], start=True, stop=False)
            P.mm(p4[:, hh * 128:(hh + 1) * 128], ident_b, Wn[:, hh, :], start=False, stop=True)
        yield
        P.copy("scalar", fl(Zt), p4)
        chk("G7")
        yield
        p6 = psA.next()
        for hh in range(4):
            P.mm(p6[:, hh * 128:(hh + 1) * 128], DRw[:, hh, :], Zt[:, hh, :])
        yield
        P.ts("vector", fl(wTn), p6, -1.0, None, ALU.mult)
        yield
        p7 = psA.next()
        for hh in range(4):
            P.mm(p7[:, hh * 128:(hh + 1) * 128], Zt[:, hh, :], DRu[:, hh, :], start=True, stop=False)
            P.mm(p7[:, hh * 128:(hh + 1) * 128], wTn[:, hh, :], Sgb[g4][:, hh, :], start=False, stop=True)
        yield
        P.copy("scalar", fl(dlt), p7)
        yield
        p8 = psA.next()
        for hh in range(4):
            P.mm(p8[:, hh * 128:(hh + 1) * 128], qdT[:, hh, :], Sgb[g4][:, hh, :], start=True, stop=False)
            P.mm(p8[:, hh * 128:(hh + 1) * 128], AqkT[:, hh, :], dlt[:, hh, :], start=False, stop=True)
        p9 = psA.next()
        for hh in range(4):
            P.mm(p9[:, hh * 128:(hh + 1) * 128], kdec[:, hh, :], dlt[:, hh, :])
        yield
        for hh, h in enumerate(hs):
            P.stt("vector", Sg[:, h, :], Sg[:, h, :], gend4[:, hh:hh + 1], p9[:, hh * 128:(hh + 1) * 128], ALU.mult, ALU.add)
        P.copy("scalar", fl(Sgb[g4]), Sg[:, g4 * 4:(g4 + 1) * 4, :].re("p a b -> p (a b)"))
        sq = wide_f.next()
        P.act(sq, p8, AF.Square)
        ss4 = smalls.next()
        P.reduce("vector", ss4[:, 0:4], sq.re("p (a b) -> p a b", a=4), ALU.add)
        P.ts("vector", ss4[:, 0:4], ss4[:, 0:4], 1.0 / HD, EPS, ALU.mult, ALU.add)
        P.act(ss4[:, 0:4], ss4[:, 0:4], AF.Ln)
        rs4 = smalls.next()
        P.act(rs4[:, 0:4], ss4[:, 0:4], AF.Exp, scale=-0.5)
        for hh, h in enumerate(hs):
            P.stt("vector", oa[:, h * 128:(h + 1) * 128], p8[:, hh * 128:(hh + 1) * 128], rs4[:, hh:hh + 1],
                  zg[:, s, h * 128:(h + 1) * 128], ALU.mult, ALU.mult)
        if g4 == 1:
            tap("oa", oa, [128, 1024], BF16)
            to_T(oa, oaT, c0)
        yield

    def gla_proj(h, hT, wt):
        pz = psA.next()
        P.mm(pz[:, 0:T], wg2[:, h * 128:(h + 1) * 128], lgT)
        pq = psA.next()
        pk = psA.next()
        for kc in range(8):
            P.mm(pq[:, 0:T], wt[:, kc, 0:128], hT[:, kc, :], start=(kc == 0), stop=(kc == 7))
        for kc in range(8):
            P.mm(pk[:, 0:T], wt[:, kc, 128:256], hT[:, kc, :], start=(kc == 0), stop=(kc == 7))
        yield
        l_ = lsc.next()
        P.act(l_, pz[:, 0:T], AF.Exp, scale=-1.0, bias=negb[:, h:h + 1])
        P.act(l_, l_, AF.Ln, bias=1.0)
        for s in range(NS):
            P.scan(cs[:, h, s * 128:(s + 1) * 128], ones_f, l_[:, s * 128:(s + 1) * 128], 0.0, ALU.mult, ALU.add)
        for s in range(NS):
            cols = slice(s * 128, (s + 1) * 128)
            cv = cs[:, h, cols]
            c2 = smallsB.next()
            P.ts("vector", c2[:, 0:1], cv[:, 64:65], 1.0 / 16, None, ALU.mult)
            P.ts("vector", c2[:, 1:2], cv[:, 64:65], -1.0 / 16, None, ALU.mult)
            P.ts("vector", c2[:, 2:3], cv[:, 127:128], -1.0 / 16, None, ALU.mult)
            eq = efac.next()
            P.act(eq, cv, AF.Exp, scale=-1.0 / 16, bias=c2[:, 0:1])
            ek = efac.next()
            P.act(ek, cv, AF.Exp, scale=1.0 / 16, bias=c2[:, 1:2])
            ed = efac.next()
            P.act(ed, cv, AF.Exp, scale=-1.0 / 16)
            ekd = efac.next()
            P.act(ekd, cv, AF.Exp, scale=1.0 / 16, bias=c2[:, 2:3])
            sc = float(LKD ** -0.5)
            P.stt("vector", glp[0][:, h, cols], pq[:, cols], sc, eq, ALU.mult, ALU.mult)
            P.tt("vector", glp[1][:, h, cols], pk[:, cols], ek, ALU.mult)
            P.stt("vector", glp[2][:, h, cols], pq[:, cols], sc, ed, ALU.mult, ALU.mult)
            P.tt("vector", glp[3][:, h, cols], pk[:, cols], ekd, ALU.mult)
            P.copy("gpsimd", gendl[:, s, h:h + 1], ed[:, 127:128])
        yield
        for s in range(NS):
            pv = psA.next()
            for kc in range(8):
                P.mm(pv[:, 0:256], hT[:, kc, s * 128:(s + 1) * 128], wt[:, kc, 256:512], start=(kc == 0), stop=(kc == 7))
            yield
            P.copy("scalar", vl[:, s, h * 256:(h + 1) * 256], pv[:, 0:256])
            yield

    gendl = P.sb("gendl", [128, NS, 4], F32)

    def gla_r(c, hT, wt):
        for s in range(NS):
            pb = psA.next()
            for kc in range(8):
                P.mm(pb, hT[:, kc, s * 128:(s + 1) * 128], wt[:, kc, :], start=(kc == 0), stop=(kc == 7))
            yield
            e = wide_f.next()
            P.act(e, pb, AF.Exp, scale=-1.0)
            P.act(e, e, AF.Ln, bias=1.0)
            P.act(e, e, AF.Exp, scale=-1.0)
            P.tt("vector", e, pb, e, ALU.mult)
            P.tt("gpsimd", rg[:, s, c * 512:(c + 1) * 512], e, lng[:, c * 512:(c + 1) * 512], ALU.mult)
            yield

    def gla_core(s, obT):
        c0 = s * 128
        cols = slice(c0, c0 + 128)
        fl = lambda t_: t_.re("p a b -> p (a b)")
        pa = psA.next()
        for h in range(4):
            P.mm(pa[:, h * 128:(h + 1) * 128], glp[1][:, h, cols], glp[0][:, h, cols])
        pbt = psA.next().bc(BF16)
        for h in range(4):
            P.tr(pbt[:, h * 128:(h + 1) * 128], glp[3][:, h, cols], ident_b)
        yield
        P.tt("vector", fl(ATm), pa, fl(m01t4), ALU.mult)
        P.copy("scalar", fl(kdl), pbt[:, 0:512])
        yield
        po = [psA.next(), psA.next()]
        for h in range(4):
            o_ = po[h // 2][:, (h % 2) * 256:(h % 2 + 1) * 256]
            P.mm(o_, glp[2][:, h, cols], Slb[:, h, :], start=True, stop=False)
            P.mm(o_, ATm[:, h, :], vl[:, s, h * 256:(h + 1) * 256], start=False, stop=True)
        yield
        ss4 = smallsB.next()
        for half in range(2):
            sq = wide_f.next()
            P.act(sq, po[half], AF.Square)
            P.reduce("vector", ss4[:, half * 2:half * 2 + 2], sq.re("p (a b) -> p a b", a=2), ALU.add)
        P.ts("vector", ss4[:, 0:4], ss4[:, 0:4], 1.0 / LVD, EPS, ALU.mult, ALU.add)
        P.act(ss4[:, 0:4], ss4[:, 0:4], AF.Ln)
        rs4 = smallsB.next()
        P.act(rs4[:, 0:4], ss4[:, 0:4], AF.Exp, scale=-0.5)
        for h in range(4):
            P.stt("vector", ob[:, h * 256:(h + 1) * 256], po[h // 2][:, (h % 2) * 256:(h % 2 + 1) * 256], rs4[:, h:h + 1],
                  rg[:, s, h * 256:(h + 1) * 256], ALU.mult, ALU.mult)
        yield
        pu = [psA.next(), psA.next()]
        for h in range(4):
            P.mm(pu[h // 2][:, (h % 2) * 256:(h % 2 + 1) * 256], kdl[:, h, :], vl[:, s, h * 256:(h + 1) * 256])
        yield
        for h in range(4):
            P.stt("vector", Sl[:, h, :], Sl[:, h, :], gendl[:, s, h:h + 1], pu[h // 2][:, (h % 2) * 256:(h % 2 + 1) * 256], ALU.mult, ALU.add)
        P.copy("scalar", fl(Slb), fl(Sl))
        yield
        tap("ob", ob, [128, 1024], BF16)
        to_T(ob, obT, c0)
        yield
        yield

    def gates_proj(i, hT, wt):
        dst = sa if i < 2 else sb_
        c = i % 2
        for s in range(NS):
            pb = psA.next()
            for kc in range(8):
                P.mm(pb, hT[:, kc, s * 128:(s + 1) * 128], wt[:, kc, :], start=(kc == 0), stop=(kc == 7))
            yield
            e = wide_f.next()
            P.act(e, pb, AF.Exp, scale=-1.0)
            P.act(e, e, AF.Ln, bias=1.0)
            P.act(dst[:, s, c * 512:(c + 1) * 512], e, AF.Exp, scale=-1.0)
            yield

    def stage_A(q_, st):
        t0 = st * T
        P.phase = "A_start"
        P.dma("sync", xt, x_d[q_, t0:t0 + T, :].rearrange("(s p) d -> p s d", p=128))
        hT = ttiles.next()
        for s in range(NS):
            norm_T(xt[:, s, :], hT, s * 128)
        tap("hT", hT, [128, 8, T], BF16)
        chk("A_norm")
        for s in range(NS):
            pb = psA.next()
            for kc in range(8):
                P.mm(pb[:, 0:32], hT[:, kc, s * 128:(s + 1) * 128], wsm[:, kc, :], start=(kc == 0), stop=(kc == 7))
            P.copy("scalar", gab[:, s, :], pb[:, 0:32])
        pb = psA.next()
        for kc in range(8):
            P.mm(pb[0:16, 0:T], wsm[:, kc, 16:32], hT[:, kc, :], start=(kc == 0), stop=(kc == 7))
        P.copy("scalar", lgT, pb[0:16, 0:T])
        chk("A_small")
        pend = []
        for h in range(8):
            wt = wget(B_GDN + h)
            cur = gdn_front(h, hT, wt)
            gdn_z(h, hT, wt)
            pend.append((h,) + cur)
            if len(pend) > 1:
                gdn_back(*pend.pop(0))
        while pend:
            gdn_back(*pend.pop(0))
        tap("qn0", qn[0], [128, 4, T], BF16); tap("kn0", kn[0], [128, 4, T], BF16); tap("vs0", vs[0], [128, 4, T], BF16)
        tap("zg", zg, [128, NS, 1024], BF16)
        chk("A_gdnproj")
        oaT = ttiles.next()
        obT = ttiles.next()
        def side_gen():
            for h in range(4):
                yield from gla_proj(h, hT, wget(B_GLA + h))
            for c in range(2):
                yield from gla_r(c, hT, wget(B_R + c))
            for s in range(NS):
                yield from gla_core(s, obT)
            for i in range(4):
                yield from gates_proj(i, hT, wget(B_GATE + i))

        side_it = side_gen()
        for s in range(NS):
            for g4 in range(2):
                for _ in gdn_unit(s, g4, oaT):
                    next(side_it, None)
        chk("A_gdncore")
        for _ in side_it:
            pass
        chk("A_glacore")
        return oaT, obT

    def resid_proj(srcT, s, blk0, after=None):
        pass

    def stage_B(q_, st, oaT, obT):
        t0 = st * T
        P.phase = "B_branch"
        for c in range(2):
            wt = wget(B_BG + c)
            for s in range(NS):
                pb = psA.next()
                for kc in range(8):
                    P.mm(pb, oaT[:, kc, s * 128:(s + 1) * 128], wt[:, kc, :], start=(kc == 0), stop=(kc == 7))
                P.tt("vector", yam[:, s, c * 512:(c + 1) * 512], pb, sa[:, s, c * 512:(c + 1) * 512], ALU.mult)
        mTt = ttiles.next()
        for c in range(2):
            wt = wget(B_BL + c)
            for s in range(NS):
                pb = psA.next()
                for kc in range(8):
                    P.mm(pb, obT[:, kc, s * 128:(s + 1) * 128], wt[:, kc, :], start=(kc == 0), stop=(kc == 7))
                m2 = wide_f.next()
                P.tt("vector", m2, pb, sb_[:, s, c * 512:(c + 1) * 512], ALU.mult)
                P.tt("gpsimd", tokb[s][:, c * 512:(c + 1) * 512], m2, yam[:, s, c * 512:(c + 1) * 512], ALU.add)
        for s in range(NS):
            to_T(tokb[s], mTt, s * 128)
        for c in range(2):
            wt = wget(B_OUT + c)
            for s in range(NS):
                pb = psA.next()
                for kc in range(8):
                    P.mm(pb, mTt[:, kc, s * 128:(s + 1) * 128], wt[:, kc, :], start=(kc == 0), stop=(kc == 7))
                P.tt("vector", xt[:, s, c * 512:(c + 1) * 512], pb, xt[:, s, c * 512:(c + 1) * 512], ALU.add)
        tap("x1", xt, [128, NS, 1024])
        chk("B_x1")
        h2T = ttiles.next()
        for s in range(NS):
            norm_T(xt[:, s, :], h2T, s * 128)
        qT = ttiles.next()
        for c2 in range(2):
            wt = wget(B_WQ + c2)
            for cc in range(4):
                pb = psA.next()
                for kc in range(8):
                    P.mm(pb[:, 0:T], wt[:, kc, cc * 128:(cc + 1) * 128], h2T[:, kc, :], start=(kc == 0), stop=(kc == 7))
                P.act(qT[:, c2 * 4 + cc, :], pb[:, 0:T], AF.Identity, scale=float(256 ** -0.5))
        oxT = ttiles.next()
        for s in range(NS):
            cols = slice(s * 128, (s + 1) * 128)
            psc = [psA.next(), psA.next()]
            for h in range(4):
                o_ = psc[h // 2][:, (h % 2) * 256:(h % 2 + 1) * 256]
                for c in range(2):
                    P.mm(o_, qT[:, 2 * h + c, cols], KT[:, 2 * h + c, :], start=(c == 0), stop=(c == 1))
            mx = smalls.next()
            for half in range(2):
                P.reduce("vector", mx[:, half * 2:half * 2 + 2], psc[half].re("p (a b) -> p a b", a=2), ALU.max)
            P.ts("vector", mx[:, 0:4], mx[:, 0:4], -1.0, None, ALU.mult)
            sm4 = smalls.next()
            for h in range(4):
                P.act(pt[:, h, :], psc[h // 2][:, (h % 2) * 256:(h % 2 + 1) * 256], AF.Exp, bias=mx[:, h:h + 1], accum_out=sm4[:, h:h + 1])
            rs4 = smalls.next()
            P.recip("vector", rs4[:, 0:4], sm4[:, 0:4])
            pbt = psA.next().bc(BF16)
            for h in range(4):
                for mc in range(2):
                    P.tr(pbt[:, (2 * h + mc) * 128:(2 * h + mc + 1) * 128], pt[:, h, mc * 128:(mc + 1) * 128], ident_b)
            P.copy("scalar", pT.re("p a b -> p (a b)"), pbt)
            pov = [psA.next(), psA.next()]
            for h in range(4):
                o_ = pov[h // 2][:, (h % 2) * 256:(h % 2 + 1) * 256]
                for mc in range(2):
                    P.mm(o_, pT[:, 2 * h + mc, :], Vt[:, mc, h * 256:(h + 1) * 256], start=(mc == 0), stop=(mc == 1))
            for h in range(4):
                P.ts("vector", ox[:, h * 256:(h + 1) * 256], pov[h // 2][:, (h % 2) * 256:(h % 2 + 1) * 256], rs4[:, h:h + 1], None, ALU.mult)
            to_T(ox, oxT, s * 128)
        for c in range(2):
            wt = wget(B_WO + c)
            for s in range(NS):
                pb = psA.next()
                for kc in range(8):
                    P.mm(pb, oxT[:, kc, s * 128:(s + 1) * 128], wt[:, kc, :], start=(kc == 0), stop=(kc == 7))
                P.tt("vector", xt[:, s, c * 512:(c + 1) * 512], pb, xt[:, s, c * 512:(c + 1) * 512], ALU.add)
        tap("x2", xt, [128, NS, 1024])
        chk("B_x2")
        P.phase = "B_mlp"
        h3T = ttiles.next()
        for s in range(NS):
            norm_T(xt[:, s, :], h3T, s * 128)
        acc = {}
        for fg in range(4):
            hq = hidq[fg % 2]
            for c2 in range(2):
                wt = wget(B_W1 + fg * 2 + c2)
                for cc in range(4):
                    fi = c2 * 4 + cc
                    pb = psA.next()
                    for kc in range(8):
                        P.mm(pb[:, 0:T], wt[:, kc, cc * 128:(cc + 1) * 128], h3T[:, kc, :], start=(kc == 0), stop=(kc == 7))
                    r_ = wide_f.next()
                    P.act(r_[:, 0:T], pb[:, 0:T], AF.Relu)
                    P.tt("vector" if fi % 2 == 0 else "gpsimd", hq[:, fi, :], r_[:, 0:T], r_[:, 0:T], ALU.mult)
            if fg == 0:
                for s in range(NS):
                    for c in range(2):
                        acc[(s, c)] = psA.next()
                        psA.hold(acc[(s, c)])
            for c in range(2):
                wt = wget(B_W2 + fg * 2 + c)
                for s in range(NS):
                    for kc in range(8):
                        P.mm(acc[(s, c)], hq[:, kc, s * 128:(s + 1) * 128], wt[:, kc, :],
                             start=(fg == 0 and kc == 0), stop=(fg == 3 and kc == 7))
        for s in range(NS):
            for c in range(2):
                P.tt("vector", xt[:, s, c * 512:(c + 1) * 512], acc[(s, c)], xt[:, s, c * 512:(c + 1) * 512], ALU.add)
                psA.release(acc[(s, c)])
        tap("x3", xt, [128, NS, 1024])
        chk("B_x3")
        P.phase = "B_final"
        for s in range(NS):
            rs = rstd_of(xt[:, s, :], 1024)
            P.stt("vector", xt[:, s, :], xt[:, s, :], rs, gfin, ALU.mult, ALU.mult)
        d = P.dma("sync", y_d[q_, t0:t0 + T, :].rearrange("(s p) d -> p s d", p=128), xt)
        P.final_deps.append(d)

    try:
      chk("prep")
      for q_ in range(nseq):
        P.memset("vector", Sg.re("p a b -> p (a b)"), 0.0)
        for g in range(2):
            P.memset("gpsimd", Sgb[g].re("p a b -> p (a b)"), 0.0)
        P.memset("vector", Sl.re("p a b -> p (a b)"), 0.0)
        P.memset("gpsimd", Slb.re("p a b -> p (a b)"), 0.0)
        P.memset("gpsimd", halo.re("p a b -> p (a b)"), 0.0)
        kv_prep(q_)
        chk("kv")
        for st in range(nst):
            oaT, obT = stage_A(q_, st)
            chk("A")
            stage_B(q_, st, oaT, obT)
    except _Stop:
        pass

    bes = ExitStack()
    finals = P.finalize(bes)
    bes.enter_context(nc.allow_low_precision(reason="bf16 matmul operands by design; fp32 accumulation"))
    block = bes.enter_context(nc.Block())
    P.emit(block, finals)
    bes.close()
    es.close()
    ninst = {e: len(P.ops[e]) for e in ENGS}
    return nc, tap_outs, dict(ninst=ninst, nwaits=P.nwaits, nsems=P.nsems, op_phase=P.op_phase)


N_CORES = 8


def kernel(**inputs):
    x = np.asarray(inputs["x"], dtype=np.float32)
    mem = np.asarray(inputs["mem"], dtype=np.float32)
    B, S, _ = x.shape
    nseq = B // N_CORES
    lay = host_layout(inputs)
    nc, _, _ = build(nseq, S)
    in_maps = []
    for c in range(N_CORES):
        m = dict(lay)
        m["x"] = np.ascontiguousarray(x[c * nseq:(c + 1) * nseq])
        m["mem"] = np.ascontiguousarray(mem[c * nseq:(c + 1) * nseq])
        in_maps.append(m)
    res = run_bass_kernel_spmd(nc, in_maps, core_ids=list(range(N_CORES)))
    out = np.concatenate([np.asarray(r["y"], dtype=np.float32) for r in res.results], axis=0)
    return out
```

```python
from contextlib import ExitStack
from concourse.bass_utils import run_bass_kernel_spmd
import numpy as np
import concourse.bass as bass
import concourse.mybir as mybir

F32 = mybir.dt.float32
BF16 = mybir.dt.bfloat16
AF = mybir.ActivationFunctionType
ALU = mybir.AluOpType
AX = mybir.AxisListType

ENGS = ("sync", "scalar", "vector", "gpsimd", "tensor")


class Buf:
    __slots__ = ("name", "last_w", "readers", "excl")

    def __init__(self, name):
        self.name = name
        self.last_w = None
        self.readers = []
        self.excl = False


class V:
    __slots__ = ("ap", "buf", "dram")

    def __init__(self, ap, buf, dram=False):
        self.ap = ap
        self.buf = buf
        self.dram = dram

    def __getitem__(self, idx):
        return V(self.ap[idx], self.buf, self.dram)

    def re(self, pat, **kw):
        return V(self.ap.rearrange(pat, **kw), self.buf, self.dram)

    def bc(self, dt):
        return V(self.ap.bitcast(dt), self.buf, self.dram)


class Tile(V):
    def __init__(self, t, name):
        V.__init__(self, t[:], Buf(name))
        self.t = t


class Op:
    __slots__ = ("eng", "fn", "deps", "is_dma", "sem", "sigval", "signal", "idx", "waits")

    def __init__(self, eng, fn, deps, is_dma):
        self.eng = eng
        self.fn = fn
        self.deps = deps
        self.is_dma = is_dma
        self.sem = None
        self.sigval = 0
        self.signal = False
        self.waits = []


class Prog:
    def __init__(self, nc, es, same_engine_sync=False):
        self.nc = nc
        self.es = es
        self.ops = {e: [] for e in ENGS}
        self.same_engine_sync = same_engine_sync
        self.dma_sems = {}
        self.final_deps = []
        self.nsb = 0
        self.phase = ""
        self.op_phase = {e: [] for e in ENGS}

    def sb(self, name, shape, dt):
        t = self.es.enter_context(self.nc.sbuf_tensor("sb_" + name, list(shape), dt))
        return Tile(t, name)

    def dram(self, ap, name):
        return V(ap, Buf(name), True)

    def ps(self, name, shape, dt):
        t = self.es.enter_context(self.nc.psum_tensor("ps_" + name, list(shape), dt))
        tl = Tile(t, name)
        tl.buf.excl = True
        return tl

    def rec(self, eng, fn, reads=(), writes=(), is_dma=False, dma_key=None):
        deps = set()
        rb = [v.buf for v in reads if isinstance(v, V)]
        wb = [v.buf for v in writes if isinstance(v, V)]
        for b in rb:
            if b.last_w is not None:
                deps.add(b.last_w)
            if b.excl:
                for r in b.readers:
                    if r.eng != eng:
                        deps.add(r)
        for b in wb:
            lw = b.last_w
            if lw is not None and (lw.eng != eng or lw.is_dma or is_dma):
                deps.add(lw)
            for r in b.readers:
                if r.eng != eng or r.is_dma or is_dma:
                    deps.add(r)
        op = Op(eng, fn, deps, is_dma)
        if is_dma:
            op.sem = dma_key
        for b in wb:
            b.last_w = op
            b.readers = []
        for b in rb:
            if b not in wb:
                b.readers.append(op)
        op.idx = len(self.ops[eng])
        self.ops[eng].append(op)
        self.op_phase[eng].append(self.phase)
        return op

    def finalize(self, block_es):
        nc = self.nc
        for e in ENGS:
            for op in self.ops[e]:
                if op.is_dma:
                    op.signal = True
                for d in op.deps:
                    if d.is_dma:
                        continue
                    d.signal = True
        last = {}
        for e in ENGS:
            for op in self.ops[e]:
                if op.is_dma:
                    last[op.sem] = op
        for op in last.values():
            if op not in self.final_deps:
                self.final_deps.append(op)
        for d in self.final_deps:
            d.signal = True
        eng_sem = {e: block_es.enter_context(nc.semaphore("sem_" + e)) for e in ENGS}
        dma_sem = {}
        for e in ENGS:
            cnt = 0
            for op in self.ops[e]:
                if op.is_dma:
                    key = op.sem
                    if key not in dma_sem:
                        dma_sem[key] = [block_es.enter_context(nc.semaphore("dsem_%d" % len(dma_sem))), 0]
                    ent = dma_sem[key]
                    ent[1] += 16
                    op.sem = ent[0]
                    op.sigval = ent[1]
                elif op.signal:
                    cnt += 1
                    op.sem = eng_sem[e]
                    op.sigval = cnt
        nwaits = 0
        for e in ENGS:
            seen = {}
            for op in self.ops[e]:
                need = {}
                for d in op.deps:
                    if d is op:
                        continue
                    k = id(d.sem)
                    if seen.get(k, 0) >= d.sigval:
                        continue
                    if k not in need or need[k][1] < d.sigval:
                        need[k] = (d.sem, d.sigval)
                for k, (s, v) in need.items():
                    seen[k] = v
                    op.waits.append((s, v))
                    nwaits += 1
        self.nwaits = nwaits
        self.nsems = len(dma_sem) + len(ENGS)
        finals = [(d.sem, d.sigval) for d in self.final_deps]
        return finals

    def emit(self, block, finals, final_eng="gpsimd"):
        P = self

        def run(ename, e):
            for op in P.ops[ename]:
                for (s, v) in op.waits:
                    e.wait_ge(s, v)
                ins = op.fn(e)
                if op.signal:
                    ins.then_inc(op.sem, 16 if op.is_dma else 1)
            if ename == final_eng:
                for (s, v) in finals:
                    e.wait_ge(s, v)

        @block.sync
        def _(e):
            run("sync", e)

        @block.scalar
        def _(e):
            run("scalar", e)

        @block.vector
        def _(e):
            run("vector", e)

        @block.gpsimd
        def _(e):
            run("gpsimd", e)

        @block.tensor
        def _(e):
            run("tensor", e)

    def dma(self, eng, out, in_, key=None):
        k = key
        if k is None:
            if isinstance(out, V) and not out.dram:
                k = out.buf
            elif isinstance(in_, V) and not in_.dram:
                k = in_.buf
            else:
                k = out.buf if isinstance(out, V) else in_.buf
        o = out.ap if isinstance(out, V) else out
        i = in_.ap if isinstance(in_, V) else in_
        return self.rec(eng, lambda e: e.dma_start(out=o, in_=i),
                        reads=[in_], writes=[out], is_dma=True, dma_key=k)

    def mm(self, out, lhsT, rhs, start=True, stop=True):
        return self.rec("tensor", lambda e: e.matmul(out.ap, lhsT.ap, rhs.ap, start=start, stop=stop),
                        reads=[lhsT, rhs], writes=[out])

    def tr(self, out, in_, ident):
        return self.rec("tensor", lambda e: e.transpose(out.ap, in_.ap, ident.ap),
                        reads=[in_, ident], writes=[out])

    def act(self, out, in_, func, bias=None, scale=None, accum_out=None, eng="scalar"):
        kw = {}
        reads = [in_]
        if bias is not None:
            kw["bias"] = bias.ap if isinstance(bias, V) else bias
            reads.append(bias)
        if scale is not None:
            kw["scale"] = scale.ap if isinstance(scale, V) else scale
            reads.append(scale)
        writes = [out]
        if accum_out is not None:
            kw["accum_out"] = accum_out.ap
            writes.append(accum_out)
        return self.rec(eng, lambda e: e.activation(out.ap, in_.ap, func, **kw), reads=reads, writes=writes)

    def ts(self, eng, out, in0, s1, s2, op0, op1=None, accum_out=None):
        reads = [in0, s1, s2]
        a1 = s1.ap if isinstance(s1, V) else s1
        a2 = s2.ap if isinstance(s2, V) else s2
        kw = {}
        writes = [out]
        if op1 is not None:
            kw["op1"] = op1
        if accum_out is not None:
            kw["accum_out"] = accum_out.ap
            writes.append(accum_out)
        return self.rec(eng, lambda e: e.tensor_scalar(out=out.ap, in0=in0.ap, scalar1=a1, scalar2=a2, op0=op0, **kw),
                        reads=reads, writes=writes)

    def stt(self, eng, out, in0, scalar, in1, op0, op1):
        sc = scalar.ap if isinstance(scalar, V) else scalar
        eng = "vector"
        return self.rec(eng, lambda e: e.scalar_tensor_tensor(out=out.ap, in0=in0.ap, scalar=sc, in1=in1.ap, op0=op0, op1=op1),
                        reads=[in0, scalar, in1], writes=[out])

    def tt(self, eng, out, in0, in1, op):
        return self.rec(eng, lambda e: e.tensor_tensor(out=out.ap, in0=in0.ap, in1=in1.ap, op=op),
                        reads=[in0, in1], writes=[out])

    def copy(self, eng, out, in_):
        if eng == "scalar":
            return self.rec(eng, lambda e: e.copy(out=out.ap, in_=in_.ap), reads=[in_], writes=[out])
        return self.rec(eng, lambda e: e.tensor_copy(out=out.ap, in_=in_.ap), reads=[in_], writes=[out])

    def reduce(self, eng, out, in_, op, axis=AX.X):
        return self.rec(eng, lambda e: e.tensor_reduce(out=out.ap, in_=in_.ap, axis=axis, op=op),
                        reads=[in_], writes=[out])

    def recip(self, eng, out, in_):
        return self.rec(eng, lambda e: e.reciprocal(out=out.ap, in_=in_.ap), reads=[in_], writes=[out])

    def scan(self, out, d0, d1, initial, op0, op1):
        return self.rec("vector", lambda e: e.tensor_tensor_scan(out=out.ap, data0=d0.ap, data1=d1.ap, initial=initial, op0=op0, op1=op1),
                        reads=[d0, d1], writes=[out])

    def memset(self, eng, out, val):
        return self.rec(eng, lambda e: e.memset(out.ap, val), reads=[], writes=[out])


D = 1024
NH = 8
HD = 128
LH = 4
LKD = 128
LVD = 256
NMEM = 256
DFF = 4096
NS = 2
T = NS * 128
EPS = 1e-6
NBLK = 48
BIG = 1.0e30
DGE_SCRATCH = 1024

B_GDN = 0
B_GLA = 8
B_R = 12
B_GATE = 14
B_BG = 18
B_BL = 20
B_OUT = 22
B_WQ = 24
B_WO = 26
B_W1 = 28
B_W2 = 36
B_WK = 44
B_WV = 46

IN_SIZES = (1024, 1024, 1024, 1024, 8, 8, 512, 512, 1024, 1024, 16, 1024, 1024)


def host_layout(inp):
    f = lambda a: np.ascontiguousarray(np.asarray(a, dtype=np.float32))
    w_in = f(inp["w_in"][0])
    offs = np.cumsum((0,) + IN_SIZES)
    gq, gk, gv, gz, ga, gb, lq, lk, lv, lr, lgate, gate_a, gate_b = [w_in[:, offs[i]:offs[i + 1]] for i in range(13)]
    blocks = []

    def blk(mat):
        assert mat.shape == (1024, 512), mat.shape
        return mat.reshape(8, 128, 512).transpose(1, 0, 2)

    for h in range(8):
        s = slice(h * 128, (h + 1) * 128)
        blocks.append(blk(np.concatenate([gq[:, s], gk[:, s], gv[:, s], gz[:, s]], axis=1)))
    for h in range(4):
        blocks.append(blk(np.concatenate([lq[:, h * 128:(h + 1) * 128], lk[:, h * 128:(h + 1) * 128],
                                          lv[:, h * 256:(h + 1) * 256]], axis=1)))
    for c in range(2):
        blocks.append(blk(lr[:, c * 512:(c + 1) * 512]))
    for g in (gate_a, gate_b):
        for c in range(2):
            blocks.append(blk(g[:, c * 512:(c + 1) * 512]))
    for name in ("w_branch_gdn", "w_branch_gla", "w_out", "xattn_wq", "xattn_wo"):
        w = f(inp[name][0])
        for c in range(2):
            blocks.append(blk(w[:, c * 512:(c + 1) * 512]))
    w1 = f(inp["mlp_w1"][0])
    for c in range(8):
        blocks.append(blk(w1[:, c * 512:(c + 1) * 512]))
    w2 = f(inp["mlp_w2"][0])
    for fg in range(4):
        for c in range(2):
            blocks.append(blk(w2[fg * 1024:(fg + 1) * 1024, c * 512:(c + 1) * 512]))
    for name in ("xattn_wk", "xattn_wv"):
        w = f(inp[name][0])
        for c in range(2):
            blocks.append(blk(w[:, c * 512:(c + 1) * 512]))
    wblk = np.ascontiguousarray(np.stack(blocks, 0)).reshape(NBLK, 128, 4096)
    wsm = np.ascontiguousarray(np.concatenate([ga, gb, lgate], axis=1).reshape(8, 128, 32).transpose(1, 0, 2)).reshape(128, 256)

    def gcol(g):
        return np.ascontiguousarray(f(g).reshape(8, 128).T)

    rep = lambda v: np.ascontiguousarray(np.broadcast_to(f(v).reshape(1, -1), (128, f(v).size)))
    gcols = np.concatenate([gcol(inp["norm_mix_g"][0]), gcol(inp["norm_xattn_g"][0]),
                            gcol(inp["norm_mlp_g"][0]), gcol(inp["norm_mem_g"][0])], axis=1)
    cwt = f(inp["gdn_conv_w"][0])
    cw = np.ascontiguousarray(cwt.reshape(4, 24, 128).transpose(2, 1, 0)).reshape(128, 96)
    small = np.concatenate([
        gcols,
        cw,
        rep(inp["gdn_a_log"][0]),
        rep(inp["gdn_dt_bias"][0]),
        np.ascontiguousarray(f(inp["gla_b_gate"][0]).reshape(4, 128).T),
    ], axis=1)
    small = np.ascontiguousarray(small)
    wide = np.concatenate([
        rep(inp["norm_final_g"]),
        rep(np.tile(f(inp["gdn_norm_g"][0]), 8)),
        rep(np.tile(f(inp["gla_norm_g"][0]), 4)),
    ], axis=1)
    wide = np.ascontiguousarray(wide)
    wg2 = f(inp["gla_w_gate2"][0])
    p = np.arange(128)[:, None]
    q = np.arange(128)[None, :]
    ident = (p == q).astype(np.float32)
    tri = (p <= q).astype(np.float32)
    same32 = (p // 32) == (q // 32)
    pm_d = np.where((p > q) & same32, 0.0, BIG).astype(np.float32)
    pm_r = np.where((p > q) & (~same32), 0.0, BIG).astype(np.float32)
    nm_t = np.where(q >= p, 0.0, -BIG).astype(np.float32)
    m01t = (q >= p).astype(np.float32)
    ones = np.ones((128, 128), np.float32)
    consts = np.ascontiguousarray(np.concatenate([ident, tri, pm_d, pm_r, nm_t, m01t, ones], axis=1))
    return dict(wblk=wblk, wsm=wsm, small=small, wide=wide, wg2=wg2, consts=consts)


class RR:
    def __init__(self, items):
        self.items = items
        self.i = 0
        self.held = set()

    def next(self):
        for _ in range(len(self.items) + 1):
            k = self.i % len(self.items)
            self.i += 1
            if k not in self.held:
                return self.items[k]
        raise RuntimeError("all held")

    def hold(self, it):
        self.held.add(self.items.index(it))

    def release(self, it):
        self.held.discard(self.items.index(it))


class _Stop(Exception):
    pass


def build(nseq, S, taps=None, stop=None):
    assert S % T == 0
    nst = S // T
    nc = bass.Bass("TRN2", target_bir_lowering=False, dynamic_dma_scratch_size=DGE_SCRATCH)
    dr = lambda name, shape, dt=F32, kind="ExternalInput": nc.dram_tensor(name, list(shape), dt, kind=kind).ap()
    x_d = dr("x", [nseq, S, D])
    mem_d = dr("mem", [nseq, NMEM, D])
    wblk_d = dr("wblk", [NBLK, 128, 4096])
    wsm_d = dr("wsm", [128, 256])
    small_d = dr("small", [128, 148])
    wide_d = dr("wide", [128, 3072])
    wg2_d = dr("wg2", [16, 512])
    consts_d = dr("consts", [128, 7 * 128])
    y_d = dr("y", [nseq, S, D], F32, "ExternalOutput")
    wbf_ap = dr("wbf", [NBLK, 128, 4096], BF16, "Internal")
    tap_outs = {}

    es = ExitStack()
    P = Prog(nc, es)
    wbf = [P.dram(wbf_ap[b], "wbf%d" % b) for b in range(NBLK)]

    def chk(name):
        P.phase = "after_" + name
        if stop == name:
            raise _Stop()

    def tap(name, v, shape, dt=F32):
        if taps is None or name not in taps or name in tap_outs:
            return
        o = dr("tap_" + name, shape, dt, "ExternalOutput")
        tap_outs[name] = o
        stg = P.sb("tapstg_" + name, shape, dt)
        P.copy("vector", stg, v)
        d = P.dma("sync", o, stg)
        P.final_deps.append(d)

    cst = P.sb("cst", [128, 7 * 128], F32)
    ident_f = cst[:, 0:128]
    tri_f = cst[:, 128:256]
    pm_d = cst[:, 256:384]
    pm_r = cst[:, 384:512]
    nm_t = cst[:, 512:640]
    m01_f = cst[:, 640:768]
    ones_f = cst[:, 768:896]
    ident_b = P.sb("ident_b", [128, 128], BF16)
    ident4_b = P.sb("ident4_b", [128, 4, 128], BF16)
    ones_b = P.sb("ones_b", [128, 128], BF16)
    m01t4 = P.sb("m01t4", [128, 4, 128], BF16)
    small = P.sb("small", [128, 148], F32)
    gcols = small[:, 0:32]
    cw = small[:, 32:128]
    alog = small[:, 128:136]
    dtb = small[:, 136:144]
    bgate = small[:, 144:148]
    negA = P.sb("negA", [128, 8], F32)
    negb = P.sb("negb", [128, 4], F32)
    gfin = P.sb("gfin", [128, 1024], F32)
    wide_b = P.sb("wide_b", [128, 2048], BF16)
    gng = wide_b[:, 0:1024]
    lng = wide_b[:, 1024:2048]
    wsm = P.sb("wsm", [128, 8, 32], BF16)
    wg2 = P.sb("wg2", [16, 512], BF16)

    Sg = P.sb("Sg", [128, 8, 128], F32)
    Sgb = [P.sb("Sgb%d" % g, [128, 4, 128], BF16) for g in range(2)]
    Sl = P.sb("Sl", [128, 4, 256], F32)
    Slb = P.sb("Slb", [128, 4, 256], BF16)
    halo = P.sb("halo", [128, 24, 3], F32)
    KT = P.sb("KT", [128, 8, 256], BF16)
    Vt = P.sb("Vt", [128, 2, 1024], BF16)

    NSLOT = 3
    slots = [P.sb("wslot%d" % i, [128, 8, 512], BF16) for i in range(NSLOT)]
    xt = P.sb("xt", [128, NS, 1024], F32)
    ttiles = RR([P.sb("tt%d" % i, [128, 8, T], BF16) for i in range(4)])
    banks = [P.ps("bank%d" % i, [128, 512], F32) for i in range(8)]
    psA = RR(banks)

    hb = P.sb("hb", [128, 1024], BF16)
    smalls = RR([P.sb("sm%d" % i, [128, 8], F32) for i in range(24)])

    raws = RR([P.sb("raw%d" % i, [128, 3 + T], F32) for i in range(6)])
    convy = RR([P.sb("convy%d" % i, [128, T], F32) for i in range(6)])
    sc_q = RR([P.sb("scq%d" % i, [128, T], BF16) for i in range(2)])
    qn = [P.sb("qn%d" % g, [128, 4, T], BF16) for g in range(2)]
    kn = [P.sb("kn%d" % g, [128, 4, T], BF16) for g in range(2)]
    vs = [P.sb("vs%d" % g, [128, 4, T], BF16) for g in range(2)]
    zg = P.sb("zg", [128, NS, 1024], BF16)
    gab = P.sb("gab", [128, NS, 32], F32)
    lgT = P.sb("lgT", [16, T], BF16)
    vl = P.sb("vl", [128, NS, 1024], BF16)
    rg = P.sb("rg", [128, NS, 1024], BF16)
    sa = P.sb("sa", [128, NS, 1024], BF16)
    sb_ = P.sb("sb_", [128, NS, 1024], BF16)
    wide_f = RR([P.sb("widef%d" % i, [128, 512], F32) for i in range(2)])

    GT = P.sb("GT", [128, 4, 128], F32)
    Gs = P.sb("Gs", [128, 4, 128], F32)
    EG = P.sb("EG", [128, 4, 128], BF16)
    args = RR([P.sb("arg%d" % i, [128, 4, 128], F32) for i in range(2)])
    mk = lambda n: P.sb(n, [128, 4, 128], BF16)
    Fd, Fr, Dt = mk("Fd"), mk("Fr"), mk("Dt")
    Ld, Rr, AqkT, LdT, qdT, Rw, Ru, kdec = [mk(n) for n in ("Ld", "Rr", "AqkT", "LdT", "qdT", "Rw", "Ru", "kdec")]
    Mp = [mk("Mp0"), mk("Mp1")]
    MTp = [mk("MTp0"), mk("MTp1")]
    Yp = [mk("Yp0"), mk("Yp1")]
    Dp = [mk("Dp0"), mk("Dp1")]
    Zt, wTn, dlt = [mk(n) for n in ("Zt", "wTn", "dlt")]
    Wn, Nb, T1 = Dt, Fr, Fd
    tokb = [P.sb("tokb%d" % i, [128, 1024], BF16) for i in range(2)]
    oa = tokb[0]

    cs = P.sb("cs", [128, 4, T], F32)
    glp = [P.sb("glp%d" % i, [128, 4, T], BF16) for i in range(4)]
    smallsB = RR([P.sb("smB%d" % i, [128, 8], F32) for i in range(12)])
    lsc = RR([P.sb("lsc%d" % i, [128, T], F32) for i in range(2)])
    efac = RR([P.sb("efac%d" % i, [128, 128], F32) for i in range(4)])
    ATm = mk("ATm")
    kdl = mk("kdl")
    ob = tokb[1]

    pt = vl[:, 0, :].re("p (a b) -> p a b", a=4)
    pT = vl[:, 1, :].re("p (a b) -> p a b", a=8)
    ox = tokb[1]
    hidq = [P.sb("hidq%d" % i, [128, 8, T], BF16) for i in range(2)]
    yam = zg


    if taps is not None and "probe" in taps:
        for kb in range(64, 0, -1):
            try:
                es2 = ExitStack()
                es2.enter_context(nc.sbuf_tensor("probe%d" % kb, [128, kb * 256], F32))
                print("SBUF slack >= %d KB" % kb)
                es2.close()
                break
            except Exception as ex:
                pass
    order = []
    for q_ in range(nseq):
        order += [B_WK, B_WK + 1, B_WV, B_WV + 1]
        for st in range(nst):
            order += list(range(0, 28)) + [28, 29, 36, 37, 30, 31, 38, 39, 32, 33, 40, 41, 34, 35, 42, 43]
    wstate = {"issued": 0, "cur": 0}

    def wissue():
        i = wstate["issued"]
        if i < len(order):
            prep_block(order[i])
            P.dma("sync", slots[i % NSLOT].re("p a b -> p (a b)"), wbf[order[i]])
            wstate["issued"] += 1

    def wget(expect):
        c = wstate["cur"]
        assert order[c] == expect, (c, order[c], expect)
        prep_pump()
        while wstate["issued"] <= min(c + NSLOT - 1, len(order) - 1):
            wissue()
        wstate["cur"] += 1
        return slots[c % NSLOT]

    P.dma("sync", cst, consts_d)
    P.dma("sync", small, small_d)
    P.dma("sync", gfin, wide_d[:, 0:1024])
    P.copy("vector", ident_b, ident_f)
    P.copy("vector", ones_b, ones_f)
    for h in range(4):
        P.copy("vector", ident4_b[:, h, :], ident_f)
        P.copy("vector", m01t4[:, h, :], m01_f)
    P.dma("sync", wide_f.items[1][0:16, :], wg2_d)
    P.copy("vector", wg2, wide_f.items[1][0:16, :])
    P.act(negA, alog, AF.Exp)
    P.ts("vector", negA, negA, -1.0, None, ALU.mult)
    P.ts("vector", negb, bgate, -1.0, None, ALU.mult)
    for c in range(4):
        sf_ = wide_f.items[c % 2]
        P.dma("sync", sf_, wide_d[:, 1024 + c * 512:1024 + (c + 1) * 512])
        P.copy("vector", wide_b[:, c * 512:(c + 1) * 512], sf_)
    gain_of = {}
    for b in range(0, 18):
        gain_of[b] = 0
    gain_of[B_WQ] = gain_of[B_WQ + 1] = 1
    for b in range(B_W1, B_W1 + 8):
        gain_of[b] = 2
    for b in range(B_WK, B_WK + 4):
        gain_of[b] = 3
    NPST = 3
    pstf = [P.sb("pstf%d" % i, [128, 1024], F32) for i in range(NPST)]
    pstb = [P.sb("pstb%d" % i, [128, 1024], BF16) for i in range(NPST)]
    pst = {"next_load": 0, "next_cast": 0, "chunks": None}

    def prep_chunks():
        if pst["chunks"] is None:
            seen, ch = set(), []
            for b_ in order:
                if b_ not in seen:
                    seen.add(b_)
                    ch += [(b_, q4) for q4 in range(4)]
            pst["chunks"] = ch
        return pst["chunks"]

    def prep_load(i):
        b_, q4 = prep_chunks()[i]
        P.dma("sync", pstf[i % NPST], wblk_d[b_][:, q4 * 1024:(q4 + 1) * 1024])

    def prep_cast_store(i):
        b_, q4 = prep_chunks()[i]
        gi = gain_of.get(b_, None)
        sf, sbf = pstf[i % NPST], pstb[i % NPST]
        for k2 in range(2):
            kc = q4 * 2 + k2
            eng = "vector" if (i + k2) % 2 == 0 else "scalar"
            o_, i_ = sbf[:, k2 * 512:(k2 + 1) * 512], sf[:, k2 * 512:(k2 + 1) * 512]
            if gi is None:
                P.copy(eng, o_, i_)
            elif eng == "scalar":
                P.act(o_, i_, AF.Identity, scale=gcols[:, gi * 8 + kc:gi * 8 + kc + 1])
            else:
                P.ts(eng, o_, i_, gcols[:, gi * 8 + kc:gi * 8 + kc + 1], None, ALU.mult)
        P.dma("sync", wbf[b_][:, q4 * 1024:(q4 + 1) * 1024], sbf)

    def prep_pump():
        ch = prep_chunks()
        hi = pst["next_load"]
        while pst["next_cast"] < hi:
            prep_cast_store(pst["next_cast"])
            pst["next_cast"] += 1
        while pst["next_load"] < len(ch) and pst["next_load"] - NPST < pst["next_cast"]:
            prep_load(pst["next_load"])
            pst["next_load"] += 1

    def prep_block(b):
        ch = prep_chunks()
        if (b, 3) not in ch:
            return
        last = ch.index((b, 3))
        while pst["next_cast"] <= last:
            prep_pump()

    stf = [wide_f.items[0]]
    sf = stf[0]
    P.dma("sync", sf[:, 0:256], wsm_d)
    for kc in range(8):
        P.ts("vector", wsm[:, kc, :], sf[:, kc * 32:(kc + 1) * 32], gcols[:, kc:kc + 1], None, ALU.mult)

    def rstd_of(xrow, ncols):
        ss = smalls.next()
        P.act(hb[:, 0:ncols], xrow, AF.Square, accum_out=ss[:, 0:1])
        rs = smalls.next()
        P.ts("vector", rs[:, 0:1], ss[:, 0:1], 1.0 / ncols, EPS, ALU.mult, ALU.add)
        P.act(rs[:, 1:2], rs[:, 0:1], AF.Ln)
        P.act(rs[:, 2:3], rs[:, 1:2], AF.Exp, scale=-0.5)
        return rs[:, 2:3]

    def to_T(src_b, dstT, col0, ncol=128, evac="scalar"):
        pb = psA.next().bc(BF16)
        for c in range(8):
            P.tr(pb[:, c * 128:(c + 1) * 128], src_b[:, c * 128:(c + 1) * 128], ident_b)
        P.copy(evac, dstT[:, :, col0:col0 + 128], pb.re("p (a b) -> p a b", a=8))

    def norm_T(xrow, dstT, col0):
        rs = rstd_of(xrow, 1024)
        P.ts("vector", hb, xrow, rs, None, ALU.mult)
        to_T(hb, dstT, col0)

    def kv_prep(q_):
        mT = ttiles.next()
        for mc in range(2):
            P.dma("sync", xt[:, 0, :], mem_d[q_, mc * 128:(mc + 1) * 128, :])
            norm_T(xt[:, 0, :], mT, mc * 128)
        for c2 in range(2):
            wt = wget(B_WK + c2)
            for cc in range(4):
                pb = psA.next()
                for kc in range(8):
                    P.mm(pb[:, 0:256], wt[:, kc, cc * 128:(cc + 1) * 128], mT[:, kc, 0:256], start=(kc == 0), stop=(kc == 7))
                P.copy("scalar", KT[:, c2 * 4 + cc, :], pb[:, 0:256])
        for c2 in range(2):
            wt = wget(B_WV + c2)
            for mc in range(2):
                pb = psA.next()
                for kc in range(8):
                    P.mm(pb, mT[:, kc, mc * 128:(mc + 1) * 128], wt[:, kc, :], start=(kc == 0), stop=(kc == 7))
                P.copy("scalar", Vt[:, mc, c2 * 512:(c2 + 1) * 512], pb)

    def gdn_front(h, hT, wt):
        pbs, rws, ys = [], [], []
        for j in range(3):
            pb = psA.next()
            for kc in range(8):
                P.mm(pb[:, 0:T], wt[:, kc, j * 128:(j + 1) * 128], hT[:, kc, :], start=(kc == 0), stop=(kc == 7))
            pbs.append(pb)
        for j in range(3):
            g = j * 8 + h
            raw = raws.next()
            P.copy("gpsimd", raw[:, 0:3], halo[:, g, :])
            P.copy("scalar", raw[:, 3:3 + T], pbs[j][:, 0:T])
            P.copy("gpsimd", halo[:, g, :], raw[:, T:T + 3])
            rws.append(raw)
        for j in range(3):
            g = j * 8 + h
            y = convy.next()
            P.ts("vector", y, rws[j][:, 3:3 + T], cw[:, g * 4 + 3:g * 4 + 4], None, ALU.mult)
            ys.append(y)
        for jj in (2, 1, 0):
            for j in range(3):
                g = j * 8 + h
                P.stt("vector", ys[j], rws[j][:, jj:jj + T], cw[:, g * 4 + jj:g * 4 + jj + 1], ys[j], ALU.mult, ALU.add)
        return ys, rws

    def gdn_back(h, ys, rws):
        g4, hh = divmod(h, 4)
        es = [rws[j][:, 3:3 + T] for j in range(3)]
        for j in range(3):
            P.act(es[j], ys[j], AF.Exp, scale=-1.0)
        for j in range(3):
            P.act(es[j], es[j], AF.Ln, bias=1.0)
        for j in range(3):
            P.act(es[j], es[j], AF.Exp, scale=-1.0)
        for j in range(2):
            P.tt("vector" if j == 0 else "gpsimd", ys[j], ys[j], es[j], ALU.mult)
        P.tt("gpsimd", vs[g4][:, hh, :], ys[2], es[2], ALU.mult)
        pns = []
        for j in range(2):
            sq = sc_q.next()
            P.tt("gpsimd", sq, ys[j], ys[j], ALU.mult)
            pn = psA.next()
            P.mm(pn[:, 0:T], ones_b, sq)
            pns.append(pn)
        for j in range(2):
            P.act(es[j], pns[j][:, 0:T], AF.Ln, bias=EPS)
        for j in range(2):
            if j == 0:
                P.act(es[j], es[j], AF.Exp, scale=-0.5, bias=float(np.log(HD ** -0.5)))
            else:
                P.act(es[j], es[j], AF.Exp, scale=-0.5)
        for j in range(2):
            dst = (qn if j == 0 else kn)[g4][:, hh, :]
            P.tt("vector", dst, ys[j], es[j], ALU.mult)

    zbank = {}

    def gdn_z(h, hT, wt):
        g4, hh = divmod(h, 4)
        for s in range(NS):
            if hh == 0:
                zbank[s] = psA.next()
                psA.hold(zbank[s])
            pb = zbank[s]
            for kc in range(8):
                P.mm(pb[:, hh * 128:(hh + 1) * 128], hT[:, kc, s * 128:(s + 1) * 128], wt[:, kc, 384:512], start=(kc == 0), stop=(kc == 7))
            if hh == 3:
                e = wide_f.next()
                P.act(e, pb, AF.Exp, scale=-1.0)
                P.act(e, e, AF.Ln, bias=1.0)
                P.act(e, e, AF.Exp, scale=-1.0)
                P.tt("vector", e, pb, e, ALU.mult)
                P.tt("gpsimd", zg[:, s, g4 * 512:(g4 + 1) * 512], e, gng[:, g4 * 512:(g4 + 1) * 512], ALU.mult)
                psA.release(pb)

    dsc = {}

    def gdn_scalars(s):
        t8 = smalls.next()
        P.tt("vector", t8, gab[:, s, 0:8], dtb, ALU.add)
        P.act(t8, t8, AF.Exp)
        sp8 = smalls.next()
        P.act(sp8, t8, AF.Ln, bias=1.0)
        g8 = smalls.next()
        P.tt("vector", g8, sp8, negA, ALU.mult)
        eb8 = smalls.next()
        P.act(eb8, gab[:, s, 8:16], AF.Exp, scale=-1.0)
        lb8 = smalls.next()
        P.act(lb8, eb8, AF.Ln, bias=1.0)
        pb = psA.next()
        P.mm(pb[:, 0:8], tri_f, g8)
        gc8 = smalls.next()
        P.copy("vector", gc8, pb[:, 0:8])
        gcb8 = smalls.next()
        P.tt("vector", gcb8, gc8, lb8, ALU.subtract)
        beta8 = smalls.next()
        P.act(beta8, lb8, AF.Exp, scale=-1.0)
        bg8 = smalls.next()
        P.act(bg8, gcb8, AF.Exp)
        tap("g8", g8, [128, 8]); tap("gc8", gc8, [128, 8]); tap("beta8", beta8, [128, 8])
        chk("G1")
        dsc[s] = (g8, gc8, gcb8, beta8, bg8)

    def gdn_unit(s, g4, oaT):
        c0 = s * 128
        cols = slice(c0, c0 + 128)
        if g4 == 0:
            gdn_scalars(s)
        g8, gc8, gcb8, beta8, bg8 = dsc[s]
        hs = [g4 * 4 + i for i in range(4)]
        for hh, h in enumerate(hs):
            P.act(GT[:, hh, :], tri_f, AF.Identity, scale=g8[:, h:h + 1])
        pb = psA.next()
        P.mm(pb, ones_f, GT.re("p a b -> p (a b)"))
        yield
        P.copy("scalar", Gs.re("p a b -> p (a b)"), pb)
        P.act(EG.re("p a b -> p (a b)"), Gs.re("p a b -> p (a b)"), AF.Exp)
        gl4 = smalls.next()
        P.copy("vector", gl4[:, 0:4], Gs[:, :, 127])
        gend4 = smalls.next()
        P.act(gend4[:, 0:4], gl4[:, 0:4], AF.Exp)
        ekd4 = smalls.next()
        P.tt("vector", ekd4[:, 0:4], gl4[:, 0:4], gc8[:, g4 * 4:g4 * 4 + 4], ALU.subtract)
        P.act(ekd4[:, 0:4], ekd4[:, 0:4], AF.Exp)
        chk("G2")
        yield
        pbk = psA.next().bc(BF16)
        pbv = psA.next().bc(BF16)
        for hh, h in enumerate(hs):
            P.tr(pbk[:, hh * 128:(hh + 1) * 128], kn[g4][:, hh, cols], ident_b)
        for hh, h in enumerate(hs):
            P.tr(pbv[:, hh * 128:(hh + 1) * 128], vs[g4][:, hh, cols], ident_b)
        yield
        for hh, h in enumerate(hs):
            P.ts("vector", Rw[:, hh, :], pbk[:, hh * 128:(hh + 1) * 128], bg8[:, h:h + 1], None, ALU.mult)
            P.act(kdec[:, hh, :], pbk[:, hh * 128:(hh + 1) * 128], AF.Identity, scale=ekd4[:, hh:hh + 1])
            P.act(Ru[:, hh, :], pbv[:, hh * 128:(hh + 1) * 128], AF.Identity, scale=beta8[:, h:h + 1])
        chk("G3")
        yield
        pkk = psA.next()
        pqk = psA.next()
        for hh, h in enumerate(hs):
            P.mm(pkk[:, hh * 128:(hh + 1) * 128], kn[g4][:, hh, cols], kn[g4][:, hh, cols])
        for hh, h in enumerate(hs):
            P.mm(pqk[:, hh * 128:(hh + 1) * 128], kn[g4][:, hh, cols], qn[g4][:, hh, cols])
        yield
        a_d = args.next()
        for hh, h in enumerate(hs):
            P.stt("vector" if hh % 2 == 0 else "gpsimd", a_d[:, hh, :], Gs[:, hh, :], gcb8[:, h:h + 1], pm_d, ALU.subtract, ALU.max)
        P.act(Fd.re("p a b -> p (a b)"), a_d.re("p a b -> p (a b)"), AF.Exp, scale=-1.0)
        a_r = args.next()
        for hh, h in enumerate(hs):
            P.stt("vector" if hh % 2 == 0 else "gpsimd", a_r[:, hh, :], Gs[:, hh, :], gcb8[:, h:h + 1], pm_r, ALU.subtract, ALU.max)
        P.act(Fr.re("p a b -> p (a b)"), a_r.re("p a b -> p (a b)"), AF.Exp, scale=-1.0)
        a_t = args.next()
        for hh, h in enumerate(hs):
            P.stt("vector" if hh % 2 == 0 else "gpsimd", a_t[:, hh, :], Gs[:, hh, :], gc8[:, h:h + 1], nm_t, ALU.subtract, ALU.min)
        P.act(Dt.re("p a b -> p (a b)"), a_t.re("p a b -> p (a b)"), AF.Exp)
        fl = lambda t_: t_.re("p a b -> p (a b)")
        P.tt("vector", fl(Ld), pkk, fl(Fd), ALU.mult)
        P.tt("vector", fl(Rr), pkk, fl(Fr), ALU.mult)
        P.tt("vector", fl(AqkT), pqk, fl(Dt), ALU.mult)
        for hh in range(4):
            P.tt("gpsimd", qdT[:, hh, :], qn[g4][:, hh, cols], EG[:, hh, :], ALU.mult)
        chk("G4")
        yield
        pbt = psA.next().bc(BF16)
        for hh in range(4):
            P.tr(pbt[:, hh * 128:(hh + 1) * 128], Ld[:, hh, :], ident_b)
        yield
        P.copy("scalar", fl(LdT), pbt[:, 0:512])
        chk("G5")
        P.tt("vector", fl(Yp[0]), fl(ident4_b), fl(LdT), ALU.subtract)
        P.tt("gpsimd", fl(Dp[0]), fl(ident4_b), fl(Ld), ALU.subtract)

        def squares(Mc, MTc, Mn, MTn):
            p1 = psA.next()
            p2 = psA.next()
            for hh in range(4):
                P.mm(p1[:, hh * 128:(hh + 1) * 128], MTc[:, hh, :], Mc[:, hh, :])
            for hh in range(4):
                P.mm(p2[:, hh * 128:(hh + 1) * 128], Mc[:, hh, :], MTc[:, hh, :])
            return p1, p2

        Mc, MTc = Ld, LdT
        Mn, MTn = Mp[1], MTp[1]
        yield
        p1, p2 = squares(Mc, MTc, Mn, MTn)
        yield
        P.copy("scalar", fl(Mn), p1)
        P.copy("vector", fl(MTn), p2)
        yi = 0
        for m in range(1, 5):
            Mc, MTc = Mn, MTn
            yield
            p3 = psA.next()
            p4 = psA.next()
            for hh in range(4):
                P.mm(p3[:, hh * 128:(hh + 1) * 128], Mc[:, hh, :], Yp[yi][:, hh, :], start=True, stop=False)
                P.mm(p3[:, hh * 128:(hh + 1) * 128], ident_b, Yp[yi][:, hh, :], start=False, stop=True)
            for hh in range(4):
                P.mm(p4[:, hh * 128:(hh + 1) * 128], MTc[:, hh, :], Dp[yi][:, hh, :], start=True, stop=False)
                P.mm(p4[:, hh * 128:(hh + 1) * 128], ident_b, Dp[yi][:, hh, :], start=False, stop=True)
            if m < 4:
                Mn, MTn = Mp[(m + 1) % 2], MTp[(m + 1) % 2]
                p1, p2 = squares(Mc, MTc, Mn, MTn)
            yield
            P.copy("scalar", fl(Yp[1 - yi]), p3)
            P.copy("vector", fl(Dp[1 - yi]), p4)
            if m < 4:
                P.copy("scalar", fl(Mn), p1)
                P.copy("vector", fl(MTn), p2)
            yi = 1 - yi
        Yd, Dd = Yp[yi], Dp[yi]
        chk("G6")
        DRu, DRw = Mp[0], MTp[0]
        yield
        p1 = psA.next()
        p2 = psA.next()
        p3 = psA.next()
        p4 = psA.next()
        for hh in range(4):
            P.mm(p1[:, hh * 128:(hh + 1) * 128], Rr[:, hh, :], Yd[:, hh, :])
        for hh in range(4):
            P.mm(p2[:, hh * 128:(hh + 1) * 128], Yd[:, hh, :], Rr[:, hh, :])
        for hh in range(4):
            P.mm(p3[:, hh * 128:(hh + 1) * 128], Yd[:, hh, :], Ru[:, hh, :])
        for hh in range(4):
            P.mm(p4[:, hh * 128:(hh + 1) * 128], Yd[:, hh, :], Rw[:, hh, :])
        yield
        P.tt("vector", fl(Wn), fl(ident4_b), p1, ALU.subtract)
        P.copy("scalar", fl(Nb), p2)
        P.copy("scalar", fl(DRu), p3)
        P.copy("vector", fl(DRw), p4)
        yield
        p3 = psA.next()
        for hh in range(4):
            P.mm(p3[:, hh * 128:(hh + 1) * 128], Nb[:, hh, :], Wn[:, hh, :])
        yield
        P.copy("scalar", fl(T1), p3)
        yield
        p4 = psA.next()
        for hh in range(4):
            P.mm(p4[:, hh * 128:(hh + 1) * 128], Nb[:, hh, :], T1[:, hh, :], start=True, stop=False)
            P.mm(p4[:, hh * 128:(hh + 1) * 128], ident_b, Wn[:, hh, :], start=False, stop=True)
        yield
        P.copy("scalar", fl(Zt), p4)
        chk("G7")
        yield
        p6 = psA.next()
        for hh in range(4):
            P.mm(p6[:, hh * 128:(hh + 1) * 128], DRw[:, hh, :], Zt[:, hh, :])
        yield
        P.ts("vector", fl(wTn), p6, -1.0, None, ALU.mult)
        yield
        p7 = psA.next()
        for hh in range(4):
            P.mm(p7[:, hh * 128:(hh + 1) * 128], Zt[:, hh, :], DRu[:, hh, :], start=True, stop=False)
            P.mm(p7[:, hh * 128:(hh + 1) * 128], wTn[:, hh, :], Sgb[g4][:, hh, :], start=False, stop=True)
        yield
        P.copy("scalar", fl(dlt), p7)
        yield
        p8 = psA.next()
        for hh in range(4):
            P.mm(p8[:, hh * 128:(hh + 1) * 128], qdT[:, hh, :], Sgb[g4][:, hh, :], start=True, stop=False)
            P.mm(p8[:, hh * 128:(hh + 1) * 128], AqkT[:, hh, :], dlt[:, hh, :], start=False, stop=True)
        p9 = psA.next()
        for hh in range(4):
            P.mm(p9[:, hh * 128:(hh + 1) * 128], kdec[:, hh, :], dlt[:, hh, :])
        yield
        for hh, h in enumerate(hs):
            P.stt("vector", Sg[:, h, :], Sg[:, h, :], gend4[:, hh:hh + 1], p9[:, hh * 128:(hh + 1) * 128], ALU.mult, ALU.add)
        P.copy("scalar", fl(Sgb[g4]), Sg[:, g4 * 4:(g4 + 1) * 4, :].re("p a b -> p (a b)"))
        sq = wide_f.next()
        P.act(sq, p8, AF.Square)
        ss4 = smalls.next()
        P.reduce("vector", ss4[:, 0:4], sq.re("p (a b) -> p a b", a=4), ALU.add)
        P.ts("vector", ss4[:, 0:4], ss4[:, 0:4], 1.0 / HD, EPS, ALU.mult, ALU.add)
        P.act(ss4[:, 0:4], ss4[:, 0:4], AF.Ln)
        rs4 = smalls.next()
        P.act(rs4[:, 0:4], ss4[:, 0:4], AF.Exp, scale=-0.5)
        for hh, h in enumerate(hs):
            P.stt("vector", oa[:, h * 128:(h + 1) * 128], p8[:, hh * 128:(hh + 1) * 128], rs4[:, hh:hh + 1],
                  zg[:, s, h * 128:(h + 1) * 128], ALU.mult, ALU.mult)
        if g4 == 1:
            tap("oa", oa, [128, 1024], BF16)
            to_T(oa, oaT, c0)
        yield

    def gla_proj(h, hT, wt):
        pz = psA.next()
        P.mm(pz[:, 0:T], wg2[:, h * 128:(h + 1) * 128], lgT)
        pq = psA.next()
        pk = psA.next()
        for kc in range(8):
            P.mm(pq[:, 0:T], wt[:, kc, 0:128], hT[:, kc, :], start=(kc == 0), stop=(kc == 7))
        for kc in range(8):
            P.mm(pk[:, 0:T], wt[:, kc, 128:256], hT[:, kc, :], start=(kc == 0), stop=(kc == 7))
        yield
        l_ = lsc.next()
        P.act(l_, pz[:, 0:T], AF.Exp, scale=-1.0, bias=negb[:, h:h + 1])
        P.act(l_, l_, AF.Ln, bias=1.0)
        for s in range(NS):
            P.scan(cs[:, h, s * 128:(s + 1) * 128], ones_f, l_[:, s * 128:(s + 1) * 128], 0.0, ALU.mult, ALU.add)
        for s in range(NS):
            cols = slice(s * 128, (s + 1) * 128)
            cv = cs[:, h, cols]
            c2 = smallsB.next()
            P.ts("vector", c2[:, 0:1], cv[:, 64:65], 1.0 / 16, None, ALU.mult)
            P.ts("vector", c2[:, 1:2], cv[:, 64:65], -1.0 / 16, None, ALU.mult)
            P.ts("vector", c2[:, 2:3], cv[:, 127:128], -1.0 / 16, None, ALU.mult)
            eq = efac.next()
            P.act(eq, cv, AF.Exp, scale=-1.0 / 16, bias=c2[:, 0:1])
            ek = efac.next()
            P.act(ek, cv, AF.Exp, scale=1.0 / 16, bias=c2[:, 1:2])
            ed = efac.next()
            P.act(ed, cv, AF.Exp, scale=-1.0 / 16)
            ekd = efac.next()
            P.act(ekd, cv, AF.Exp, scale=1.0 / 16, bias=c2[:, 2:3])
            sc = float(LKD ** -0.5)
            P.stt("vector", glp[0][:, h, cols], pq[:, cols], sc, eq, ALU.mult, ALU.mult)
            P.tt("vector", glp[1][:, h, cols], pk[:, cols], ek, ALU.mult)
            P.stt("vector", glp[2][:, h, cols], pq[:, cols], sc, ed, ALU.mult, ALU.mult)
            P.tt("vector", glp[3][:, h, cols], pk[:, cols], ekd, ALU.mult)
            P.copy("gpsimd", gendl[:, s, h:h + 1], ed[:, 127:128])
        yield
        for s in range(NS):
            pv = psA.next()
            for kc in range(8):
                P.mm(pv[:, 0:256], hT[:, kc, s * 128:(s + 1) * 128], wt[:, kc, 256:512], start=(kc == 0), stop=(kc == 7))
            yield
            P.copy("scalar", vl[:, s, h * 256:(h + 1) * 256], pv[:, 0:256])
            yield

    gendl = P.sb("gendl", [128, NS, 4], F32)

    def gla_r(c, hT, wt):
        for s in range(NS):
            pb = psA.next()
            for kc in range(8):
                P.mm(pb, hT[:, kc, s * 128:(s + 1) * 128], wt[:, kc, :], start=(kc == 0), stop=(kc == 7))
            yield
            e = wide_f.next()
            P.act(e, pb, AF.Exp, scale=-1.0)
            P.act(e, e, AF.Ln, bias=1.0)
            P.act(e, e, AF.Exp, scale=-1.0)
            P.tt("vector", e, pb, e, ALU.mult)
            P.tt("gpsimd", rg[:, s, c * 512:(c + 1) * 512], e, lng[:, c * 512:(c + 1) * 512], ALU.mult)
            yield

    def gla_core(s, obT):
        c0 = s * 128
        cols = slice(c0, c0 + 128)
        fl = lambda t_: t_.re("p a b -> p (a b)")
        pa = psA.next()
        for h in range(4):
            P.mm(pa[:, h * 128:(h + 1) * 128], glp[1][:, h, cols], glp[0][:, h, cols])
        pbt = psA.next().bc(BF16)
        for h in range(4):
            P.tr(pbt[:, h * 128:(h + 1) * 128], glp[3][:, h, cols], ident_b)
        yield
        P.tt("vector", fl(ATm), pa, fl(m01t4), ALU.mult)
        P.copy("scalar", fl(kdl), pbt[:, 0:512])
        yield
        po = [psA.next(), psA.next()]
        for h in range(4):
            o_ = po[h // 2][:, (h % 2) * 256:(h % 2 + 1) * 256]
            P.mm(o_, glp[2][:, h, cols], Slb[:, h, :], start=True, stop=False)
            P.mm(o_, ATm[:, h, :], vl[:, s, h * 256:(h + 1) * 256], start=False, stop=True)
        yield
        ss4 = smallsB.next()
        for half in range(2):
            sq = wide_f.next()
            P.act(sq, po[half], AF.Square)
            P.reduce("vector", ss4[:, half * 2:half * 2 + 2], sq.re("p (a b) -> p a b", a=2), ALU.add)
        P.ts("vector", ss4[:, 0:4], ss4[:, 0:4], 1.0 / LVD, EPS, ALU.mult, ALU.add)
        P.act(ss4[:, 0:4], ss4[:, 0:4], AF.Ln)
        rs4 = smallsB.next()
        P.act(rs4[:, 0:4], ss4[:, 0:4], AF.Exp, scale=-0.5)
        for h in range(4):
            P.stt("vector", ob[:, h * 256:(h + 1) * 256], po[h // 2][:, (h % 2) * 256:(h % 2 + 1) * 256], rs4[:, h:h + 1],
                  rg[:, s, h * 256:(h + 1) * 256], ALU.mult, ALU.mult)
        yield
        pu = [psA.next(), psA.next()]
        for h in range(4):
            P.mm(pu[h // 2][:, (h % 2) * 256:(h % 2 + 1) * 256], kdl[:, h, :], vl[:, s, h * 256:(h + 1) * 256])
        yield
        for h in range(4):
            P.stt("vector", Sl[:, h, :], Sl[:, h, :], gendl[:, s, h:h + 1], pu[h // 2][:, (h % 2) * 256:(h % 2 + 1) * 256], ALU.mult, ALU.add)
        P.copy("scalar", fl(Slb), fl(Sl))
        yield
        tap("ob", ob, [128, 1024], BF16)
        to_T(ob, obT, c0)
        yield
        yield

    def gates_proj(i, hT, wt):
        dst = sa if i < 2 else sb_
        c = i % 2
        for s in range(NS):
            pb = psA.next()
            for kc in range(8):
                P.mm(pb, hT[:, kc, s * 128:(s + 1) * 128], wt[:, kc, :], start=(kc == 0), stop=(kc == 7))
            yield
            e = wide_f.next()
            P.act(e, pb, AF.Exp, scale=-1.0)
            P.act(e, e, AF.Ln, bias=1.0)
            P.act(dst[:, s, c * 512:(c + 1) * 512], e, AF.Exp, scale=-1.0)
            yield

    def stage_A(q_, st):
        t0 = st * T
        P.phase = "A_start"
        P.dma("sync", xt, x_d[q_, t0:t0 + T, :].rearrange("(s p) d -> p s d", p=128))
        hT = ttiles.next()
        for s in range(NS):
            norm_T(xt[:, s, :], hT, s * 128)
        tap("hT", hT, [128, 8, T], BF16)
        chk("A_norm")
        for s in range(NS):
            pb = psA.next()
            for kc in range(8):
                P.mm(pb[:, 0:32], hT[:, kc, s * 128:(s + 1) * 128], wsm[:, kc, :], start=(kc == 0), stop=(kc == 7))
            P.copy("scalar", gab[:, s, :], pb[:, 0:32])
        pb = psA.next()
        for kc in range(8):
            P.mm(pb[0:16, 0:T], wsm[:, kc, 16:32], hT[:, kc, :], start=(kc == 0), stop=(kc == 7))
        P.copy("scalar", lgT, pb[0:16, 0:T])
        chk("A_small")
        pend = []
        for h in range(8):
            wt = wget(B_GDN + h)
            cur = gdn_front(h, hT, wt)
            prep_pump()
            gdn_z(h, hT, wt)
            prep_pump()
            pend.append((h,) + cur)
            if len(pend) > 1:
                gdn_back(*pend.pop(0))
        while pend:
            gdn_back(*pend.pop(0))
        tap("qn0", qn[0], [128, 4, T], BF16); tap("kn0", kn[0], [128, 4, T], BF16); tap("vs0", vs[0], [128, 4, T], BF16)
        tap("zg", zg, [128, NS, 1024], BF16)
        chk("A_gdnproj")
        oaT = ttiles.next()
        obT = ttiles.next()
        def side_gen():
            for h in range(4):
                yield from gla_proj(h, hT, wget(B_GLA + h))
            for c in range(2):
                yield from gla_r(c, hT, wget(B_R + c))
            for s in range(NS):
                yield from gla_core(s, obT)
            for i in range(4):
                yield from gates_proj(i, hT, wget(B_GATE + i))

        side_it = side_gen()
        for s in range(NS):
            for g4 in range(2):
                for _ in gdn_unit(s, g4, oaT):
                    next(side_it, None)
                    prep_pump()
        chk("A_gdncore")
        for _ in side_it:
            pass
        chk("A_glacore")
        return oaT, obT

    def resid_proj(srcT, s, blk0, after=None):
        pass

    def stage_B(q_, st, oaT, obT):
        t0 = st * T
        P.phase = "B_branch"
        for c in range(2):
            wt = wget(B_BG + c)
            for s in range(NS):
                pb = psA.next()
                for kc in range(8):
                    P.mm(pb, oaT[:, kc, s * 128:(s + 1) * 128], wt[:, kc, :], start=(kc == 0), stop=(kc == 7))
                P.tt("vector", yam[:, s, c * 512:(c + 1) * 512], pb, sa[:, s, c * 512:(c + 1) * 512], ALU.mult)
        mTt = ttiles.next()
        for c in range(2):
            wt = wget(B_BL + c)
            for s in range(NS):
                pb = psA.next()
                for kc in range(8):
                    P.mm(pb, obT[:, kc, s * 128:(s + 1) * 128], wt[:, kc, :], start=(kc == 0), stop=(kc == 7))
                m2 = wide_f.next()
                P.tt("vector", m2, pb, sb_[:, s, c * 512:(c + 1) * 512], ALU.mult)
                P.tt("gpsimd", tokb[s][:, c * 512:(c + 1) * 512], m2, yam[:, s, c * 512:(c + 1) * 512], ALU.add)
        for s in range(NS):
            to_T(tokb[s], mTt, s * 128)
        for c in range(2):
            wt = wget(B_OUT + c)
            for s in range(NS):
                pb = psA.next()
                for kc in range(8):
                    P.mm(pb, mTt[:, kc, s * 128:(s + 1) * 128], wt[:, kc, :], start=(kc == 0), stop=(kc == 7))
                P.tt("vector", xt[:, s, c * 512:(c + 1) * 512], pb, xt[:, s, c * 512:(c + 1) * 512], ALU.add)
        tap("x1", xt, [128, NS, 1024])
        chk("B_x1")
        h2T = ttiles.next()
        for s in range(NS):
            norm_T(xt[:, s, :], h2T, s * 128)
        qT = ttiles.next()
        for c2 in range(2):
            wt = wget(B_WQ + c2)
            for cc in range(4):
                pb = psA.next()
                for kc in range(8):
                    P.mm(pb[:, 0:T], wt[:, kc, cc * 128:(cc + 1) * 128], h2T[:, kc, :], start=(kc == 0), stop=(kc == 7))
                P.act(qT[:, c2 * 4 + cc, :], pb[:, 0:T], AF.Identity, scale=float(256 ** -0.5))
        oxT = ttiles.next()
        for s in range(NS):
            cols = slice(s * 128, (s + 1) * 128)
            psc = [psA.next(), psA.next()]
            for h in range(4):
                o_ = psc[h // 2][:, (h % 2) * 256:(h % 2 + 1) * 256]
                for c in range(2):
                    P.mm(o_, qT[:, 2 * h + c, cols], KT[:, 2 * h + c, :], start=(c == 0), stop=(c == 1))
            mx = smalls.next()
            for half in range(2):
                P.reduce("vector", mx[:, half * 2:half * 2 + 2], psc[half].re("p (a b) -> p a b", a=2), ALU.max)
            P.ts("vector", mx[:, 0:4], mx[:, 0:4], -1.0, None, ALU.mult)
            sm4 = smalls.next()
            for h in range(4):
                P.act(pt[:, h, :], psc[h // 2][:, (h % 2) * 256:(h % 2 + 1) * 256], AF.Exp, bias=mx[:, h:h + 1], accum_out=sm4[:, h:h + 1])
            rs4 = smalls.next()
            P.recip("vector", rs4[:, 0:4], sm4[:, 0:4])
            pbt = psA.next().bc(BF16)
            for h in range(4):
                for mc in range(2):
                    P.tr(pbt[:, (2 * h + mc) * 128:(2 * h + mc + 1) * 128], pt[:, h, mc * 128:(mc + 1) * 128], ident_b)
            P.copy("scalar", pT.re("p a b -> p (a b)"), pbt)
            pov = [psA.next(), psA.next()]
            for h in range(4):
                o_ = pov[h // 2][:, (h % 2) * 256:(h % 2 + 1) * 256]
                for mc in range(2):
                    P.mm(o_, pT[:, 2 * h + mc, :], Vt[:, mc, h * 256:(h + 1) * 256], start=(mc == 0), stop=(mc == 1))
            for h in range(4):
                P.ts("vector", ox[:, h * 256:(h + 1) * 256], pov[h // 2][:, (h % 2) * 256:(h % 2 + 1) * 256], rs4[:, h:h + 1], None, ALU.mult)
            to_T(ox, oxT, s * 128)
            prep_pump()
        for c in range(2):
            wt = wget(B_WO + c)
            for s in range(NS):
                pb = psA.next()
                for kc in range(8):
                    P.mm(pb, oxT[:, kc, s * 128:(s + 1) * 128], wt[:, kc, :], start=(kc == 0), stop=(kc == 7))
                P.tt("vector", xt[:, s, c * 512:(c + 1) * 512], pb, xt[:, s, c * 512:(c + 1) * 512], ALU.add)
        tap("x2", xt, [128, NS, 1024])
        chk("B_x2")
        P.phase = "B_mlp"
        h3T = ttiles.next()
        for s in range(NS):
            norm_T(xt[:, s, :], h3T, s * 128)
        acc = {}
        for fg in range(4):
            hq = hidq[fg % 2]
            for c2 in range(2):
                wt = wget(B_W1 + fg * 2 + c2)
                for cc in range(4):
                    fi = c2 * 4 + cc
                    pb = psA.next()
                    for kc in range(8):
                        P.mm(pb[:, 0:T], wt[:, kc, cc * 128:(cc + 1) * 128], h3T[:, kc, :], start=(kc == 0), stop=(kc == 7))
                    r_ = wide_f.next()
                    P.act(r_[:, 0:T], pb[:, 0:T], AF.Relu)
                    P.tt("vector" if fi % 2 == 0 else "gpsimd", hq[:, fi, :], r_[:, 0:T], r_[:, 0:T], ALU.mult)
                    prep_pump()
            if fg == 0:
                for s in range(NS):
                    for c in range(2):
                        acc[(s, c)] = psA.next()
                        psA.hold(acc[(s, c)])
            for c in range(2):
                wt = wget(B_W2 + fg * 2 + c)
                for s in range(NS):
                    for kc in range(8):
                        P.mm(acc[(s, c)], hq[:, kc, s * 128:(s + 1) * 128], wt[:, kc, :],
                             start=(fg == 0 and kc == 0), stop=(fg == 3 and kc == 7))
        for s in range(NS):
            for c in range(2):
                P.tt("vector", xt[:, s, c * 512:(c + 1) * 512], acc[(s, c)], xt[:, s, c * 512:(c + 1) * 512], ALU.add)
                psA.release(acc[(s, c)])
        tap("x3", xt, [128, NS, 1024])
        chk("B_x3")
        P.phase = "B_final"
        for s in range(NS):
            rs = rstd_of(xt[:, s, :], 1024)
            P.stt("vector", xt[:, s, :], xt[:, s, :], rs, gfin, ALU.mult, ALU.mult)
        d = P.dma("sync", y_d[q_, t0:t0 + T, :].rearrange("(s p) d -> p s d", p=128), xt)
        P.final_deps.append(d)

    try:
      chk("prep")
      for q_ in range(nseq):
        P.memset("vector", Sg.re("p a b -> p (a b)"), 0.0)
        for g in range(2):
            P.memset("gpsimd", Sgb[g].re("p a b -> p (a b)"), 0.0)
        P.memset("vector", Sl.re("p a b -> p (a b)"), 0.0)
        P.memset("gpsimd", Slb.re("p a b -> p (a b)"), 0.0)
        P.memset("gpsimd", halo.re("p a b -> p (a b)"), 0.0)
        kv_prep(q_)
        chk("kv")
        for st in range(nst):
            oaT, obT = stage_A(q_, st)
            chk("A")
            stage_B(q_, st, oaT, obT)
    except _Stop:
        pass

    bes = ExitStack()
    finals = P.finalize(bes)
    bes.enter_context(nc.allow_low_precision(reason="bf16 matmul operands by design; fp32 accumulation"))
    block = bes.enter_context(nc.Block())
    P.emit(block, finals)
    bes.close()
    es.close()
    ninst = {e: len(P.ops[e]) for e in ENGS}
    return nc, tap_outs, dict(ninst=ninst, nwaits=P.nwaits, nsems=P.nsems, op_phase=P.op_phase)


N_CORES = 8


def kernel(**inputs):
    x = np.asarray(inputs["x"], dtype=np.float32)
    mem = np.asarray(inputs["mem"], dtype=np.float32)
    B, S, _ = x.shape
    nseq = B // N_CORES
    lay = host_layout(inputs)
    nc, _, _ = build(nseq, S)
    in_maps = []
    for c in range(N_CORES):
        m = dict(lay)
        m["x"] = np.ascontiguousarray(x[c * nseq:(c + 1) * nseq])
        m["mem"] = np.ascontiguousarray(mem[c * nseq:(c + 1) * nseq])
        in_maps.append(m)
    res = run_bass_kernel_spmd(nc, in_maps, core_ids=list(range(N_CORES)))
    out = np.concatenate([np.asarray(r["y"], dtype=np.float32) for r in res.results], axis=0)
    return out
```
